# Optimizing a Trainium2 kernel written in Bass

```python
import jax, jax.numpy as jnp
from jax import lax
import numpy as np

D_MODEL = 2048
BATCH = 16
SEQ = 256
DEPTH = 4
DEC_BATCH = 4
DEC_SEQ = 2048
PAST_LEN = 256

GRID_W = 64
N_DIR = 2
A_HEAD_DIM = 64
A_WIDTH = D_MODEL // 2
A_HEADS = A_WIDTH // A_HEAD_DIM
DECAY_LORA = 96
ICLR_LORA = 96
DECAY_SCALE = 0.606531
RWKV_GN_EPS = 64e-5
B_HEAD_DIM = 128
B_WIDTH = D_MODEL // 2
B_HEADS = B_WIDTH // B_HEAD_DIM
CONV_K = 3
CHUNK = 64
NORM_EPS = 1e-6
N_IN = 4 * A_WIDTH + 2 * DECAY_LORA + 2 * ICLR_LORA + 4 * B_WIDTH + 2 * N_DIR * B_HEADS + 2 * D_MODEL

kernel_name = "bidir_rwkv7_gdn_flow_backbone"


def rms_norm(x, g, eps=NORM_EPS):
    xf = x.astype(jnp.float32)
    y = xf * lax.rsqrt(jnp.mean(xf * xf, axis=-1, keepdims=True) + eps)
    return (y * g.astype(jnp.float32)).astype(x.dtype)


def l2_normalize(x, eps=NORM_EPS):
    xf = x.astype(jnp.float32)
    return xf * lax.rsqrt(jnp.sum(xf * xf, axis=-1, keepdims=True) + eps)


def grid_transpose(x, rows, cols):
    b, t, d = x.shape
    return x.reshape(b, rows, cols, d).swapaxes(1, 2).reshape(b, t, d)


def centred_depthwise_conv(x, w):
    t = x.shape[1]
    half = CONV_K // 2
    xp = jnp.pad(x, ((0, 0), (half, half), (0, 0)))
    return sum(xp[:, j:j + t] * w[j] for j in range(CONV_K))


def split_in_proj(proj):
    sizes = (A_WIDTH,) * 4 + (DECAY_LORA, DECAY_LORA, ICLR_LORA, ICLR_LORA) + (B_WIDTH,) * 4 + (N_DIR * B_HEADS,) * 2 + (D_MODEL,) * 2
    idx = np.cumsum(sizes)[:-1].tolist()
    return jnp.split(proj, idx, axis=-1)


def rwkv7_scan(r, w, k, v, kk, a, s0, reverse):
    def step(s, inp):
        r_t, w_t, k_t, v_t, kk_t, a_t = inp
        sa = jnp.einsum('bhvk,bhk->bhv', s, -kk_t)
        s = (s * w_t[:, :, None, :]
             + sa[..., None] * (kk_t * a_t)[:, :, None, :]
             + v_t[..., None] * k_t[:, :, None, :])
        return s, jnp.einsum('bhvk,bhk->bhv', s, r_t)
    xs = tuple(jnp.swapaxes(z, 0, 1) for z in (r, w, k, v, kk, a))
    s_fin, ys = lax.scan(step, s0, xs, reverse=reverse)
    return jnp.swapaxes(ys, 0, 1), s_fin


def gated_delta_chunked(q, k, v, g, beta, s0):
    b, t, h, _ = q.shape
    dv = v.shape[-1]
    n = t // CHUNK

    def chunks(z):
        z = z.reshape(b, n, CHUNK, h, *z.shape[3:])
        return jnp.moveaxis(z, (1, 3), (0, 2))

    qc, kc, vc, gc, bc = (chunks(z) for z in (q, k, v, g, beta))
    gcum = jnp.cumsum(gc, axis=-1)
    causal = jnp.tril(jnp.ones((CHUNK, CHUNK), bool))
    strict = jnp.tril(jnp.ones((CHUNK, CHUNK), bool), -1)
    decay = jnp.exp(jnp.where(causal, gcum[..., :, None] - gcum[..., None, :], -jnp.inf))
    kb = kc * bc[..., None]
    lower = jnp.where(strict, jnp.einsum('nbhik,nbhjk->nbhij', kb, kc) * decay, 0.0)
    eye = jnp.eye(CHUNK, dtype=jnp.float32)
    t_inv = lax.linalg.triangular_solve(eye + lower, jnp.broadcast_to(eye, lower.shape),
                                        left_side=True, lower=True, unit_diagonal=True)
    u = t_inv @ (vc * bc[..., None])
    wk = t_inv @ (kb * jnp.exp(gcum)[..., None])
    qk = jnp.einsum('nbhik,nbhjk->nbhij', qc, kc) * decay

    def step(s, inp):
        q_i, k_i, u_i, w_i, g_i, qk_i = inp
        v_new = u_i - w_i @ s
        o = (q_i * jnp.exp(g_i)[..., None]) @ s + qk_i @ v_new
        g_last = g_i[..., -1:]
        s = s * jnp.exp(g_last)[..., None] + jnp.einsum(
            'bhck,bhcv->bhkv', k_i * jnp.exp(g_last - g_i)[..., None], v_new)
        return s, o

    s_fin, o = lax.scan(step, s0, (qc, kc, u, wk, gcum, qk))
    o = jnp.moveaxis(o, (0, 2), (1, 3)).reshape(b, t, h, dv)
    return o, s_fin


def mixer_layer(x, mod, s_rwkv0, s_delta0, lp, grid_rows):
    f32 = jnp.float32
    shift, scale, gate = jnp.split(mod, 3, axis=-1)
    h = rms_norm(x, lp['g_pre']) * (1 + scale[:, None]) + shift[:, None]
    if grid_rows:
        h = grid_transpose(h, grid_rows, GRID_W)
    bsz, t, _ = h.shape
    (r, k, v, z_a, wlo_f, wlo_b, alo_f, alo_b,
     q_b, k_b, v_b, z_b, beta_in, alpha_in, gate_a_in, gate_b_in) = split_in_proj(h @ lp['w_in'])

    def heads_a(z):
        return z.reshape(bsz, t, A_HEADS, A_HEAD_DIM).astype(f32)

    decay_logit = lp['w0'] + jnp.einsum('btdr,drc->btdc', jnp.stack([jnp.tanh(wlo_f), jnp.tanh(wlo_b)], 2), lp['w_up'])
    w_dec = jnp.exp(-DECAY_SCALE * jax.nn.sigmoid(decay_logit.astype(f32)))
    a_rate = jax.nn.sigmoid((lp['a0'] + jnp.einsum('btdr,drc->btdc', jnp.stack([alo_f, alo_b], 2), lp['a_up'])).astype(f32))
    kk = l2_normalize(heads_a(k * lp['k_k']))
    k_dir = k.astype(f32)[:, :, None] * (1 + (a_rate - 1) * lp['k_a'].astype(f32))
    rf, vf = heads_a(r), heads_a(v)
    r_k = lp['r_k'].reshape(A_HEADS, A_HEAD_DIM).astype(f32)
    y_a = 0.0
    s_rwkv = []
    for d, rev in ((0, False), (1, True)):
        kd = heads_a(k_dir[:, :, d])
        y, s_fin = rwkv7_scan(rf, heads_a(w_dec[:, :, d]), kd, vf, kk, heads_a(a_rate[:, :, d]),
                              s_rwkv0[:, d].astype(f32), rev)
        y_a = y_a + y + jnp.sum(rf * kd * r_k, axis=-1, keepdims=True) * vf
        s_rwkv.append(s_fin)
    mu = jnp.mean(y_a, axis=-1, keepdims=True)
    var = jnp.mean(jnp.square(y_a - mu), axis=-1, keepdims=True)
    y_a = ((y_a - mu) * lax.rsqrt(var + RWKV_GN_EPS)).reshape(bsz, t, A_WIDTH)
    y_a = (y_a * lp['gn_w'] + lp['gn_b']).astype(x.dtype) * jax.nn.silu(z_a)
    branch_a = y_a @ lp['w_pa']

    qkv = jax.nn.silu(centred_depthwise_conv(jnp.concatenate([q_b, k_b, v_b], -1), lp['conv_w']))
    qh, kh, vh = (z.reshape(bsz, t, B_HEADS, B_HEAD_DIM) for z in jnp.split(qkv, 3, axis=-1))
    qh = l2_normalize(qh) * (B_HEAD_DIM ** -0.5)
    kh = l2_normalize(kh)
    vh = vh.astype(f32)
    beta = jax.nn.sigmoid(beta_in.reshape(bsz, t, N_DIR, B_HEADS).astype(f32))
    g = -jnp.exp(lp['a_log'].astype(f32)) * jax.nn.softplus(
        alpha_in.reshape(bsz, t, N_DIR, B_HEADS).astype(f32) + lp['dt_bias'].astype(f32))
    o_f, sd_f = gated_delta_chunked(qh, kh, vh, g[:, :, 0], beta[:, :, 0], s_delta0[:, 0].astype(f32))
    flip = lambda z: jnp.flip(z, axis=1)
    o_b, sd_b = gated_delta_chunked(flip(qh), flip(kh), flip(vh), flip(g[:, :, 1]), flip(beta[:, :, 1]),
                                    s_delta0[:, 1].astype(f32))
    o = rms_norm(o_f + flip(o_b), lp['o_norm_w']).reshape(bsz, t, B_WIDTH)
    y_b = o.astype(x.dtype) * jax.nn.silu(z_b)
    branch_b = y_b @ lp['w_pb']

    merged = jax.nn.sigmoid(gate_a_in) * branch_a + jax.nn.sigmoid(gate_b_in) * branch_b
    out = merged @ lp['w_o']
    if grid_rows:
        out = grid_transpose(out, GRID_W, grid_rows)
    x = x + gate[:, None] * rms_norm(out, lp['g_post'])
    return x, jnp.stack(s_rwkv, axis=1), jnp.stack([sd_f, sd_b], axis=1)


def setup_inputs(seed: int = 0) -> dict:
    key = jax.random.key(seed)
    ks = iter(jax.random.split(key, 40))
    f32 = jnp.float32
    D = D_MODEL

    def nrm(shape, s):
        return jax.random.normal(next(ks), shape, f32) * s

    dt = jnp.exp(jax.random.uniform(next(ks), (DEPTH, N_DIR, B_HEADS), f32, np.log(1e-3), np.log(1e-1)))
    return {
        'x_prompt': nrm((BATCH, SEQ, D), 1.0),
        'x_sample': nrm((DEC_BATCH, DEC_SEQ, D), 1.0),
        'state_rwkv': nrm((DEC_BATCH, DEPTH, N_DIR, A_HEADS, A_HEAD_DIM, A_HEAD_DIM), 0.1),
        'state_delta': nrm((DEC_BATCH, DEPTH, N_DIR, B_HEADS, B_HEAD_DIM, B_HEAD_DIM), 0.1),
        'c': nrm((DEC_BATCH, D), 1.0),
        'c_ctx': nrm((D,), 1.0),
        'w_mod': nrm((DEPTH, D, 3 * D), 0.5 * D ** -0.5),
        'b_mod': nrm((DEPTH, 3 * D), 0.01),
        'g_pre': 1.0 + nrm((DEPTH, D), 0.02),
        'g_post': 1.0 + nrm((DEPTH, D), 0.02),
        'w_in': nrm((DEPTH, D, N_IN), D ** -0.5),
        'w0': jax.random.uniform(next(ks), (DEPTH, N_DIR, A_WIDTH), f32, -3.0, 3.0),
        'w_up': nrm((DEPTH, N_DIR, DECAY_LORA, A_WIDTH), 0.1 * DECAY_LORA ** -0.5),
        'a0': nrm((DEPTH, N_DIR, A_WIDTH), 0.5),
        'a_up': nrm((DEPTH, N_DIR, ICLR_LORA, A_WIDTH), 0.1 * ICLR_LORA ** -0.5),
        'k_k': 0.85 + nrm((DEPTH, A_WIDTH), 0.02),
        'k_a': 1.0 + nrm((DEPTH, A_WIDTH), 0.02),
        'r_k': nrm((DEPTH, A_WIDTH), 0.1),
        'gn_w': 1.0 + nrm((DEPTH, A_WIDTH), 0.02),
        'gn_b': nrm((DEPTH, A_WIDTH), 0.01),
        'conv_w': nrm((DEPTH, CONV_K, 3 * B_WIDTH), CONV_K ** -0.5),
        'a_log': jnp.log(jax.random.uniform(next(ks), (DEPTH, N_DIR, B_HEADS), f32, 1.0, 16.0)),
        'dt_bias': dt + jnp.log(-jnp.expm1(-dt)),
        'o_norm_w': 1.0 + nrm((DEPTH, B_HEAD_DIM), 0.02),
        'w_pa': nrm((DEPTH, A_WIDTH, D), A_WIDTH ** -0.5),
        'w_pb': nrm((DEPTH, B_WIDTH, D), B_WIDTH ** -0.5),
        'w_o': nrm((DEPTH, D, D), D ** -0.5),
    }


def reference(x_prompt, x_sample, state_rwkv, state_delta, c, c_ctx, w_mod, b_mod, g_pre, g_post,
              w_in, w0, w_up, a0, a_up, k_k, k_a, r_k, gn_w, gn_b, conv_w, a_log, dt_bias,
              o_norm_w, w_pa, w_pb, w_o):
    rows = x_sample.shape[1] // GRID_W
    bp = x_prompt.shape[0]
    y_prompt, y_sample = x_prompt, x_sample
    zeros_r = jnp.zeros((bp, N_DIR, A_HEADS, A_HEAD_DIM, A_HEAD_DIM), jnp.float32)
    zeros_d = jnp.zeros((bp, N_DIR, B_HEADS, B_HEAD_DIM, B_HEAD_DIM), jnp.float32)
    new_r, new_d = [], []
    for l in range(DEPTH):
        lp = {'g_pre': g_pre[l], 'g_post': g_post[l], 'w_in': w_in[l], 'w0': w0[l], 'w_up': w_up[l],
              'a0': a0[l], 'a_up': a_up[l], 'k_k': k_k[l], 'k_a': k_a[l], 'r_k': r_k[l],
              'gn_w': gn_w[l], 'gn_b': gn_b[l], 'conv_w': conv_w[l], 'a_log': a_log[l],
              'dt_bias': dt_bias[l], 'o_norm_w': o_norm_w[l], 'w_pa': w_pa[l], 'w_pb': w_pb[l],
              'w_o': w_o[l]}
        mod_ctx = (jax.nn.silu(c_ctx) @ w_mod[l] + b_mod[l])[None]
        mod_lat = jax.nn.silu(c) @ w_mod[l] + b_mod[l]
        y_prompt, s_r, s_d = mixer_layer(y_prompt, mod_ctx, zeros_r, zeros_d, lp, 0)
        new_r.append(s_r.astype(x_prompt.dtype))
        new_d.append(s_d.astype(x_prompt.dtype))
        y_sample, _, _ = mixer_layer(y_sample, mod_lat, state_rwkv[:, l], state_delta[:, l], lp,
                                     rows if l % 2 == 1 else 0)
    new_state_rwkv = jnp.stack(new_r, axis=1)
    new_state_delta = jnp.stack(new_d, axis=1)
    return (y_prompt, y_sample, new_state_rwkv, new_state_delta)
```

```python
import numpy as np
import concourse.bass as bass
import concourse.mybir as mybir
from concourse.bass_utils import run_bass_kernel_spmd

F32 = mybir.dt.float32
BF16 = mybir.dt.bfloat16
I32 = mybir.dt.int32
ALU = mybir.AluOpType
AF = mybir.ActivationFunctionType
AX = mybir.AxisListType

D = 2048
T = 2048
NT = 16
DEPTH = 4
NIN = 12704
AW = 1024
DECAY_SCALE = 0.606531
GN_EPS = 64e-5
EPS = 1e-6
C_R, C_K, C_V, C_ZA = 0, 1024, 2048, 3072
C_LO = 4096
C_QKV = 4480
C_ZB = 7552
C_BETA, C_ALPHA = 8576, 8592
C_GA, C_GB = 8608, 10656


class Buf:
    __slots__ = ("name", "w", "r", "excl")

    def __init__(self, name, excl=False):
        self.name = name
        self.w = None
        self.r = []
        self.excl = excl


class Tl:
    __slots__ = ("t", "b")

    def __init__(self, t, b):
        self.t = t
        self.b = b

    def __getitem__(self, k):
        return self.t[k]


class Sched:
    LIMIT = 30000

    def __init__(self, nc, n_dma_sems=10):
        self.nc = nc
        self.eng = {"pe": nc.tensor, "act": nc.scalar, "dve": nc.vector, "pool": nc.gpsimd, "sp": nc.sync}
        self.nsem = 0
        self.sem = {k: self._newsem(k) for k in self.eng}
        self.cnt = {k: 0 for k in self.eng}
        self.seen = {k: {} for k in self.eng}
        self.dsems = {}
        self.n_dma_sems = n_dma_sems
        self.all_dma = []

    def _newsem(self, k):
        self.nsem += 1
        return self.nc.alloc_semaphore("s_%s_%d" % (k, self.nsem))

    def _wait(self, e, tok):
        sem, val = tok
        sid = id(sem)
        if self.seen[e].get(sid, 0) >= val:
            return
        self.seen[e][sid] = val
        self.eng[e].wait_ge(sem, val)

    def _deps(self, e, reads, writes, pe_skip):
        for b in reads:
            if b.w is not None and not (pe_skip and b.w[2]):
                self._wait(e, b.w[:2])
            if b.excl:
                for t in b.r:
                    self._wait(e, t[:2])
        for b in writes:
            if b.w is not None and not (pe_skip and b.w[2]):
                self._wait(e, b.w[:2])
            for t in b.r:
                if not (pe_skip and t[2]):
                    self._wait(e, t[:2])

    def op(self, e, fn, reads=(), writes=()):
        reads = [x.b if isinstance(x, Tl) else x for x in reads]
        writes = [x.b if isinstance(x, Tl) else x for x in writes]
        self._deps(e, reads, writes, e == "pe")
        if self.cnt[e] >= self.LIMIT:
            self.sem[e] = self._newsem(e)
            self.cnt[e] = 0
        ins = fn(self.eng[e])
        self.cnt[e] += 1
        tok = (self.sem[e], self.cnt[e], e == "pe")
        ins.then_inc(self.sem[e], 1)
        for b in reads:
            b.r.append(tok)
        for b in writes:
            b.w = tok
            b.r = []
        return tok

    def dma(self, e, fn, reads=(), writes=()):
        reads = [x.b if isinstance(x, Tl) else x for x in reads]
        writes = [x.b if isinstance(x, Tl) else x for x in writes]
        if e not in self.dsems:
            self.dsems[e] = [[self.nc.alloc_semaphore("d_%s_%d" % (e, i)), 0] for i in range(self.n_dma_sems)]
            self.dsems[e + "_i"] = 0
            self.all_dma.extend(self.dsems[e])
        i = self.dsems[e + "_i"]
        self.dsems[e + "_i"] = (i + 1) % self.n_dma_sems
        slot = self.dsems[e][i]
        self._deps(e, reads, writes, False)
        if slot[1] > 0:
            self._wait(e, (slot[0], slot[1]))
        ins = fn(self.eng[e])
        slot[1] += 16
        tok = (slot[0], slot[1], False)
        ins.then_inc(slot[0], 16)
        for b in reads:
            b.r.append(tok)
        for b in writes:
            b.w = tok
            b.r = []
        return tok

    def barrier(self):
        for e in self.eng:
            for e2 in self.eng:
                if self.cnt[e2] > 0:
                    self._wait(e, (self.sem[e2], self.cnt[e2]))
            for s in self.all_dma:
                if s[1] > 0:
                    self._wait(e, (s[0], s[1]))


class Ctx:
    pass


class _Cut(Exception):
    pass


def build_program(stop=None):
    nc = bass.Bass("TRN2", target_bir_lowering=False)
    S = Sched(nc)
    g = Ctx()

    def cut(tag):
        if stop == tag:
            raise _Cut()

    def din(name, shape, dt=F32):
        return nc.dram_tensor(name, list(shape), dt, kind="ExternalInput").ap()

    def dout(name, shape):
        return nc.dram_tensor(name, list(shape), F32, kind="ExternalOutput").ap()

    import os as _os2
    _dbg_out = bool(_os2.environ.get("KDBG_OUT"))

    def dscr(name, shape):
        return nc.dram_tensor(name, list(shape), F32, kind="ExternalOutput" if _dbg_out else "Internal").ap()

    x_in = din("x_in", [T, D])
    cvec = din("cvec", [128, 16])
    flag_in = din("flag", [128, 1])
    idx_in = din("idx", [128, DEPTH * NT], I32)
    cmask_in = din("cmask", [128, 2 * NT])
    lmask_in = din("lmask", [128, 7, 384])
    init_r = din("init_r", [DEPTH, 2, 64, 1024])
    init_d = din("init_d", [DEPTH, 2, 128, 1024])
    w_mod = din("w_mod", [DEPTH, D, 3 * D])
    b_mod = din("b_mod", [DEPTH, 3 * D])
    g_pre = din("g_pre", [DEPTH, D])
    g_post = din("g_post", [DEPTH, D])
    w_in = din("w_in", [DEPTH, D, NIN])
    w0 = din("w0", [DEPTH, 2, AW])
    w_up = din("w_up", [DEPTH, 2, 96, AW])
    a0 = din("a0", [DEPTH, 2, AW])
    a_up = din("a_up", [DEPTH, 2, 96, AW])
    k_k = din("k_k", [DEPTH, AW])
    k_a = din("k_a", [DEPTH, AW])
    r_k = din("r_k", [DEPTH, AW])
    gn_w = din("gn_w", [DEPTH, AW])
    gn_b = din("gn_b", [DEPTH, AW])
    conv_w = din("conv_w", [DEPTH, 3, 3072])
    a_log = din("a_log", [DEPTH, 16])
    dt_bias = din("dt_bias", [DEPTH, 16])
    o_norm_w = din("o_norm_w", [DEPTH, 128])
    w_pa = din("w_pa", [DEPTH, AW, D])
    w_pb = din("w_pb", [DEPTH, AW, D])
    w_o = din("w_o", [DEPTH, D, D])

    y_out = dout("y_out", [T, D])
    fin_r = dout("fin_r", [DEPTH, 8, 2, 64, 1024])
    fin_d = dout("fin_d", [DEPTH, 8, 2, 128, 1024])

    XS = [dscr("xs0", [T, D]), dscr("xs1", [T, D])]
    P = dscr("proj", [T + 2, NIN])
    GPG = dscr("gpg", [128, D])
    SR = {"GT": dscr("sr_gt", [NT, 2, 64, 1024]), "H": dscr("sr_h", [NT, 2, 64, 1024]),
          "RT": dscr("sr_rt", [NT, 2, 64, 2048]), "Y0": dscr("sr_y0", [NT, 2, 128, 1024])}
    SD = {"GT": dscr("sd_gt", [NT, 2, 128, 1024]), "H": dscr("sd_h", [NT, 2, 128, 1024]),
          "RT": dscr("sd_rt", [NT, 2, 128, 1024]), "Y0": dscr("sd_y0", [NT, 2, 128, 1024])}
    YA = dscr("ya", [2, T, 1024])
    YB = dscr("yb", [2, T, 1024])
    BON = dscr("bon", [T, 32])
    MG = dscr("mg", [T, D])

    names = [0]

    def sb(shape, dt=F32, name=None):
        names[0] += 1
        nm = "%s_%d" % (name or "t", names[0])
        return Tl(nc.alloc_sbuf_tensor(nm, list(shape), dt), Buf(nm))

    ident = sb([128, 128], name="ident")
    ones = sb([128, 128], name="ones")
    mU, mUi, mL, mLi = (sb([128, 128], name="m") for _ in range(4))
    cm = [sb([128, 128], name="cm") for _ in range(2)]
    A1 = [sb([128, 128], name="a1") for _ in range(2)]
    A2 = [sb([128, 128], name="a2") for _ in range(2)]
    A6 = [sb([128, 128], name="a6") for _ in range(2)]
    mask4 = [sb([128, 512], name="mask4") for _ in range(2)]
    flagc = sb([128, 1], name="flag")
    idxt = sb([128, DEPTH * NT], I32, name="idx")
    cmask = sb([128, 2 * NT], name="cmask")
    zrow = sb([1, 512], name="zrow")
    LM = sb([128, 7, 384], name="LM")
    II = sb([128, 256], name="II")
    identb = sb([128, 128], BF16, name="identb")

    psb = [Tl(nc.alloc_psum_tensor("psb%d" % i, [128, 512], F32), Buf("psb%d" % i, excl=True)) for i in range(8)]
    pctr = [0]

    def PS():
        pctr[0] += 1
        return psb[pctr[0] % 8]

    def sel(t, pat, op, base, cmul):
        S.op("pool", lambda e: e.memset(t[:, :], 1.0), writes=[t])
        S.op("pool", lambda e: e.affine_select(out=t[:, :], in_=t[:, :], pattern=[[pat, 128]], compare_op=op,
                                               fill=0.0, base=base, channel_multiplier=cmul), reads=[t], writes=[t])

    sel(ident, -1, ALU.is_equal, 0, 1)
    S.op("pool", lambda e: e.tensor_copy(out=II[:, 0:128], in_=ident[:, :]), reads=[ident], writes=[II])
    S.op("pool", lambda e: e.tensor_copy(out=II[:, 128:256], in_=ident[:, :]), reads=[ident], writes=[II])
    S.op("pool", lambda e: e.tensor_copy(out=identb[:, :], in_=ident[:, :]), reads=[ident], writes=[identb])
    S.op("pool", lambda e: e.memset(ones[:, :], 1.0), writes=[ones])
    sel(mU, 1, ALU.is_gt, 0, -1)
    sel(mUi, 1, ALU.is_ge, 0, -1)
    sel(mL, -1, ALU.is_gt, 0, 1)
    sel(mLi, -1, ALU.is_ge, 0, 1)
    sel(cm[0], 0, ALU.is_ge, 64, -1)
    sel(cm[1], 0, ALU.is_ge, -64, 1)
    CI = [mUi, mLi]
    CS = [mU, mL]
    MN = [mL, mU]
    for d in range(2):
        S.op("dve", lambda e: e.tensor_tensor(out=A1[d][:, :], in0=CI[d][:, :], in1=cm[d][:, :], op=ALU.subtract),
             reads=[CI[d], cm[d]], writes=[A1[d]])
        S.op("dve", lambda e: e.tensor_tensor(out=A2[d][:, :], in0=CS[d][:, :], in1=cm[d][:, :], op=ALU.subtract),
             reads=[CS[d], cm[d]], writes=[A2[d]])
        S.op("dve", lambda e: e.tensor_tensor(out=A6[d][:, :], in0=ones[:, :], in1=CI[d][:, :], op=ALU.subtract),
             reads=[ones, CI[d]], writes=[A6[d]])
        for q, m in enumerate((CS[d], CI[d], CS[d], CI[d])):
            S.op("pool", lambda e: e.tensor_copy(out=mask4[d][:, q * 128:(q + 1) * 128], in_=m[:, :]),
                 reads=[m], writes=[mask4[d]])
    NEGT_S = [sb([128, 128], name="negts") for _ in range(2)]
    NEGT_I = [sb([128, 128], name="negti") for _ in range(2)]
    NEG_S = [sb([128, 128], name="negs") for _ in range(2)]
    for d in range(2):
        for dst, src in ((NEGT_S[d], CS[d]), (NEGT_I[d], CI[d]), (NEG_S[d], MN[d])):
            S.op("dve", lambda e: e.tensor_scalar(out=dst[:, :], in0=src[:, :], scalar1=-1.0, scalar2=1.0e5, op0=ALU.add,
                                                  op1=ALU.mult), reads=[src], writes=[dst])
    S.dma("sp", lambda e: e.dma_start(out=flagc[:, :], in_=flag_in), writes=[flagc])
    S.dma("sp", lambda e: e.dma_start(out=idxt[:, :], in_=idx_in), writes=[idxt])
    S.dma("sp", lambda e: e.dma_start(out=cmask[:, :], in_=cmask_in), writes=[cmask])
    S.dma("sp", lambda e: e.dma_start(out=LM[:, :, :], in_=lmask_in), writes=[LM])
    S.op("pool", lambda e: e.memset(zrow[:, :], 0.0), writes=[zrow])
    bP = Buf("P")
    for r0 in (0, T + 1):
        for c0 in range(0, NIN, 512):
            cw0 = min(512, NIN - c0)
            S.dma("sp", lambda e: e.dma_start(out=P[r0:r0 + 1, c0:c0 + cw0], in_=zrow[:, 0:cw0]), reads=[zrow], writes=[bP])

    def V(fn, r, w):
        S.op("dve", fn, r, w)

    def A(fn, r, w):
        S.op("act", fn, r, w)

    def G(fn, r, w):
        S.op("pool", fn, r, w)

    def MM(out, lhsT, rhs, r, w, start=True, stop=True):
        S.op("pe", lambda e: e.matmul(out, lhsT=lhsT, rhs=rhs, start=start, stop=stop), r, w)

    def TR(out, in_, r, w):
        S.op("pe", lambda e: e.transpose(out, in_, ident[:, :]), list(r) + [ident], w)

    def LD(out, in_, w, r=()):
        S.dma("sp", lambda e: e.dma_start(out=out, in_=in_), r, w)

    def ST(out, in_, r, w=()):
        S.dma("sp", lambda e: e.dma_start(out=out, in_=in_), r, w)

    def bc3(ap2, n):
        return ap2.unsqueeze(2).broadcast_to([ap2.shape[0], ap2.shape[1], n])

    def rsqrt(out_t, in_ap, scale, bias, r):
        A(lambda e: e.activation(out=out_t, in_=in_ap, func=AF.Ln, bias=bias, scale=scale), r, r)
        A(lambda e: e.activation(out=out_t, in_=out_t, func=AF.Exp, scale=-0.5), r, r)

    class Phase:
        def __init__(self):
            self.guards = []

        def sb(self, shape, dt=F32, name="p"):
            names[0] += 1
            nm = "%s_%d" % (name, names[0])
            gd = nc.sbuf_tensor(nm, list(shape), dt)
            t = gd.__enter__()
            self.guards.append(gd)
            return Tl(t, Buf(nm))

        def close(self):
            S.barrier()
            for gd in reversed(self.guards):
                gd.__exit__(None, None, None)

    def drive(gens, width=4):
        act = []
        gens = list(gens)
        while gens or act:
            while gens and len(act) < width:
                act.append(gens.pop(0))
            nxt = []
            for gen in act:
                try:
                    next(gen)
                    nxt.append(gen)
                except StopIteration:
                    pass
            act = nxt

    NU = 4

    def unit_bufs(ph, dec=False):
        return [(ph.sb([128, 512], name="chT"), ph.sb([128, 512], name="AT"),
                 ph.sb([128, 256], name="NQ"),
                 [ph.sb([128, 256], name="B") for _ in range(2)], ph.sb([128, 128], name="dg"),
                 ph.sb([128, 384], name="DM") if dec else None, ph.sb([128, 384], name="DR") if dec else None,
                 [ph.sb([128, 256], BF16, name="TW") for _ in range(2)], ph.sb([128, 256], BF16, name="YX"),
                 (ph.sb([128, 256], BF16, name="NQb"), ph.sb([128, 256], name="TWf")))
                for _ in range(NU)]

    def unit(bset, d, K, Vd, hcols, vcols, X, outs, h, dec=None):
        chT, AT, NQ0, B, dg, DM, DR, TW, YX, (NQb, TWf) = bset
        NQ = [NQ0]

        def run():
            if dec is not None:
                V(lambda e: e.tensor_scalar(out=DR[:, 0:128], in0=CS[d][:, :], scalar1=dec["g"], scalar2=None, op0=ALU.mult),
                  [CS[d], dec["b"]], [DR])
                V(lambda e: e.tensor_scalar(out=DR[:, 128:256], in0=CI[d][:, :], scalar1=dec["g"], scalar2=None, op0=ALU.mult),
                  [CI[d], dec["b"]], [DR])
                V(lambda e: e.tensor_scalar(out=DR[:, 256:384], in0=DR[:, 128:256], scalar1=-1.0, scalar2=None, op0=ALU.mult),
                  [DR], [DR])
                p = PS()
                for q, neg in enumerate((NEGT_S[d], NEGT_I[d], NEG_S[d])):
                    MM(p[:, q * 128:(q + 1) * 128], ones[:, :], DR[:, q * 128:(q + 1) * 128], [ones, DR], [p], start=True, stop=False)
                    MM(p[:, q * 128:(q + 1) * 128], ident[:, :], neg[:, :], [ident, neg], [p], start=False, stop=True)
                A(lambda e: e.activation(out=DM[:, 0:256], in_=p[:, 0:256], func=AF.Exp, bias=dec["ncw"], scale=1.0),
                  [p, dec["b"]], [DM])
                A(lambda e: e.activation(out=DM[:, 256:384], in_=p[:, 256:384], func=AF.Exp, bias=dec["cwx"], scale=1.0),
                  [p, dec["b"]], [DM])
                yield
            p = PS()
            for q, key in enumerate(("nm", "rm", "pm", "km")):
                TR(p[0:K, q * 128:(q + 1) * 128], X[key][:, hcols], [X[key]], [p])
            A(lambda e: e.activation(out=chT[0:K, :], in_=p[0:K, :], func=AF.Copy), [p], [chT])
            if h == 0:
                cut("u.1")
            yield
            p = PS()
            MM(p[:, 0:256], chT[0:K, 256:384], chT[0:K, 0:256], [chT], [p])
            MM(p[:, 256:512], chT[0:K, 384:512], chT[0:K, 0:256], [chT], [p])
            if dec is None:
                V(lambda e: e.tensor_tensor(out=AT[:, :], in0=p[:, :], in1=mask4[d][:, :], op=ALU.mult), [p, mask4[d]], [AT])
            else:
                V(lambda e: e.tensor_tensor(out=AT[:, 0:256], in0=p[:, 0:256], in1=DM[:, 0:256], op=ALU.mult), [p, DM], [AT])
                V(lambda e: e.tensor_tensor(out=AT[:, 256:512], in0=p[:, 256:512], in1=DM[:, 0:256], op=ALU.mult), [p, DM], [AT])
            p2 = PS()
            MM(p2[:, 0:128], chT[0:K, 0:128], chT[0:K, 256:384], [chT], [p2])
            if dec is None:
                V(lambda e: e.tensor_tensor(out=NQ[0][:, 0:128], in0=p2[:, 0:128], in1=MN[d][:, :], op=ALU.mult),
                  [p2, MN[d]], [NQ[0]])
            else:
                V(lambda e: e.tensor_tensor(out=NQ[0][:, 0:128], in0=p2[:, 0:128], in1=DM[:, 256:384], op=ALU.mult),
                  [p2, DM], [NQ[0]])
            G(lambda e: e.tensor_copy(out=NQ[0][:, 128:256], in_=AT[:, 0:128]), [AT], [NQ[0]])
            G(lambda e: e.tensor_copy(out=B[0][:, 0:K], in_=X["ntr"][:, hcols]), [X["ntr"]], [B[0]])
            if h == 0:
                cut("u.2")
            yield
            p = PS()
            MM(p[:, 0:Vd], AT[:, 256:384], X["v"][:, vcols], [AT, X["v"]], [p])
            A(lambda e: e.activation(out=B[0][:, K:K + Vd], in_=p[:, 0:Vd], func=AF.Copy), [p], [B[0]])
            if h == 0:
                cut("u.3")
            yield
            moff = 0 if d == 0 else 128
            G(lambda e: e.tensor_copy(out=NQb[:, :], in_=NQ0[:, :]), [NQ0], [NQb])
            G(lambda e: e.tensor_tensor(out=YX[:, :], in0=NQ0[:, :], in1=LM[:, 0, moff:moff + 256], op=ALU.mult), [NQ0, LM], [YX])
            G(lambda e: e.tensor_tensor(out=TW[1][:, :], in0=YX[:, :], in1=II[:, :], op=ALU.add), [YX, II], [TW[1]])
            cur = 1
            yield
            for lv in range(1, 7):
                twc = TW[cur]
                twn = TW[1 - cur] if lv < 6 else TWf
                p = PS()
                MM(p[:, 0:128], NQb[:, 128:256], twc[:, 0:128], [NQb, twc], [p])
                MM(p[:, 128:256], NQb[:, 0:128], twc[:, 128:256], [NQb, twc], [p])
                V(lambda e: e.tensor_tensor(out=YX[:, :], in0=p[:, 0:256], in1=LM[:, lv, moff:moff + 256], op=ALU.mult),
                  [p, LM], [YX])
                p2 = PS()
                MM(p2[:, 0:128], identb[:, :], twc[:, 0:128], [identb, twc], [p2], start=True, stop=False)
                MM(p2[:, 0:128], twc[:, 128:256], YX[:, 0:128], [twc, YX], [p2], start=False, stop=True)
                MM(p2[:, 128:256], identb[:, :], twc[:, 128:256], [identb, twc], [p2], start=True, stop=False)
                MM(p2[:, 128:256], twc[:, 0:128], YX[:, 128:256], [twc, YX], [p2], start=False, stop=True)
                A(lambda e: e.activation(out=twn[:, :], in_=p2[:, 0:256], func=AF.Copy), [p2], [twn])
                cur = 1 - cur
                yield
            p = PS()
            MM(p[:, 0:K + Vd], TWf[:, 128:256], B[0][:, 0:K + Vd], [TWf, B[0]], [p])
            A(lambda e: e.activation(out=B[1][:, 0:K + Vd], in_=p[:, 0:K + Vd], func=AF.Copy), [p], [B[1]])
            cur = 1
            Bf = B[cur]
            if h == 0:
                cut("u.4")
            G(lambda e: e.tensor_tensor(out=dg[0:K, 0:K], in0=ident[0:K, 0:K], in1=X["gam"][0:K, hcols], op=ALU.mult),
              [ident, X["gam"]], [dg])
            if h == 0:
                cut("u.41")
            p = PS()
            MM(p[0:K, 0:K], Bf[:, 0:K], X["ph"][:, hcols], [Bf, X["ph"]], [p])
            if h == 0:
                cut("u.415")
            MM(p[0:K, K:K + Vd], X["ph"][:, hcols], Bf[:, K:K + Vd], [Bf, X["ph"]], [p], start=True, stop=False)
            MM(p[0:K, K:K + Vd], X["kh"][:, hcols], X["v"][:, vcols], [X["kh"], X["v"]], [p], start=False, stop=True)
            if h == 0:
                cut("u.42")
            V(lambda e: e.tensor_tensor(out=outs["GT"][0:K, h * K:(h + 1) * K], in0=p[0:K, 0:K], in1=dg[0:K, 0:K], op=ALU.add),
              [p, dg], [outs["GT"]])
            A(lambda e: e.activation(out=outs["H"][0:K, h * Vd:(h + 1) * Vd], in_=p[0:K, K:K + Vd], func=AF.Copy),
              [p], [outs["H"]])
            if h == 0:
                cut("u.43")
            p = PS()
            MM(p[0:K, 0:128], Bf[:, 0:K], AT[:, 128:256], [Bf, AT], [p], start=True, stop=False)
            MM(p[0:K, 0:128], X["rtr"][:, hcols], ident[:, :], [X["rtr"], ident], [p], start=False, stop=True)
            MM(p[:, 128:128 + Vd], AT[:, 128:256], Bf[:, K:K + Vd], [Bf, AT], [p], start=True, stop=False)
            MM(p[:, 128:128 + Vd], AT[:, 384:512], X["v"][:, vcols], [AT, X["v"]], [p], start=False, stop=True)
            if h == 0:
                cut("u.44")
            A(lambda e: e.activation(out=outs["RT"][0:K, h * 128:(h + 1) * 128], in_=p[0:K, 0:128], func=AF.Copy),
              [p], [outs["RT"]])
            V(lambda e: e.tensor_copy(out=outs["Y0"][:, h * Vd:(h + 1) * Vd], in_=p[:, 128:128 + Vd]), [p], [outs["Y0"]])
            if h == 0:
                cut("u.5")
            yield
        return run()

    def summaries(ph, d, K, Vd, H, X, SUM, tile, units_bufs):
        outs = units_bufs["outs"]
        decs = units_bufs.get("decs")
        gens = [unit(units_bufs["sets"][h % NU], d, K, Vd, slice(h * K, (h + 1) * K), slice(h * Vd, (h + 1) * Vd), X, outs, h,
                     None if decs is None else decs[h]) for h in range(H)]
        drive(gens, NU)
        bsum = units_bufs["bsum"]
        ST(SUM["GT"][tile, d], outs["GT"][0:K, :], [outs["GT"]], [bsum])
        ST(SUM["H"][tile, d], outs["H"][0:K, :], [outs["H"]], [bsum])
        ST(SUM["RT"][tile, d], outs["RT"][0:K, 0:H * 128], [outs["RT"]], [bsum])
        ST(SUM["Y0"][tile, d], outs["Y0"][:, :], [outs["Y0"]], [bsum])

    if stop == "init":
        S.barrier()
        return nc
    bX = [Buf("xs0"), Buf("xs1"), Buf("xin"), Buf("yout")]
    bGPG = Buf("gpg")
    bSR = Buf("sr")
    bSD = Buf("sd")
    bYA = Buf("ya")
    bYB = Buf("yb")
    bBON = Buf("bon")
    bMG = Buf("mg")
    bFIN = Buf("fin")

    def layer(l):
        x_src, bxs = (x_in, bX[2]) if l == 0 else (XS[(l - 1) % 2], bX[(l - 1) % 2])
        x_dst, bxd = (y_out, bX[3]) if l == DEPTH - 1 else (XS[l % 2], bX[l % 2])

        def gather(dst_tile, i, src=x_src, bsrc=bxs):
            S.dma("pool", lambda e: e.indirect_dma_start(
                out=dst_tile[:, :], out_offset=None, in_=src,
                in_offset=bass.IndirectOffsetOnAxis(ap=idxt[:, l * NT + i:l * NT + i + 1], axis=0)),
                [idxt, bsrc], [dst_tile])

        ph = Phase()
        hT = ph.sb([128, 16, T], BF16, name="hT")
        ph1 = Phase()
        cT = ph1.sb([128, 16], name="cT")
        scB = ph1.sb([128, 16, 128], name="scB")
        modt = ph1.sb([128, 3 * D], name="mod")
        bmod = [ph1.sb([128, 256], name="bmod") for _ in range(2)]
        gpre = ph1.sb([128, D], name="gpre")
        wst = [ph1.sb([128, 16, 256], name="wst") for _ in range(2)]
        LD(cT[:, :], cvec, [cT])
        A(lambda e: e.activation(out=cT[:, :], in_=cT[:, :], func=AF.Silu), [cT], [cT])
        V(lambda e: e.tensor_copy(out=scB[:, :, :], in_=bc3(cT[:, :], 128)), [cT], [scB])
        LD(gpre[:, :], g_pre[l].partition_broadcast(128), [gpre])
        wm = w_mod[l].rearrange("(kc p) n -> p kc n", p=128)
        for cg in range(24):
            w = wst[cg % 2]
            bm_ = bmod[cg % 2]
            LD(w[:, :, :], wm[:, :, cg * 256:(cg + 1) * 256], [w])
            LD(bm_[:, :], b_mod[l, cg * 256:(cg + 1) * 256].partition_broadcast(128), [bm_])
            p = PS()
            for kc in range(16):
                MM(p[:, 0:256], scB[:, kc, :], w[:, kc, :], [scB, w], [p], start=(kc == 0), stop=(kc == 15))
            V(lambda e: e.tensor_tensor(out=modt[:, cg * 256:(cg + 1) * 256], in0=p[:, 0:256],
                                        in1=bm_[:, :], op=ALU.add), [p, bm_], [modt])
        V(lambda e: e.scalar_tensor_tensor(out=modt[:, D:2 * D], in0=modt[:, D:2 * D], scalar=1.0, in1=gpre[:, :],
                                           op0=ALU.add, op1=ALU.mult), [modt, gpre], [modt])
        V(lambda e: e.tensor_scalar(out=modt[:, D:2 * D], in0=modt[:, D:2 * D], scalar1=float(D ** 0.5), scalar2=None,
                                    op0=ALU.mult), [modt], [modt])
        LD(gpre[:, :], g_post[l].partition_broadcast(128), [gpre])
        V(lambda e: e.scalar_tensor_tensor(out=modt[:, 2 * D:3 * D], in0=modt[:, 2 * D:3 * D], scalar=float(D ** 0.5),
                                           in1=gpre[:, :], op0=ALU.mult, op1=ALU.mult), [modt, gpre], [modt])
        ST(GPG, modt[:, 2 * D:3 * D], [modt], [bGPG])
        xt = [ph1.sb([128, D], name="xt") for _ in range(2)]
        hh = [ph1.sb([128, D], name="hh") for _ in range(2)]
        ssq = [ph1.sb([128, 1], name="ssq") for _ in range(2)]
        for i in range(NT):
            x_t, h_t, s_t = xt[i % 2], hh[i % 2], ssq[i % 2]
            gather(x_t, i)
            A(lambda e: e.activation(out=h_t[:, :], in_=x_t[:, :], func=AF.Square, accum_out=s_t[:, :]), [x_t], [h_t, s_t])
            rsqrt(s_t[:, :], s_t[:, :], 1.0, float(EPS * D), [s_t])
            V(lambda e: e.scalar_tensor_tensor(out=h_t[:, :], in0=x_t[:, :], scalar=s_t[:, 0:1], in1=modt[:, D:2 * D],
                                               op0=ALU.mult, op1=ALU.mult), [x_t, s_t, modt], [h_t])
            G(lambda e: e.tensor_tensor(out=h_t[:, :], in0=h_t[:, :], in1=modt[:, 0:D], op=ALU.add), [h_t, modt], [h_t])
            for c4 in range(4):
                p = PS()
                for q in range(4):
                    kc = c4 * 4 + q
                    TR(p[:, q * 128:(q + 1) * 128], h_t[:, kc * 128:(kc + 1) * 128], [h_t], [p])
                A(lambda e: e.activation(out=hT[:, c4 * 4:(c4 + 1) * 4, i * 128:(i + 1) * 128],
                                         in_=p[:, :].rearrange("p (q t) -> p q t", q=4), func=AF.Copy), [p], [hT])
        ph1.close()
        if stop == "P1":
            S.barrier()
            return nc

        ph2 = Phase()
        wst = [ph2.sb([128, 16, 512], name="wst") for _ in range(2)]
        wbf = [ph2.sb([128, 16, 512], BF16, name="wbf") for _ in range(2)]
        ost = [ph2.sb([128, 512], name="ost") for _ in range(4)]
        wi = w_in[l].rearrange("(kc p) n -> p kc n", p=128)
        oc = 0
        ncg = (NIN + 511) // 512

        def fetch(cg):
            c0 = min(cg * 512, NIN - 512)
            w, wb = wst[cg % 2], wbf[cg % 2]
            for k4 in range(4):
                LD(w[:, k4 * 4:(k4 + 1) * 4, :], wi[:, k4 * 4:(k4 + 1) * 4, c0:c0 + 512], [w])
            for k4 in range(4):
                G(lambda e: e.tensor_copy(out=wb[:, k4 * 4:(k4 + 1) * 4, :], in_=w[:, k4 * 4:(k4 + 1) * 4, :]), [w], [wb])
        fetch(0)
        for cg in range(ncg):
            if cg + 1 < ncg:
                fetch(cg + 1)
            c0 = min(cg * 512, NIN - 512)
            cw = 512
            wb = wbf[cg % 2]
            for i in range(NT):
                p = PS()
                for kc in range(16):
                    MM(p[:, 0:cw], hT[:, kc, i * 128:(i + 1) * 128], wb[:, kc, 0:cw], [hT, wb], [p],
                       start=(kc == 0), stop=(kc == 15))
                o = ost[oc % 4]
                oc += 1
                A(lambda e: e.activation(out=o[:, 0:cw], in_=p[:, 0:cw], func=AF.Copy), [p], [o])
                ST(P[1 + i * 128:1 + (i + 1) * 128, c0:c0 + cw], o[:, 0:cw], [o], [bP])
        ph2.close()
        ph.close()
        if stop == "P2":
            S.barrier()
            return nc

        ph3 = Phase()
        cst = {}
        for nm_, src in (("k_k", k_k[l]), ("k_a", k_a[l]), ("r_k", r_k[l])):
            cst[nm_] = ph3.sb([128, AW], name=nm_)
            LD(cst[nm_][:, :], src.partition_broadcast(128), [cst[nm_]])
        wup = [ph3.sb([128, AW], name="wup") for _ in range(2)]
        aup = [ph3.sb([128, AW], name="aup") for _ in range(2)]
        for d in range(2):
            LD(wup[d][0:96, :], w_up[l, d], [wup[d]])
            LD(wup[d][96:97, :], w0[l, d:d + 1, :], [wup[d]])
            LD(aup[d][0:96, :], a_up[l, d], [aup[d]])
            LD(aup[d][96:97, :], a0[l, d:d + 1, :], [aup[d]])
        usets = unit_bufs(ph3)
        rkv = ph3.sb([128, 3072], name="rkv")
        lo = ph3.sb([128, 384], name="lo")
        loT = ph3.sb([128, 4, 128], name="loT")
        G(lambda e: e.memset(loT[:, :, :], 1.0), [], [loT])
        kx = ph3.sb([128, AW], name="kx")
        kk = ph3.sb([128, AW], name="kk")
        sm = ph3.sb([128, 64], name="sm")
        bon = ph3.sb([128, 32], name="bon")
        sg = [ph3.sb([128, AW], name="sg") for _ in range(2)]
        ad = [ph3.sb([128, AW], name="ad") for _ in range(2)]
        kd = [ph3.sb([128, AW], name="kd") for _ in range(2)]
        pd = [ph3.sb([128, AW], name="pd") for _ in range(2)]
        tmp = kx
        ex = ad[0]
        X = {k_: ph3.sb([128, AW], name=k_) for k_ in ("nm", "rm", "pm", "km", "ntr", "rtr", "ph", "kh")}
        X["gam"] = ad[1]
        outs = {"GT": ph3.sb([128, 1024], name="oGT"), "H": ph3.sb([128, 1024], name="oH"),
                "RT": ph3.sb([128, 2048], name="oRT"), "Y0": ph3.sb([128, 1024], name="oY0")}
        cut("p3.0")
        for i in range(NT):
            rows = slice(1 + i * 128, 1 + (i + 1) * 128)
            LD(rkv[:, :], P[rows, 0:3072], [rkv], [bP])
            LD(lo[:, :], P[rows, C_LO:C_LO + 384], [lo], [bP])
            cut("p3.1")
            A(lambda e: e.activation(out=lo[:, 0:192], in_=lo[:, 0:192], func=AF.Tanh), [lo], [lo])
            p = PS()
            for q in range(4):
                TR(p[0:96, q * 128:(q + 1) * 128], lo[:, q * 96:(q + 1) * 96], [lo], [p])
            A(lambda e: e.activation(out=loT[0:96, :, :], in_=p[0:96, :].rearrange("p (q t) -> p q t", q=4), func=AF.Copy),
              [p], [loT])
            cut("p3.2")
            for d in range(2):
                for hf in range(2):
                    cs = slice(hf * 512, (hf + 1) * 512)
                    p = PS()
                    MM(p[:, :], loT[0:97, d, :], wup[d][0:97, cs], [loT, wup[d]], [p])
                    A(lambda e: e.activation(out=sg[d][:, cs], in_=p[:, :], func=AF.Sigmoid), [p], [sg[d]])
                    p = PS()
                    MM(p[:, :], loT[0:97, 2 + d, :], aup[d][0:97, cs], [loT, aup[d]], [p])
                    A(lambda e: e.activation(out=ad[d][:, cs], in_=p[:, :], func=AF.Sigmoid), [p], [ad[d]])
            cut("p3.3")
            r_ap, k_ap, v_ap = rkv[:, 0:1024], rkv[:, 1024:2048], rkv[:, 2048:3072]
            V(lambda e: e.tensor_tensor(out=kx[:, :], in0=k_ap, in1=cst["k_k"][:, :], op=ALU.mult), [rkv, cst["k_k"]], [kx])
            A(lambda e: e.activation(out=kk[:, :], in_=kx[:, :], func=AF.Square), [kx], [kk])
            V(lambda e: e.tensor_reduce(out=sm[:, 0:16], in_=kk[:, :].rearrange("p (h k) -> p h k", k=64), axis=AX.X,
                                        op=ALU.add), [kk], [sm])
            rsqrt(sm[:, 0:16], sm[:, 0:16], 1.0, float(EPS), [sm])
            V(lambda e: e.tensor_tensor(out=kk[:, :].rearrange("p (h k) -> p h k", k=64),
                                        in0=kx[:, :].rearrange("p (h k) -> p h k", k=64),
                                        in1=bc3(sm[:, 0:16], 64), op=ALU.mult), [kx, sm], [kk])
            for d in range(2):
                V(lambda e: e.scalar_tensor_tensor(out=tmp[:, :], in0=ad[d][:, :], scalar=-1.0, in1=cst["k_a"][:, :],
                                                   op0=ALU.add, op1=ALU.mult), [ad[d], cst["k_a"]], [tmp])
                V(lambda e: e.scalar_tensor_tensor(out=kd[d][:, :], in0=tmp[:, :], scalar=1.0, in1=k_ap,
                                                   op0=ALU.add, op1=ALU.mult), [tmp, rkv], [kd[d]])
                G(lambda e: e.tensor_tensor(out=pd[d][:, :], in0=kk[:, :], in1=ad[d][:, :], op=ALU.mult), [kk, ad[d]], [pd[d]])
                G(lambda e: e.tensor_tensor(out=tmp[:, :], in0=kd[d][:, :], in1=cst["r_k"][:, :], op=ALU.mult),
                  [kd[d], cst["r_k"]], [tmp])
                V(lambda e: e.tensor_tensor(out=tmp[:, :], in0=tmp[:, :], in1=r_ap, op=ALU.mult), [tmp, rkv], [tmp])
                V(lambda e: e.tensor_reduce(out=bon[:, d * 16:(d + 1) * 16], in_=tmp[:, :].rearrange("p (h k) -> p h k", k=64),
                                            axis=AX.X, op=ALU.add), [tmp], [bon])
            ST(BON[i * 128:(i + 1) * 128, :], bon[:, :], [bon], [bBON])
            cut("p3.5")
            for d in range(2):
                def expo(lhs, scale, fn2):
                    for hf in range(2):
                        cs = slice(hf * 512, (hf + 1) * 512)
                        p = PS()
                        MM(p[:, :], lhs[:, :], sg[d][:, cs], [lhs, sg[d]], [p])
                        A(lambda e: e.activation(out=ex[:, cs], in_=p[:, :], func=AF.Exp, scale=scale), [p], [ex])
                    fn2()
                sc = -DECAY_SCALE
                expo(A1[d], sc, lambda: V(lambda e: e.tensor_tensor(out=X["rm"][:, :], in0=r_ap, in1=ex[:, :], op=ALU.mult),
                                          [rkv, ex], [X["rm"]]))
                expo(A2[d], sc, lambda: V(lambda e: e.scalar_tensor_tensor(out=X["nm"][:, :], in0=kk[:, :], scalar=-1.0,
                                                                           in1=ex[:, :], op0=ALU.mult, op1=ALU.mult),
                                          [kk, ex], [X["nm"]]))

                def f3():
                    V(lambda e: e.tensor_tensor(out=X["pm"][:, :], in0=pd[d][:, :], in1=ex[:, :], op=ALU.mult), [pd[d], ex], [X["pm"]])
                    G(lambda e: e.tensor_tensor(out=X["km"][:, :], in0=kd[d][:, :], in1=ex[:, :], op=ALU.mult), [kd[d], ex], [X["km"]])
                expo(A1[d], -sc, f3)
                expo(CI[d], sc, lambda: V(lambda e: e.tensor_tensor(out=X["rtr"][:, :], in0=r_ap, in1=ex[:, :], op=ALU.mult),
                                          [rkv, ex], [X["rtr"]]))
                expo(CS[d], sc, lambda: V(lambda e: e.scalar_tensor_tensor(out=X["ntr"][:, :], in0=kk[:, :], scalar=-1.0,
                                                                           in1=ex[:, :], op0=ALU.mult, op1=ALU.mult),
                                          [kk, ex], [X["ntr"]]))

                def f6():
                    V(lambda e: e.tensor_tensor(out=X["ph"][:, :], in0=pd[d][:, :], in1=ex[:, :], op=ALU.mult), [pd[d], ex], [X["ph"]])
                    G(lambda e: e.tensor_tensor(out=X["kh"][:, :], in0=kd[d][:, :], in1=ex[:, :], op=ALU.mult), [kd[d], ex], [X["kh"]])
                expo(A6[d], sc, f6)
                expo(ones, sc, lambda: G(lambda e: e.tensor_copy(out=X["gam"][:, :], in_=ex[:, :]), [ex], [X["gam"]]))
                cut("p3.7")
                Xd = dict(X)
                Xd["v"] = Tl(rkv.t[:, 2048:3072], rkv.b)
                summaries(ph3, d, 64, 64, 16, Xd, SR, i, {"outs": outs, "bsum": bSR, "sets": usets})
                cut("p3.8")
        ph3.close()
        if stop == "P3":
            S.barrier()
            return nc

        ph4 = Phase()
        cw_ = [ph4.sb([128, 3072], name="convw") for _ in range(3)]
        for j in range(3):
            LD(cw_[j][:, :], conv_w[l, j].partition_broadcast(128), [cw_[j]])
        usets = unit_bufs(ph4, dec=True)
        alg = ph4.sb([128, 16], name="alg")
        dtb = ph4.sb([128, 16], name="dtb")
        LD(alg[:, :], a_log[l].partition_broadcast(128), [alg])
        LD(dtb[:, :], dt_bias[l].partition_broadcast(128), [dtb])
        A(lambda e: e.activation(out=alg[:, :], in_=alg[:, :], func=AF.Exp), [alg], [alg])
        acc = ph4.sb([128, 3072], name="acc")
        xw = [ph4.sb([128, 3072], name="xw"), acc, ph4.sb([128, 3072], name="xw")]
        ba = ph4.sb([128, 32], name="ba")
        sm = ph4.sb([128, 16 * 12], name="sm4")
        X = {k_: ph4.sb([128, AW], name=k_) for k_ in ("pm", "km", "ntr", "rtr", "ph", "kh", "gam")}
        outs = {"GT": ph4.sb([128, 1024], name="oGT"), "H": ph4.sb([128, 1024], name="oH"),
                "RT": ph4.sb([128, 1024], name="oRT"), "Y0": ph4.sb([128, 1024], name="oY0")}
        for i in range(NT):
            for j in range(3):
                r0 = i * 128 + j
                LD(xw[j][:, :], P[r0:r0 + 128, C_QKV:C_QKV + 3072], [xw[j]], [bP])
            LD(ba[:, :], P[1 + i * 128:1 + (i + 1) * 128, C_BETA:C_BETA + 32], [ba], [bP])
            V(lambda e: e.tensor_tensor(out=acc[:, :], in0=acc[:, :], in1=cw_[1][:, :], op=ALU.mult), [acc, cw_[1]], [acc])
            V(lambda e: e.scalar_tensor_tensor(out=xw[0][:, :], in0=xw[0][:, :], scalar=cmask[:, 2 * i:2 * i + 1], in1=cw_[0][:, :],
                                               op0=ALU.mult, op1=ALU.mult), [xw[0], cmask, cw_[0]], [xw[0]])
            G(lambda e: e.tensor_tensor(out=acc[:, :], in0=acc[:, :], in1=xw[0][:, :], op=ALU.add), [acc, xw[0]], [acc])
            V(lambda e: e.scalar_tensor_tensor(out=xw[2][:, :], in0=xw[2][:, :], scalar=cmask[:, 2 * i + 1:2 * i + 2],
                                               in1=cw_[2][:, :], op0=ALU.mult, op1=ALU.mult), [xw[2], cmask, cw_[2]], [xw[2]])
            G(lambda e: e.tensor_tensor(out=acc[:, :], in0=acc[:, :], in1=xw[2][:, :], op=ALU.add), [acc, xw[2]], [acc])
            tmp = xw[0]
            A(lambda e: e.activation(out=acc[:, :], in_=acc[:, :], func=AF.Silu), [acc], [acc])
            A(lambda e: e.activation(out=tmp[:, 0:2048], in_=acc[:, 0:2048], func=AF.Square), [acc], [tmp])
            V(lambda e: e.tensor_reduce(out=sm[:, 0:16], in_=tmp[:, 0:2048].rearrange("p (h k) -> p h k", k=128), axis=AX.X,
                                        op=ALU.add), [tmp], [sm])
            rsqrt(sm[:, 0:16], sm[:, 0:16], 1.0, float(EPS), [sm])
            V(lambda e: e.tensor_scalar(out=sm[:, 0:8], in0=sm[:, 0:8], scalar1=float(128 ** -0.5), scalar2=None, op0=ALU.mult),
              [sm], [sm])
            V(lambda e: e.tensor_tensor(out=acc[:, 0:2048].rearrange("p (h k) -> p h k", k=128),
                                        in0=acc[:, 0:2048].rearrange("p (h k) -> p h k", k=128),
                                        in1=bc3(sm[:, 0:16], 128), op=ALU.mult), [acc, sm], [acc])
            qn, kn = acc[:, 0:1024], acc[:, 1024:2048]
            A(lambda e: e.activation(out=sm[:, 16:32], in_=ba[:, 0:16], func=AF.Sigmoid), [ba], [sm])
            V(lambda e: e.tensor_tensor(out=sm[:, 32:48], in0=ba[:, 16:32], in1=dtb[:, :], op=ALU.add), [ba, dtb], [sm])
            A(lambda e: e.activation(out=sm[:, 32:48], in_=sm[:, 32:48], func=AF.Exp), [sm], [sm])
            A(lambda e: e.activation(out=sm[:, 32:48], in_=sm[:, 32:48], func=AF.Ln, bias=1.0), [sm], [sm])
            V(lambda e: e.scalar_tensor_tensor(out=sm[:, 32:48], in0=sm[:, 32:48], scalar=-1.0, in1=alg[:, :],
                                               op0=ALU.mult, op1=ALU.mult), [sm, alg], [sm])
            A(lambda e: e.activation(out=sm[:, 48:64], in_=sm[:, 32:48], func=AF.Exp), [sm], [sm])
            V(lambda e: e.scalar_tensor_tensor(out=sm[:, 48:64], in0=sm[:, 48:64], scalar=-1.0, in1=sm[:, 16:32],
                                               op0=ALU.mult, op1=ALU.mult), [sm], [sm])
            for d in range(2):
                gcol = sm[:, 32 + d * 8:32 + (d + 1) * 8]
                p = PS()
                for q, lhs in ((3, CI[d]), (4, CS[d]), (5, A6[d]), (6, ones)):
                    MM(p[:, q * 8:(q + 1) * 8], lhs[:, :], gcol, [lhs, sm], [p])
                A(lambda e: e.activation(out=sm[:, 64 + 24:64 + 56], in_=p[:, 24:56], func=AF.Exp), [p], [sm])
                V(lambda e: e.tensor_scalar(out=sm[:, 176:184], in0=p[:, 24:32], scalar1=-1.0, scalar2=None, op0=ALU.mult),
                  [p], [sm])
                V(lambda e: e.tensor_copy(out=sm[:, 168:176], in_=p[:, 32:40]), [p], [sm])
                E = lambda q: sm[:, 64 + q * 8:64 + (q + 1) * 8]
                bet = sm[:, 16 + d * 8:16 + (d + 1) * 8]
                nb = sm[:, 48 + d * 8:48 + (d + 1) * 8]
                sc_ = lambda q: sm[:, 128 + q * 8:128 + (q + 1) * 8]
                V(lambda e: e.tensor_tensor(out=sc_(2), in0=nb, in1=E(5), op=ALU.mult), [sm], [sm])
                V(lambda e: e.tensor_tensor(out=sc_(3), in0=bet, in1=E(5), op=ALU.mult), [sm], [sm])

                def bm(dst, src_ap, s_ap, eng):
                    eng(lambda e: e.tensor_tensor(out=dst[:, :].rearrange("p (h k) -> p h k", k=128),
                                                  in0=src_ap.rearrange("p (h k) -> p h k", k=128),
                                                  in1=bc3(s_ap, 128), op=ALU.mult), [acc, sm], [dst])
                bm(X["pm"], kn, nb, V)
                bm(X["km"], kn, bet, G)
                bm(X["rtr"], qn, E(3), V)
                bm(X["ntr"], kn, E(4), G)
                bm(X["ph"], kn, sc_(2), V)
                bm(X["kh"], kn, sc_(3), G)
                V(lambda e: e.tensor_copy(out=X["gam"][:, :].rearrange("p (h k) -> p h k", k=128), in_=bc3(E(6), 128)),
                  [sm], [X["gam"]])
                Xd = dict(X)
                Xd["v"] = Tl(acc.t[:, 2048:3072], acc.b)
                Xd["rm"] = Tl(acc.t[:, 0:1024], acc.b)
                Xd["nm"] = Tl(acc.t[:, 1024:2048], acc.b)
                decs = [dict(g=sm[:, 32 + d * 8 + hh_i:32 + d * 8 + hh_i + 1], ncw=sm[:, 176 + hh_i:177 + hh_i],
                             cwx=sm[:, 168 + hh_i:169 + hh_i], b=sm) for hh_i in range(8)]
                summaries(ph4, d, 128, 128, 8, Xd, SD, i, {"outs": outs, "bsum": bSD, "sets": usets, "decs": decs})
        ph4.close()
        if stop == "P4":
            S.barrier()
            return nc

        ph5 = Phase()
        for (K, Hn, Vd, SUM, bS, init, fin, YD, bY) in ((64, 16, 64, SR, bSR, init_r, fin_r, YA, bYA),
                                                        (128, 8, 128, SD, bSD, init_d, fin_d, YB, bYB)):
            M = [[ph5.sb([128, 1024], name="M") for _ in range(2)] for _ in range(2)]
            gt = [ph5.sb([128, 1024], name="gt") for _ in range(2)]
            hh_ = [ph5.sb([128, 1024], name="h") for _ in range(2)]
            rt = [ph5.sb([128, Hn * 128], name="rt") for _ in range(2)]
            y0 = [ph5.sb([128, 1024], name="y0") for _ in range(2)]
            yo = [ph5.sb([128, 1024], name="yo") for _ in range(2)]
            cur = [0, 0]
            for d in range(2):
                LD(M[d][0][0:K, :], init[l, d], [M[d][0]])
            hpj = 512 // Vd
            for step in range(NT):
                for d in range(2):
                    c = step if d == 0 else NT - 1 - step
                    Mc, Mn = M[d][cur[d]], M[d][1 - cur[d]]
                    if step > 0 and step % 2 == 0:
                        slot = (c // 2 - 1) if d == 0 else ((c + 1) // 2)
                        ST(fin[l, slot, d], Mc[0:K, :], [Mc], [bFIN])
                        V(lambda e: e.tensor_scalar(out=Mc[0:K, :], in0=Mc[0:K, :], scalar1=flagc[0:K, 0:1], scalar2=None,
                                                    op0=ALU.mult), [Mc, flagc], [Mc])
                    LD(gt[d][0:K, :], SUM["GT"][c, d], [gt[d]], [bS])
                    LD(hh_[d][0:K, :], SUM["H"][c, d], [hh_[d]], [bS])
                    LD(rt[d][0:K, :], SUM["RT"][c, d], [rt[d]], [bS])
                    LD(y0[d][:, :], SUM["Y0"][c, d], [y0[d]], [bS])
                    for j in range(Hn // hpj):
                        p = PS()
                        for hq in range(hpj):
                            h = j * hpj + hq
                            MM(p[:, hq * Vd:(hq + 1) * Vd], rt[d][0:K, h * 128:(h + 1) * 128], Mc[0:K, h * Vd:(h + 1) * Vd],
                               [rt[d], Mc], [p])
                        V(lambda e: e.tensor_tensor(out=yo[d][:, j * 512:(j + 1) * 512], in0=p[:, :],
                                                    in1=y0[d][:, j * 512:(j + 1) * 512], op=ALU.add), [p, y0[d]], [yo[d]])
                        p = PS()
                        for hq in range(hpj):
                            h = j * hpj + hq
                            MM(p[0:K, hq * Vd:(hq + 1) * Vd], gt[d][0:K, h * K:(h + 1) * K], Mc[0:K, h * Vd:(h + 1) * Vd],
                               [gt[d], Mc], [p])
                        V(lambda e: e.tensor_tensor(out=Mn[0:K, j * 512:(j + 1) * 512], in0=p[0:K, :],
                                                    in1=hh_[d][0:K, j * 512:(j + 1) * 512], op=ALU.add), [p, hh_[d]], [Mn])
                    ST(YD[d, c * 128:(c + 1) * 128, :], yo[d][:, :], [yo[d]], [bY])
                    cur[d] = 1 - cur[d]
            for d in range(2):
                slot = 7 if d == 0 else 0
                ST(fin[l, slot, d], M[d][cur[d]][0:K, :], [M[d][cur[d]]], [bFIN])
        ph5.close()
        if stop == "P5":
            S.barrier()
            return nc

        ph6 = Phase()
        yT = ph6.sb([128, 16, T], BF16, name="yT")
        ph6a = Phase()
        gnw = ph6a.sb([128, AW], name="gnw")
        gnb = ph6a.sb([128, AW], name="gnb")
        onw = ph6a.sb([128, 128], name="onw")
        LD(gnw[:, :], gn_w[l].partition_broadcast(128), [gnw])
        LD(gnb[:, :], gn_b[l].partition_broadcast(128), [gnb])
        LD(onw[:, :], o_norm_w[l].partition_broadcast(128), [onw])
        V(lambda e: e.tensor_scalar(out=gnw[:, :], in0=gnw[:, :], scalar1=8.0, scalar2=None, op0=ALU.mult), [gnw], [gnw])
        V(lambda e: e.tensor_scalar(out=onw[:, :], in0=onw[:, :], scalar1=float(128 ** 0.5), scalar2=None, op0=ALU.mult),
          [onw], [onw])
        yf = ph6a.sb([128, AW], name="yf")
        yb_ = ph6a.sb([128, AW], name="yb")
        vz = ph6a.sb([128, 2048], name="vz")
        zb = ph6a.sb([128, AW], name="zb")
        bon = ph6a.sb([128, 32], name="bon6")
        t6 = ph6a.sb([128, AW], name="t6")
        sm = ph6a.sb([128, 64], name="sm6")
        for i in range(NT):
            rows = slice(i * 128, (i + 1) * 128)
            prow = slice(1 + i * 128, 1 + (i + 1) * 128)
            LD(yf[:, :], YA[0, rows, :], [yf], [bYA])
            LD(yb_[:, :], YA[1, rows, :], [yb_], [bYA])
            LD(vz[:, :], P[prow, C_V:C_V + 2048], [vz], [bP])
            LD(bon[:, :], BON[rows, :], [bon], [bBON])
            V(lambda e: e.tensor_tensor(out=yf[:, :], in0=yf[:, :], in1=yb_[:, :], op=ALU.add), [yf, yb_], [yf])
            V(lambda e: e.tensor_tensor(out=bon[:, 0:16], in0=bon[:, 0:16], in1=bon[:, 16:32], op=ALU.add), [bon], [bon])
            V(lambda e: e.tensor_tensor(out=t6[:, :].rearrange("p (h k) -> p h k", k=64),
                                        in0=vz[:, 0:1024].rearrange("p (h k) -> p h k", k=64),
                                        in1=bc3(bon[:, 0:16], 64), op=ALU.mult), [vz, bon], [t6])
            G(lambda e: e.tensor_tensor(out=yf[:, :], in0=yf[:, :], in1=t6[:, :], op=ALU.add), [yf, t6], [yf])
            V(lambda e: e.tensor_reduce(out=sm[:, 0:16], in_=yf[:, :].rearrange("p (h k) -> p h k", k=64), axis=AX.X, op=ALU.add),
              [yf], [sm])
            V(lambda e: e.tensor_scalar(out=sm[:, 0:16], in0=sm[:, 0:16], scalar1=-1.0 / 64, scalar2=None, op0=ALU.mult), [sm], [sm])
            V(lambda e: e.tensor_tensor(out=yf[:, :].rearrange("p (h k) -> p h k", k=64),
                                        in0=yf[:, :].rearrange("p (h k) -> p h k", k=64),
                                        in1=bc3(sm[:, 0:16], 64), op=ALU.add), [yf, sm], [yf])
            A(lambda e: e.activation(out=t6[:, :], in_=yf[:, :], func=AF.Square), [yf], [t6])
            V(lambda e: e.tensor_reduce(out=sm[:, 16:32], in_=t6[:, :].rearrange("p (h k) -> p h k", k=64), axis=AX.X, op=ALU.add),
              [t6], [sm])
            rsqrt(sm[:, 16:32], sm[:, 16:32], 1.0, float(GN_EPS * 64), [sm])
            V(lambda e: e.tensor_tensor(out=yf[:, :].rearrange("p (h k) -> p h k", k=64),
                                        in0=yf[:, :].rearrange("p (h k) -> p h k", k=64),
                                        in1=bc3(sm[:, 16:32], 64), op=ALU.mult), [yf, sm], [yf])
            V(lambda e: e.tensor_tensor(out=yf[:, :], in0=yf[:, :], in1=gnw[:, :], op=ALU.mult), [yf, gnw], [yf])
            G(lambda e: e.tensor_tensor(out=yf[:, :], in0=yf[:, :], in1=gnb[:, :], op=ALU.add), [yf, gnb], [yf])
            A(lambda e: e.activation(out=vz[:, 1024:2048], in_=vz[:, 1024:2048], func=AF.Silu), [vz], [vz])
            V(lambda e: e.tensor_tensor(out=yf[:, :], in0=yf[:, :], in1=vz[:, 1024:2048], op=ALU.mult), [yf, vz], [yf])
            for c4 in range(2):
                p = PS()
                for q in range(4):
                    kc = c4 * 4 + q
                    TR(p[:, q * 128:(q + 1) * 128], yf[:, kc * 128:(kc + 1) * 128], [yf], [p])
                A(lambda e: e.activation(out=yT[:, c4 * 4:(c4 + 1) * 4, i * 128:(i + 1) * 128],
                                         in_=p[:, :].rearrange("p (q t) -> p q t", q=4), func=AF.Copy), [p], [yT])
            LD(yf[:, :], YB[0, rows, :], [yf], [bYB])
            LD(yb_[:, :], YB[1, rows, :], [yb_], [bYB])
            LD(zb[:, :], P[prow, C_ZB:C_ZB + 1024], [zb], [bP])
            V(lambda e: e.tensor_tensor(out=yf[:, :], in0=yf[:, :], in1=yb_[:, :], op=ALU.add), [yf, yb_], [yf])
            A(lambda e: e.activation(out=t6[:, :], in_=yf[:, :], func=AF.Square), [yf], [t6])
            V(lambda e: e.tensor_reduce(out=sm[:, 32:40], in_=t6[:, :].rearrange("p (h k) -> p h k", k=128), axis=AX.X, op=ALU.add),
              [t6], [sm])
            rsqrt(sm[:, 32:40], sm[:, 32:40], 1.0, float(EPS * 128), [sm])
            V(lambda e: e.tensor_tensor(out=yf[:, :].rearrange("p (h k) -> p h k", k=128),
                                        in0=yf[:, :].rearrange("p (h k) -> p h k", k=128),
                                        in1=bc3(sm[:, 32:40], 128), op=ALU.mult), [yf, sm], [yf])
            V(lambda e: e.tensor_tensor(out=yf[:, :].rearrange("p (h k) -> p h k", k=128),
                                        in0=yf[:, :].rearrange("p (h k) -> p h k", k=128),
                                        in1=onw[:, :].unsqueeze(1).broadcast_to([128, 8, 128]), op=ALU.mult), [yf, onw], [yf])
            A(lambda e: e.activation(out=zb[:, :], in_=zb[:, :], func=AF.Silu), [zb], [zb])
            V(lambda e: e.tensor_tensor(out=yf[:, :], in0=yf[:, :], in1=zb[:, :], op=ALU.mult), [yf, zb], [yf])
            for c4 in range(2):
                p = PS()
                for q in range(4):
                    kc = c4 * 4 + q
                    TR(p[:, q * 128:(q + 1) * 128], yf[:, kc * 128:(kc + 1) * 128], [yf], [p])
                A(lambda e: e.activation(out=yT[:, 8 + c4 * 4:8 + (c4 + 1) * 4, i * 128:(i + 1) * 128],
                                         in_=p[:, :].rearrange("p (q t) -> p q t", q=4), func=AF.Copy), [p], [yT])
        ph6a.close()
        if stop == "P6a":
            S.barrier()
            return nc

        ph6b = Phase()
        wst = [ph6b.sb([128, 16, 512], name="wst")] * 2
        wbf = [ph6b.sb([128, 16, 512], BF16, name="wbf") for _ in range(2)]
        gts = [ph6b.sb([128, 1024], name="gts") for _ in range(2)]
        mgt = [ph6b.sb([128, 512], name="mgt") for _ in range(2)]
        t2 = [ph6b.sb([128, 512], name="t2") for _ in range(2)]
        wpa = w_pa[l].rearrange("(kc p) n -> p kc n", p=128)
        wpb = w_pb[l].rearrange("(kc p) n -> p kc n", p=128)
        for cg in range(4):
            cs = slice(cg * 512, (cg + 1) * 512)
            w, wb = wst[cg % 2], wbf[cg % 2]
            for k4 in range(2):
                LD(w[:, k4 * 4:(k4 + 1) * 4, :], wpa[:, k4 * 4:(k4 + 1) * 4, cs], [w])
                LD(w[:, 8 + k4 * 4:8 + (k4 + 1) * 4, :], wpb[:, k4 * 4:(k4 + 1) * 4, cs], [w])
            for k4 in range(4):
                G(lambda e: e.tensor_copy(out=wb[:, k4 * 4:(k4 + 1) * 4, :], in_=w[:, k4 * 4:(k4 + 1) * 4, :]), [w], [wb])
            for i in range(NT):
                prow = slice(1 + i * 128, 1 + (i + 1) * 128)
                gt_, mg_, t2_ = gts[i % 2], mgt[i % 2], t2[i % 2]
                LD(gt_[:, 0:512], P[prow, C_GA + cg * 512:C_GA + (cg + 1) * 512], [gt_], [bP])
                LD(gt_[:, 512:1024], P[prow, C_GB + cg * 512:C_GB + (cg + 1) * 512], [gt_], [bP])
                A(lambda e: e.activation(out=gt_[:, :], in_=gt_[:, :], func=AF.Sigmoid), [gt_], [gt_])
                pa = PS()
                for kc in range(8):
                    MM(pa[:, :], yT[:, kc, i * 128:(i + 1) * 128], wb[:, kc, :], [yT, wb], [pa], start=(kc == 0), stop=(kc == 7))
                pb = PS()
                for kc in range(8):
                    MM(pb[:, :], yT[:, 8 + kc, i * 128:(i + 1) * 128], wb[:, 8 + kc, :], [yT, wb], [pb], start=(kc == 0), stop=(kc == 7))
                V(lambda e: e.tensor_tensor(out=mg_[:, :], in0=pa[:, :], in1=gt_[:, 0:512], op=ALU.mult), [pa, gt_], [mg_])
                V(lambda e: e.tensor_tensor(out=t2_[:, :], in0=pb[:, :], in1=gt_[:, 512:1024], op=ALU.mult), [pb, gt_], [t2_])
                G(lambda e: e.tensor_tensor(out=mg_[:, :], in0=mg_[:, :], in1=t2_[:, :], op=ALU.add), [mg_, t2_], [mg_])
                ST(MG[i * 128:(i + 1) * 128, cs], mg_[:, :], [mg_], [bMG])
        ph6b.close()
        ph6.close()
        if stop == "P6b":
            S.barrier()
            return nc

        ph7 = Phase()
        wo = ph7.sb([128, 16, D], BF16, name="wo")
        wst = [ph7.sb([128, 16, 256], name="wst7") for _ in range(2)]
        wov = w_o[l].rearrange("(kc p) n -> p kc n", p=128)
        for cg in range(8):
            w = wst[cg % 2]
            LD(w[:, :, :], wov[:, :, cg * 256:(cg + 1) * 256], [w])
            for k4 in range(2):
                G(lambda e: e.tensor_copy(out=wo[:, k4 * 8:(k4 + 1) * 8, cg * 256:(cg + 1) * 256], in_=w[:, k4 * 8:(k4 + 1) * 8, :]), [w], [wo])
        gpg = ph7.sb([128, D], name="gpg")
        LD(gpg[:, :], GPG, [gpg], [bGPG])
        mg = [ph7.sb([128, D], name="mg7") for _ in range(2)]
        mT = [ph7.sb([128, 16, 128], BF16, name="mT") for _ in range(2)]
        xr = [ph7.sb([128, D], name="xr") for _ in range(2)]
        ot = [ph7.sb([128, D], name="ot") for _ in range(2)]
        junk = ph7.sb([128, 512], name="junk7")
        ss4 = [ph7.sb([128, 8], name="ss4") for _ in range(2)]
        for i in range(NT):
            m_, mT_, x_, o_, s_ = mg[i % 2], mT[i % 2], xr[i % 2], ot[i % 2], ss4[i % 2]
            LD(m_[:, :], MG[i * 128:(i + 1) * 128, :], [m_], [bMG])
            gather(x_, i)
            for c4 in range(4):
                p = PS()
                for q in range(4):
                    kc = c4 * 4 + q
                    TR(p[:, q * 128:(q + 1) * 128], m_[:, kc * 128:(kc + 1) * 128], [m_], [p])
                A(lambda e: e.activation(out=mT_[:, c4 * 4:(c4 + 1) * 4, :], in_=p[:, :].rearrange("p (q t) -> p q t", q=4),
                                         func=AF.Copy), [p], [mT_])
            pj = []
            for cg in range(4):
                p = PS()
                pj.append(p)
                for kc in range(16):
                    MM(p[:, :], mT_[:, kc, :], wo[:, kc, cg * 512:(cg + 1) * 512], [mT_, wo], [p], start=(kc == 0), stop=(kc == 15))
                A(lambda e: e.activation(out=junk[:, :], in_=p[:, :], func=AF.Square, accum_out=s_[:, cg:cg + 1]), [p], [junk, s_])
            V(lambda e: e.tensor_reduce(out=s_[:, 4:5], in_=s_[:, 0:4], axis=AX.X, op=ALU.add), [s_], [s_])
            rsqrt(s_[:, 4:5], s_[:, 4:5], 1.0, float(EPS * D), [s_])
            for cg in range(4):
                cs = slice(cg * 512, (cg + 1) * 512)
                V(lambda e: e.scalar_tensor_tensor(out=o_[:, cs], in0=pj[cg][:, :], scalar=s_[:, 4:5], in1=gpg[:, cs],
                                                   op0=ALU.mult, op1=ALU.mult), [pj[cg], s_, gpg], [o_])
            G(lambda e: e.tensor_tensor(out=o_[:, :], in0=o_[:, :], in1=x_[:, :], op=ALU.add), [o_, x_], [o_])
            S.dma("pool", lambda e: e.indirect_dma_start(
                out=x_dst, out_offset=bass.IndirectOffsetOnAxis(ap=idxt[:, l * NT + i:l * NT + i + 1], axis=0),
                in_=o_[:, :], in_offset=None), [o_, idxt], [bxd])
        ph7.close()
        if stop == "P7":
            S.barrier()
            return nc

    try:
        for l in range(DEPTH):
            r_ = layer(l)
            if r_ is not None:
                break
    except _Cut:
        pass
    S.barrier()
    return nc


_PROG = {}


def kernel(x_prompt, x_sample, state_rwkv, state_delta, c, c_ctx, w_mod, b_mod, g_pre, g_post,
           w_in, w0, w_up, a0, a_up, k_k, k_a, r_k, gn_w, gn_b, conv_w, a_log, dt_bias,
           o_norm_w, w_pa, w_pb, w_o):
    f32 = np.float32
    if "nc" not in _PROG:
        _PROG["nc"] = build_program()
    nc = _PROG["nc"]
    arr = lambda z: np.ascontiguousarray(np.asarray(z, dtype=f32))
    shared = {"w_mod": arr(w_mod), "b_mod": arr(b_mod), "g_pre": arr(g_pre), "g_post": arr(g_post), "w_in": arr(w_in),
              "w0": arr(w0), "w_up": arr(w_up), "a0": arr(a0), "a_up": arr(a_up), "k_k": arr(k_k), "k_a": arr(k_a),
              "r_k": arr(r_k), "gn_w": arr(gn_w), "gn_b": arr(gn_b), "conv_w": arr(conv_w),
              "a_log": arr(a_log).reshape(DEPTH, 16), "dt_bias": arr(dt_bias).reshape(DEPTH, 16),
              "o_norm_w": arr(o_norm_w), "w_pa": arr(w_pa), "w_pb": arr(w_pb), "w_o": arr(w_o)}
    x_prompt = arr(x_prompt); x_sample = arr(x_sample)
    state_rwkv = arr(state_rwkv); state_delta = arr(state_delta)
    c = arr(c); c_ctx = arr(c_ctx)
    tpos = np.arange(T)
    perm = (tpos % 32) * 64 + tpos // 32
    ii = np.arange(128)
    lmask = np.zeros((7, 3, 128, 128), f32)
    for lv in range(7):
        bsz = 1 << lv
        pb = ii[:, None] // bsz
        fb = ii[None, :] // bsz
        la = ((pb % 2 == 1) & (fb == pb - 1)).astype(f32)
        lmask[lv, 0] = la
        lmask[lv, 1] = la.T
        lmask[lv, 2] = la
    lmask = np.ascontiguousarray(lmask.transpose(2, 0, 1, 3)).reshape(128, 7, 384)
    in_maps = []
    for core in range(8):
        m = dict(shared)
        idx = np.zeros((DEPTH, T), np.int32)
        cmask = np.ones((128, 2 * NT), f32)
        if core < 4:
            m["x_in"] = x_sample[core]
            cv = c[core]
            m["flag"] = np.ones((128, 1), f32)
            m["init_r"] = np.ascontiguousarray(state_rwkv[core].transpose(0, 1, 4, 2, 3)).reshape(DEPTH, 2, 64, 1024)
            m["init_d"] = np.ascontiguousarray(state_delta[core].transpose(0, 1, 3, 2, 4)).reshape(DEPTH, 2, 128, 1024)
            for l in range(DEPTH):
                idx[l] = perm if l % 2 == 1 else tpos
            cmask[0, 0] = 0.0
            cmask[127, 2 * (NT - 1) + 1] = 0.0
        else:
            j = core - 4
            xs = np.zeros((T, D), f32)
            xs[:1024] = x_prompt[4 * j:4 * j + 4].reshape(1024, D)
            xs[1024:] = xs[:1024]
            m["x_in"] = xs
            cv = c_ctx
            m["flag"] = np.zeros((128, 1), f32)
            m["init_r"] = np.zeros((DEPTH, 2, 64, 1024), f32)
            m["init_d"] = np.zeros((DEPTH, 2, 128, 1024), f32)
            for l in range(DEPTH):
                idx[l] = tpos
            for i in range(NT):
                if i % 2 == 0:
                    cmask[0, 2 * i] = 0.0
                else:
                    cmask[127, 2 * i + 1] = 0.0
        m["cvec"] = np.ascontiguousarray(cv.reshape(16, 128).T)
        m["idx"] = np.ascontiguousarray(idx.reshape(DEPTH, NT, 128).transpose(2, 0, 1).reshape(128, DEPTH * NT))
        m["cmask"] = cmask
        m["lmask"] = lmask
        in_maps.append(m)
    res = run_bass_kernel_spmd(nc, in_maps, core_ids=list(range(8)))
    R = res.results
    y_sample = np.stack([R[b]["y_out"] for b in range(4)], 0).astype(f32)
    y_prompt = np.zeros((16, 256, D), f32)
    new_r = np.zeros((16, DEPTH, 2, 16, 64, 64), f32)
    new_d = np.zeros((16, DEPTH, 2, 8, 128, 128), f32)
    for j in range(4):
        r = R[4 + j]
        y_prompt[4 * j:4 * j + 4] = r["y_out"][:1024].reshape(4, 256, D)
        fr = r["fin_r"].reshape(DEPTH, 8, 2, 64, 16, 64)
        fd = r["fin_d"].reshape(DEPTH, 8, 2, 128, 8, 128)
        for s in range(4):
            new_r[4 * j + s] = fr[:, s].transpose(0, 1, 3, 4, 2)
            new_d[4 * j + s] = fd[:, s].transpose(0, 1, 3, 2, 4)
    return (y_prompt, y_sample, new_r, new_d)
```

```python
import numpy as np
import concourse.bass as bass
import concourse.mybir as mybir
from concourse.bass_utils import run_bass_kernel_spmd

F32 = mybir.dt.float32
BF16 = mybir.dt.bfloat16
I32 = mybir.dt.int32
ALU = mybir.AluOpType
AF = mybir.ActivationFunctionType
AX = mybir.AxisListType

D = 2048
T = 2048
NT = 16
DEPTH = 4
NIN = 12704
AW = 1024
DECAY_SCALE = 0.606531
GN_EPS = 64e-5
EPS = 1e-6
C_R, C_K, C_V, C_ZA = 0, 1024, 2048, 3072
C_LO = 4096
C_QKV = 4480
C_ZB = 7552
C_BETA, C_ALPHA = 8576, 8592
C_GA, C_GB = 8608, 10656


class Buf:
    __slots__ = ("name", "w", "r", "excl")

    def __init__(self, name, excl=False):
        self.name = name
        self.w = None
        self.r = []
        self.excl = excl


class Tl:
    __slots__ = ("t", "b")

    def __init__(self, t, b):
        self.t = t
        self.b = b

    def __getitem__(self, k):
        return self.t[k]


class Sched:
    LIMIT = 30000

    def __init__(self, nc, n_dma_sems=10):
        self.nc = nc
        self.eng = {"pe": nc.tensor, "act": nc.scalar, "dve": nc.vector, "pool": nc.gpsimd, "sp": nc.sync}
        self.nsem = 0
        self.sem = {k: self._newsem(k) for k in self.eng}
        self.cnt = {k: 0 for k in self.eng}
        self.seen = {k: {} for k in self.eng}
        self.dsems = {}
        self.n_dma_sems = n_dma_sems
        self.all_dma = []

    def _newsem(self, k):
        self.nsem += 1
        return self.nc.alloc_semaphore("s_%s_%d" % (k, self.nsem))

    def _wait(self, e, tok):
        sem, val = tok
        sid = id(sem)
        if self.seen[e].get(sid, 0) >= val:
            return
        self.seen[e][sid] = val
        self.eng[e].wait_ge(sem, val)

    def _deps(self, e, reads, writes, pe_skip):
        for b in reads:
            if b.w is not None and not (pe_skip and b.w[2]):
                self._wait(e, b.w[:2])
            if b.excl:
                for t in b.r:
                    self._wait(e, t[:2])
        for b in writes:
            if b.w is not None and not (pe_skip and b.w[2]):
                self._wait(e, b.w[:2])
            for t in b.r:
                if not (pe_skip and t[2]):
                    self._wait(e, t[:2])

    def op(self, e, fn, reads=(), writes=()):
        reads = [x.b if isinstance(x, Tl) else x for x in reads]
        writes = [x.b if isinstance(x, Tl) else x for x in writes]
        self._deps(e, reads, writes, e == "pe")
        if self.cnt[e] >= self.LIMIT:
            self.sem[e] = self._newsem(e)
            self.cnt[e] = 0
        ins = fn(self.eng[e])
        self.cnt[e] += 1
        tok = (self.sem[e], self.cnt[e], e == "pe")
        ins.then_inc(self.sem[e], 1)
        for b in reads:
            b.r.append(tok)
        for b in writes:
            b.w = tok
            b.r = []
        return tok

    def dma(self, e, fn, reads=(), writes=()):
        reads = [x.b if isinstance(x, Tl) else x for x in reads]
        writes = [x.b if isinstance(x, Tl) else x for x in writes]
        if e not in self.dsems:
            self.dsems[e] = [[self.nc.alloc_semaphore("d_%s_%d" % (e, i)), 0] for i in range(self.n_dma_sems)]
            self.dsems[e + "_i"] = 0
            self.all_dma.extend(self.dsems[e])
        i = self.dsems[e + "_i"]
        self.dsems[e + "_i"] = (i + 1) % self.n_dma_sems
        slot = self.dsems[e][i]
        self._deps(e, reads, writes, False)
        if slot[1] > 0:
            self._wait(e, (slot[0], slot[1]))
        ins = fn(self.eng[e])
        slot[1] += 16
        tok = (slot[0], slot[1], False)
        ins.then_inc(slot[0], 16)
        for b in reads:
            b.r.append(tok)
        for b in writes:
            b.w = tok
            b.r = []
        return tok

    def barrier(self):
        for e in self.eng:
            for e2 in self.eng:
                if self.cnt[e2] > 0:
                    self._wait(e, (self.sem[e2], self.cnt[e2]))
            for s in self.all_dma:
                if s[1] > 0:
                    self._wait(e, (s[0], s[1]))


class Ctx:
    pass


class _Cut(Exception):
    pass


def build_program(stop=None):
    nc = bass.Bass("TRN2", target_bir_lowering=False)
    S = Sched(nc)
    g = Ctx()

    def cut(tag):
        if stop == tag:
            raise _Cut()

    def din(name, shape, dt=F32):
        return nc.dram_tensor(name, list(shape), dt, kind="ExternalInput").ap()

    def dout(name, shape):
        return nc.dram_tensor(name, list(shape), F32, kind="ExternalOutput").ap()

    import os as _os2
    _dbg_out = bool(_os2.environ.get("KDBG_OUT"))

    def dscr(name, shape):
        return nc.dram_tensor(name, list(shape), F32, kind="ExternalOutput" if _dbg_out else "Internal").ap()

    x_in = din("x_in", [T, D])
    cvec = din("cvec", [128, 16])
    flag_in = din("flag", [128, 1])
    idx_in = din("idx", [128, DEPTH * NT], I32)
    cmask_in = din("cmask", [128, 2 * NT])
    lmask_in = din("lmask", [128, 7, 384])
    init_r = din("init_r", [DEPTH, 2, 64, 1024])
    init_d = din("init_d", [DEPTH, 2, 128, 1024])
    w_mod = din("w_mod", [DEPTH, D, 3 * D])
    b_mod = din("b_mod", [DEPTH, 3 * D])
    g_pre = din("g_pre", [DEPTH, D])
    g_post = din("g_post", [DEPTH, D])
    w_in = din("w_in", [DEPTH, D, NIN])
    w0 = din("w0", [DEPTH, 2, AW])
    w_up = din("w_up", [DEPTH, 2, 96, AW])
    a0 = din("a0", [DEPTH, 2, AW])
    a_up = din("a_up", [DEPTH, 2, 96, AW])
    k_k = din("k_k", [DEPTH, AW])
    k_a = din("k_a", [DEPTH, AW])
    r_k = din("r_k", [DEPTH, AW])
    gn_w = din("gn_w", [DEPTH, AW])
    gn_b = din("gn_b", [DEPTH, AW])
    conv_w = din("conv_w", [DEPTH, 3, 3072])
    a_log = din("a_log", [DEPTH, 16])
    dt_bias = din("dt_bias", [DEPTH, 16])
    o_norm_w = din("o_norm_w", [DEPTH, 128])
    w_pa = din("w_pa", [DEPTH, AW, D])
    w_pb = din("w_pb", [DEPTH, AW, D])
    w_o = din("w_o", [DEPTH, D, D])

    y_out = dout("y_out", [T, D])
    fin_r = dout("fin_r", [DEPTH, 8, 2, 64, 1024])
    fin_d = dout("fin_d", [DEPTH, 8, 2, 128, 1024])

    XS = [dscr("xs0", [T, D]), dscr("xs1", [T, D])]
    P = dscr("proj", [T + 2, NIN])
    GPG = dscr("gpg", [128, D])
    SR = {"GT": dscr("sr_gt", [NT, 2, 64, 1024]), "H": dscr("sr_h", [NT, 2, 64, 1024]),
          "RT": dscr("sr_rt", [NT, 2, 64, 2048]), "Y0": dscr("sr_y0", [NT, 2, 128, 1024])}
    SD = {"GT": dscr("sd_gt", [NT, 2, 128, 1024]), "H": dscr("sd_h", [NT, 2, 128, 1024]),
          "RT": dscr("sd_rt", [NT, 2, 128, 1024]), "Y0": dscr("sd_y0", [NT, 2, 128, 1024])}
    YA = dscr("ya", [2, T, 1024])
    YB = dscr("yb", [2, T, 1024])
    BON = dscr("bon", [T, 32])
    MG = dscr("mg", [T, D])

    names = [0]

    def sb(shape, dt=F32, name=None):
        names[0] += 1
        nm = "%s_%d" % (name or "t", names[0])
        return Tl(nc.alloc_sbuf_tensor(nm, list(shape), dt), Buf(nm))

    ident = sb([128, 128], name="ident")
    ones = sb([128, 128], name="ones")
    mU, mUi, mL, mLi = (sb([128, 128], name="m") for _ in range(4))
    cm = [sb([128, 128], name="cm") for _ in range(2)]
    A1 = [sb([128, 128], name="a1") for _ in range(2)]
    A2 = [sb([128, 128], name="a2") for _ in range(2)]
    A6 = [sb([128, 128], name="a6") for _ in range(2)]
    mask4 = [sb([128, 512], name="mask4") for _ in range(2)]
    flagc = sb([128, 1], name="flag")
    idxt = sb([128, DEPTH * NT], I32, name="idx")
    cmask = sb([128, 2 * NT], name="cmask")
    zrow = sb([1, 512], name="zrow")
    LM = sb([128, 7, 384], name="LM")
    II = sb([128, 256], name="II")
    identb = sb([128, 128], BF16, name="identb")

    psb = [Tl(nc.alloc_psum_tensor("psb%d" % i, [128, 512], F32), Buf("psb%d" % i, excl=True)) for i in range(8)]
    pctr = [0]

    def PS():
        pctr[0] += 1
        return psb[pctr[0] % 8]

    def sel(t, pat, op, base, cmul):
        S.op("pool", lambda e: e.memset(t[:, :], 1.0), writes=[t])
        S.op("pool", lambda e: e.affine_select(out=t[:, :], in_=t[:, :], pattern=[[pat, 128]], compare_op=op,
                                               fill=0.0, base=base, channel_multiplier=cmul), reads=[t], writes=[t])

    sel(ident, -1, ALU.is_equal, 0, 1)
    S.op("pool", lambda e: e.tensor_copy(out=II[:, 0:128], in_=ident[:, :]), reads=[ident], writes=[II])
    S.op("pool", lambda e: e.tensor_copy(out=II[:, 128:256], in_=ident[:, :]), reads=[ident], writes=[II])
    S.op("pool", lambda e: e.tensor_copy(out=identb[:, :], in_=ident[:, :]), reads=[ident], writes=[identb])
    S.op("pool", lambda e: e.memset(ones[:, :], 1.0), writes=[ones])
    sel(mU, 1, ALU.is_gt, 0, -1)
    sel(mUi, 1, ALU.is_ge, 0, -1)
    sel(mL, -1, ALU.is_gt, 0, 1)
    sel(mLi, -1, ALU.is_ge, 0, 1)
    sel(cm[0], 0, ALU.is_ge, 64, -1)
    sel(cm[1], 0, ALU.is_ge, -64, 1)
    CI = [mUi, mLi]
    CS = [mU, mL]
    MN = [mL, mU]
    for d in range(2):
        S.op("dve", lambda e: e.tensor_tensor(out=A1[d][:, :], in0=CI[d][:, :], in1=cm[d][:, :], op=ALU.subtract),
             reads=[CI[d], cm[d]], writes=[A1[d]])
        S.op("dve", lambda e: e.tensor_tensor(out=A2[d][:, :], in0=CS[d][:, :], in1=cm[d][:, :], op=ALU.subtract),
             reads=[CS[d], cm[d]], writes=[A2[d]])
        S.op("dve", lambda e: e.tensor_tensor(out=A6[d][:, :], in0=ones[:, :], in1=CI[d][:, :], op=ALU.subtract),
             reads=[ones, CI[d]], writes=[A6[d]])
        for q, m in enumerate((CS[d], CI[d], CS[d], CI[d])):
            S.op("pool", lambda e: e.tensor_copy(out=mask4[d][:, q * 128:(q + 1) * 128], in_=m[:, :]),
                 reads=[m], writes=[mask4[d]])
    NEGT_S = [sb([128, 128], name="negts") for _ in range(2)]
    NEGT_I = [sb([128, 128], name="negti") for _ in range(2)]
    NEG_S = [sb([128, 128], name="negs") for _ in range(2)]
    for d in range(2):
        for dst, src in ((NEGT_S[d], CS[d]), (NEGT_I[d], CI[d]), (NEG_S[d], MN[d])):
            S.op("dve", lambda e: e.tensor_scalar(out=dst[:, :], in0=src[:, :], scalar1=-1.0, scalar2=1.0e5, op0=ALU.add,
                                                  op1=ALU.mult), reads=[src], writes=[dst])
    S.dma("sp", lambda e: e.dma_start(out=flagc[:, :], in_=flag_in), writes=[flagc])
    S.dma("sp", lambda e: e.dma_start(out=idxt[:, :], in_=idx_in), writes=[idxt])
    S.dma("sp", lambda e: e.dma_start(out=cmask[:, :], in_=cmask_in), writes=[cmask])
    S.dma("sp", lambda e: e.dma_start(out=LM[:, :, :], in_=lmask_in), writes=[LM])
    S.op("pool", lambda e: e.memset(zrow[:, :], 0.0), writes=[zrow])
    bP = Buf("P")
    for r0 in (0, T + 1):
        for c0 in range(0, NIN, 512):
            cw0 = min(512, NIN - c0)
            S.dma("sp", lambda e: e.dma_start(out=P[r0:r0 + 1, c0:c0 + cw0], in_=zrow[:, 0:cw0]), reads=[zrow], writes=[bP])

    def V(fn, r, w):
        S.op("dve", fn, r, w)

    def A(fn, r, w):
        S.op("act", fn, r, w)

    def G(fn, r, w):
        S.op("pool", fn, r, w)

    def MM(out, lhsT, rhs, r, w, start=True, stop=True):
        S.op("pe", lambda e: e.matmul(out, lhsT=lhsT, rhs=rhs, start=start, stop=stop), r, w)

    def TR(out, in_, r, w):
        S.op("pe", lambda e: e.transpose(out, in_, ident[:, :]), list(r) + [ident], w)

    def LD(out, in_, w, r=()):
        S.dma("sp", lambda e: e.dma_start(out=out, in_=in_), r, w)

    def ST(out, in_, r, w=()):
        S.dma("sp", lambda e: e.dma_start(out=out, in_=in_), r, w)

    def bc3(ap2, n):
        return ap2.unsqueeze(2).broadcast_to([ap2.shape[0], ap2.shape[1], n])

    def rsqrt(out_t, in_ap, scale, bias, r):
        A(lambda e: e.activation(out=out_t, in_=in_ap, func=AF.Ln, bias=bias, scale=scale), r, r)
        A(lambda e: e.activation(out=out_t, in_=out_t, func=AF.Exp, scale=-0.5), r, r)

    class Phase:
        def __init__(self):
            self.guards = []

        def sb(self, shape, dt=F32, name="p"):
            names[0] += 1
            nm = "%s_%d" % (name, names[0])
            gd = nc.sbuf_tensor(nm, list(shape), dt)
            t = gd.__enter__()
            self.guards.append(gd)
            return Tl(t, Buf(nm))

        def close(self):
            S.barrier()
            for gd in reversed(self.guards):
                gd.__exit__(None, None, None)

    def drive(gens, width=4):
        act = []
        gens = list(gens)
        while gens or act:
            while gens and len(act) < width:
                act.append(gens.pop(0))
            nxt = []
            for gen in act:
                try:
                    next(gen)
                    nxt.append(gen)
                except StopIteration:
                    pass
            act = nxt

    import os as _os3
    NU = int(_os3.environ.get("KDBG_NU", 4))

    def unit_bufs(ph, dec=False):
        return [(ph.sb([128, 512], BF16, name="chT"), ph.sb([128, 512], name="AT"),
                 ph.sb([128, 256], name="NQ"),
                 [ph.sb([128, 256], name="B") for _ in range(2)], ph.sb([128, 128], name="dg"),
                 ph.sb([128, 384], name="DM") if dec else None, ph.sb([128, 384], name="DR") if dec else None,
                 [ph.sb([128, 256], BF16, name="TW") for _ in range(2)], ph.sb([128, 256], BF16, name="YX"),
                 (ph.sb([128, 256], BF16, name="NQb"), ph.sb([128, 256], name="TWf")))
                for _ in range(NU)]

    def unit(bset, d, K, Vd, hcols, vcols, X, outs, h, dec=None):
        chT, AT, NQ0, B, dg, DM, DR, TW, YX, (NQb, TWf) = bset
        NQ = [NQ0]

        def run():
            if dec is not None:
                V(lambda e: e.tensor_scalar(out=DR[:, 0:128], in0=CS[d][:, :], scalar1=dec["g"], scalar2=None, op0=ALU.mult),
                  [CS[d], dec["b"]], [DR])
                V(lambda e: e.tensor_scalar(out=DR[:, 128:256], in0=CI[d][:, :], scalar1=dec["g"], scalar2=None, op0=ALU.mult),
                  [CI[d], dec["b"]], [DR])
                V(lambda e: e.tensor_scalar(out=DR[:, 256:384], in0=DR[:, 128:256], scalar1=-1.0, scalar2=None, op0=ALU.mult),
                  [DR], [DR])
                p = PS()
                for q, neg in enumerate((NEGT_S[d], NEGT_I[d], NEG_S[d])):
                    MM(p[:, q * 128:(q + 1) * 128], ones[:, :], DR[:, q * 128:(q + 1) * 128], [ones, DR], [p], start=True, stop=False)
                    MM(p[:, q * 128:(q + 1) * 128], ident[:, :], neg[:, :], [ident, neg], [p], start=False, stop=True)
                A(lambda e: e.activation(out=DM[:, 0:256], in_=p[:, 0:256], func=AF.Exp, bias=dec["ncw"], scale=1.0),
                  [p, dec["b"]], [DM])
                A(lambda e: e.activation(out=DM[:, 256:384], in_=p[:, 256:384], func=AF.Exp, bias=dec["cwx"], scale=1.0),
                  [p, dec["b"]], [DM])
                yield
            p = PS()
            for q, key in enumerate(("nm", "rm", "pm", "km")):
                TR(p[0:K, q * 128:(q + 1) * 128], X[key][:, hcols], [X[key]], [p])
            A(lambda e: e.activation(out=chT[0:K, :], in_=p[0:K, :], func=AF.Copy), [p], [chT])
            if h == 0:
                cut("u.1")
            yield
            p = PS()
            MM(p[:, 0:256], chT[0:K, 256:384], chT[0:K, 0:256], [chT], [p])
            MM(p[:, 256:512], chT[0:K, 384:512], chT[0:K, 0:256], [chT], [p])
            if dec is None:
                V(lambda e: e.tensor_tensor(out=AT[:, :], in0=p[:, :], in1=mask4[d][:, :], op=ALU.mult), [p, mask4[d]], [AT])
            else:
                V(lambda e: e.tensor_tensor(out=AT[:, 0:256], in0=p[:, 0:256], in1=DM[:, 0:256], op=ALU.mult), [p, DM], [AT])
                V(lambda e: e.tensor_tensor(out=AT[:, 256:512], in0=p[:, 256:512], in1=DM[:, 0:256], op=ALU.mult), [p, DM], [AT])
            p2 = PS()
            MM(p2[:, 0:128], chT[0:K, 0:128], chT[0:K, 256:384], [chT], [p2])
            if dec is None:
                V(lambda e: e.tensor_tensor(out=NQ[0][:, 0:128], in0=p2[:, 0:128], in1=MN[d][:, :], op=ALU.mult),
                  [p2, MN[d]], [NQ[0]])
            else:
                V(lambda e: e.tensor_tensor(out=NQ[0][:, 0:128], in0=p2[:, 0:128], in1=DM[:, 256:384], op=ALU.mult),
                  [p2, DM], [NQ[0]])
            G(lambda e: e.tensor_copy(out=NQ[0][:, 128:256], in_=AT[:, 0:128]), [AT], [NQ[0]])
            G(lambda e: e.tensor_copy(out=B[0][:, 0:K], in_=X["ntr"][:, hcols]), [X["ntr"]], [B[0]])
            if h == 0:
                cut("u.2")
            yield
            p = PS()
            MM(p[:, 0:Vd], AT[:, 256:384], X["v"][:, vcols], [AT, X["v"]], [p])
            A(lambda e: e.activation(out=B[0][:, K:K + Vd], in_=p[:, 0:Vd], func=AF.Copy), [p], [B[0]])
            if h == 0:
                cut("u.3")
            yield
            moff = 0 if d == 0 else 128
            G(lambda e: e.tensor_copy(out=NQb[:, :], in_=NQ0[:, :]), [NQ0], [NQb])
            G(lambda e: e.tensor_tensor(out=YX[:, :], in0=NQ0[:, :], in1=LM[:, 0, moff:moff + 256], op=ALU.mult), [NQ0, LM], [YX])
            G(lambda e: e.tensor_tensor(out=TW[1][:, :], in0=YX[:, :], in1=II[:, :], op=ALU.add), [YX, II], [TW[1]])
            cur = 1
            yield
            for lv in range(1, 7):
                twc = TW[cur]
                twn = TW[1 - cur] if lv < 6 else TWf
                p = PS()
                MM(p[:, 0:128], NQb[:, 128:256], twc[:, 0:128], [NQb, twc], [p])
                MM(p[:, 128:256], NQb[:, 0:128], twc[:, 128:256], [NQb, twc], [p])
                V(lambda e: e.tensor_tensor(out=YX[:, :], in0=p[:, 0:256], in1=LM[:, lv, moff:moff + 256], op=ALU.mult),
                  [p, LM], [YX])
                p2 = PS()
                MM(p2[:, 0:128], twc[:, 128:256], YX[:, 0:128], [twc, YX], [p2])
                MM(p2[:, 128:256], twc[:, 0:128], YX[:, 128:256], [twc, YX], [p2])
                V(lambda e: e.tensor_tensor(out=twn[:, :], in0=p2[:, 0:256], in1=twc[:, :], op=ALU.add), [p2, twc], [twn])
                cur = 1 - cur
                yield
            p = PS()
            MM(p[:, 0:K + Vd], TWf[:, 128:256], B[0][:, 0:K + Vd], [TWf, B[0]], [p])
            A(lambda e: e.activation(out=B[1][:, 0:K + Vd], in_=p[:, 0:K + Vd], func=AF.Copy), [p], [B[1]])
            cur = 1
            Bf = B[cur]
            if h == 0:
                cut("u.4")
            G(lambda e: e.tensor_tensor(out=dg[0:K, 0:K], in0=ident[0:K, 0:K], in1=X["gam"][0:K, hcols], op=ALU.mult),
              [ident, X["gam"]], [dg])
            if h == 0:
                cut("u.41")
            p = PS()
            MM(p[0:K, 0:K], Bf[:, 0:K], X["ph"][:, hcols], [Bf, X["ph"]], [p])
            if h == 0:
                cut("u.415")
            MM(p[0:K, K:K + Vd], X["ph"][:, hcols], Bf[:, K:K + Vd], [Bf, X["ph"]], [p], start=True, stop=False)
            MM(p[0:K, K:K + Vd], X["kh"][:, hcols], X["v"][:, vcols], [X["kh"], X["v"]], [p], start=False, stop=True)
            if h == 0:
                cut("u.42")
            V(lambda e: e.tensor_tensor(out=outs["GT"][0:K, h * K:(h + 1) * K], in0=p[0:K, 0:K], in1=dg[0:K, 0:K], op=ALU.add),
              [p, dg], [outs["GT"]])
            A(lambda e: e.activation(out=outs["H"][0:K, h * Vd:(h + 1) * Vd], in_=p[0:K, K:K + Vd], func=AF.Copy),
              [p], [outs["H"]])
            if h == 0:
                cut("u.43")
            p = PS()
            MM(p[0:K, 0:128], Bf[:, 0:K], AT[:, 128:256], [Bf, AT], [p], start=True, stop=False)
            MM(p[0:K, 0:128], X["rtr"][:, hcols], ident[:, :], [X["rtr"], ident], [p], start=False, stop=True)
            MM(p[:, 128:128 + Vd], AT[:, 128:256], Bf[:, K:K + Vd], [Bf, AT], [p], start=True, stop=False)
            MM(p[:, 128:128 + Vd], AT[:, 384:512], X["v"][:, vcols], [AT, X["v"]], [p], start=False, stop=True)
            if h == 0:
                cut("u.44")
            A(lambda e: e.activation(out=outs["RT"][0:K, h * 128:(h + 1) * 128], in_=p[0:K, 0:128], func=AF.Copy),
              [p], [outs["RT"]])
            V(lambda e: e.tensor_copy(out=outs["Y0"][:, h * Vd:(h + 1) * Vd], in_=p[:, 128:128 + Vd]), [p], [outs["Y0"]])
            if h == 0:
                cut("u.5")
            yield
        return run()

    def summaries(ph, d, K, Vd, H, X, SUM, tile, units_bufs):
        outs = units_bufs["outs"]
        decs = units_bufs.get("decs")
        gens = [unit(units_bufs["sets"][h % NU], d, K, Vd, slice(h * K, (h + 1) * K), slice(h * Vd, (h + 1) * Vd), X, outs, h,
                     None if decs is None else decs[h]) for h in range(H)]
        drive(gens, NU)
        bsum = units_bufs["bsum"]
        ST(SUM["GT"][tile, d], outs["GT"][0:K, :], [outs["GT"]], [bsum])
        ST(SUM["H"][tile, d], outs["H"][0:K, :], [outs["H"]], [bsum])
        ST(SUM["RT"][tile, d], outs["RT"][0:K, 0:H * 128], [outs["RT"]], [bsum])
        ST(SUM["Y0"][tile, d], outs["Y0"][:, :], [outs["Y0"]], [bsum])

    if stop == "init":
        S.barrier()
        return nc
    bX = [Buf("xs0"), Buf("xs1"), Buf("xin"), Buf("yout")]
    bGPG = Buf("gpg")
    bSR = Buf("sr")
    bSD = Buf("sd")
    bYA = Buf("ya")
    bYB = Buf("yb")
    bBON = Buf("bon")
    bMG = Buf("mg")
    bFIN = Buf("fin")

    def layer(l):
        x_src, bxs = (x_in, bX[2]) if l == 0 else (XS[(l - 1) % 2], bX[(l - 1) % 2])
        x_dst, bxd = (y_out, bX[3]) if l == DEPTH - 1 else (XS[l % 2], bX[l % 2])

        def gather(dst_tile, i, src=x_src, bsrc=bxs):
            S.dma("pool", lambda e: e.indirect_dma_start(
                out=dst_tile[:, :], out_offset=None, in_=src,
                in_offset=bass.IndirectOffsetOnAxis(ap=idxt[:, l * NT + i:l * NT + i + 1], axis=0)),
                [idxt, bsrc], [dst_tile])

        ph = Phase()
        hT = ph.sb([128, 16, T], BF16, name="hT")
        ph1 = Phase()
        cT = ph1.sb([128, 16], name="cT")
        scB = ph1.sb([128, 16, 128], name="scB")
        modt = ph1.sb([128, 3 * D], name="mod")
        bmod = [ph1.sb([128, 256], name="bmod") for _ in range(2)]
        gpre = ph1.sb([128, D], name="gpre")
        wst = [ph1.sb([128, 16, 256], name="wst") for _ in range(2)]
        LD(cT[:, :], cvec, [cT])
        A(lambda e: e.activation(out=cT[:, :], in_=cT[:, :], func=AF.Silu), [cT], [cT])
        V(lambda e: e.tensor_copy(out=scB[:, :, :], in_=bc3(cT[:, :], 128)), [cT], [scB])
        LD(gpre[:, :], g_pre[l].partition_broadcast(128), [gpre])
        wm = w_mod[l].rearrange("(kc p) n -> p kc n", p=128)
        for cg in range(24):
            w = wst[cg % 2]
            bm_ = bmod[cg % 2]
            LD(w[:, :, :], wm[:, :, cg * 256:(cg + 1) * 256], [w])
            LD(bm_[:, :], b_mod[l, cg * 256:(cg + 1) * 256].partition_broadcast(128), [bm_])
            p = PS()
            for kc in range(16):
                MM(p[:, 0:256], scB[:, kc, :], w[:, kc, :], [scB, w], [p], start=(kc == 0), stop=(kc == 15))
            V(lambda e: e.tensor_tensor(out=modt[:, cg * 256:(cg + 1) * 256], in0=p[:, 0:256],
                                        in1=bm_[:, :], op=ALU.add), [p, bm_], [modt])
        V(lambda e: e.scalar_tensor_tensor(out=modt[:, D:2 * D], in0=modt[:, D:2 * D], scalar=1.0, in1=gpre[:, :],
                                           op0=ALU.add, op1=ALU.mult), [modt, gpre], [modt])
        V(lambda e: e.tensor_scalar(out=modt[:, D:2 * D], in0=modt[:, D:2 * D], scalar1=float(D ** 0.5), scalar2=None,
                                    op0=ALU.mult), [modt], [modt])
        LD(gpre[:, :], g_post[l].partition_broadcast(128), [gpre])
        V(lambda e: e.scalar_tensor_tensor(out=modt[:, 2 * D:3 * D], in0=modt[:, 2 * D:3 * D], scalar=float(D ** 0.5),
                                           in1=gpre[:, :], op0=ALU.mult, op1=ALU.mult), [modt, gpre], [modt])
        ST(GPG, modt[:, 2 * D:3 * D], [modt], [bGPG])
        xt = [ph1.sb([128, D], name="xt") for _ in range(2)]
        hh = [ph1.sb([128, D], name="hh") for _ in range(2)]
        ssq = [ph1.sb([128, 1], name="ssq") for _ in range(2)]
        for i in range(NT):
            x_t, h_t, s_t = xt[i % 2], hh[i % 2], ssq[i % 2]
            gather(x_t, i)
            A(lambda e: e.activation(out=h_t[:, :], in_=x_t[:, :], func=AF.Square, accum_out=s_t[:, :]), [x_t], [h_t, s_t])
            rsqrt(s_t[:, :], s_t[:, :], 1.0, float(EPS * D), [s_t])
            V(lambda e: e.scalar_tensor_tensor(out=h_t[:, :], in0=x_t[:, :], scalar=s_t[:, 0:1], in1=modt[:, D:2 * D],
                                               op0=ALU.mult, op1=ALU.mult), [x_t, s_t, modt], [h_t])
            G(lambda e: e.tensor_tensor(out=h_t[:, :], in0=h_t[:, :], in1=modt[:, 0:D], op=ALU.add), [h_t, modt], [h_t])
            for c4 in range(4):
                p = PS()
                for q in range(4):
                    kc = c4 * 4 + q
                    TR(p[:, q * 128:(q + 1) * 128], h_t[:, kc * 128:(kc + 1) * 128], [h_t], [p])
                A(lambda e: e.activation(out=hT[:, c4 * 4:(c4 + 1) * 4, i * 128:(i + 1) * 128],
                                         in_=p[:, :].rearrange("p (q t) -> p q t", q=4), func=AF.Copy), [p], [hT])
        ph1.close()
        if stop == "P1":
            S.barrier()
            return nc

        ph2 = Phase()
        wst = [ph2.sb([128, 16, 512], name="wst") for _ in range(2)]
        wbf = [ph2.sb([128, 16, 512], BF16, name="wbf") for _ in range(2)]
        ost = [ph2.sb([128, 512], name="ost") for _ in range(4)]
        wi = w_in[l].rearrange("(kc p) n -> p kc n", p=128)
        oc = 0
        ncg = (NIN + 511) // 512

        def fetch(cg):
            c0 = min(cg * 512, NIN - 512)
            w, wb = wst[cg % 2], wbf[cg % 2]
            for k4 in range(4):
                LD(w[:, k4 * 4:(k4 + 1) * 4, :], wi[:, k4 * 4:(k4 + 1) * 4, c0:c0 + 512], [w])
            for k4 in range(4):
                G(lambda e: e.tensor_copy(out=wb[:, k4 * 4:(k4 + 1) * 4, :], in_=w[:, k4 * 4:(k4 + 1) * 4, :]), [w], [wb])
        fetch(0)
        for cg in range(ncg):
            if cg + 1 < ncg:
                fetch(cg + 1)
            c0 = min(cg * 512, NIN - 512)
            cw = 512
            wb = wbf[cg % 2]
            for i in range(NT):
                p = PS()
                for kc in range(16):
                    MM(p[:, 0:cw], hT[:, kc, i * 128:(i + 1) * 128], wb[:, kc, 0:cw], [hT, wb], [p],
                       start=(kc == 0), stop=(kc == 15))
                o = ost[oc % 4]
                oc += 1
                A(lambda e: e.activation(out=o[:, 0:cw], in_=p[:, 0:cw], func=AF.Copy), [p], [o])
                ST(P[1 + i * 128:1 + (i + 1) * 128, c0:c0 + cw], o[:, 0:cw], [o], [bP])
        ph2.close()
        ph.close()
        if stop == "P2":
            S.barrier()
            return nc

        ph3 = Phase()
        cst = {}
        for nm_, src in (("k_k", k_k[l]), ("k_a", k_a[l]), ("r_k", r_k[l])):
            cst[nm_] = ph3.sb([128, AW], name=nm_)
            LD(cst[nm_][:, :], src.partition_broadcast(128), [cst[nm_]])
        wup = [ph3.sb([128, AW], name="wup") for _ in range(2)]
        aup = [ph3.sb([128, AW], name="aup") for _ in range(2)]
        for d in range(2):
            LD(wup[d][0:96, :], w_up[l, d], [wup[d]])
            LD(wup[d][96:97, :], w0[l, d:d + 1, :], [wup[d]])
            LD(aup[d][0:96, :], a_up[l, d], [aup[d]])
            LD(aup[d][96:97, :], a0[l, d:d + 1, :], [aup[d]])
        usets = unit_bufs(ph3)
        rkv = ph3.sb([128, 3072], name="rkv")
        lo = ph3.sb([128, 384], name="lo")
        loT = ph3.sb([128, 4, 128], name="loT")
        G(lambda e: e.memset(loT[:, :, :], 1.0), [], [loT])
        kx = ph3.sb([128, AW], name="kx")
        kk = ph3.sb([128, AW], name="kk")
        sm = ph3.sb([128, 64], name="sm")
        bon = ph3.sb([128, 32], name="bon")
        sg = [ph3.sb([128, AW], name="sg") for _ in range(2)]
        ad = [ph3.sb([128, AW], name="ad") for _ in range(2)]
        kd = [ph3.sb([128, AW], name="kd") for _ in range(2)]
        pd = [ph3.sb([128, AW], name="pd") for _ in range(2)]
        tmp = kx
        ex = ad[0]
        X = {k_: ph3.sb([128, AW], name=k_) for k_ in ("nm", "rm", "pm", "km", "ntr", "rtr", "ph", "kh")}
        X["gam"] = ad[1]
        outs = {"GT": ph3.sb([128, 1024], name="oGT"), "H": ph3.sb([128, 1024], name="oH"),
                "RT": ph3.sb([128, 2048], name="oRT"), "Y0": ph3.sb([128, 1024], name="oY0")}
        cut("p3.0")
        for i in range(NT):
            rows = slice(1 + i * 128, 1 + (i + 1) * 128)
            LD(rkv[:, :], P[rows, 0:3072], [rkv], [bP])
            LD(lo[:, :], P[rows, C_LO:C_LO + 384], [lo], [bP])
            cut("p3.1")
            A(lambda e: e.activation(out=lo[:, 0:192], in_=lo[:, 0:192], func=AF.Tanh), [lo], [lo])
            p = PS()
            for q in range(4):
                TR(p[0:96, q * 128:(q + 1) * 128], lo[:, q * 96:(q + 1) * 96], [lo], [p])
            A(lambda e: e.activation(out=loT[0:96, :, :], in_=p[0:96, :].rearrange("p (q t) -> p q t", q=4), func=AF.Copy),
              [p], [loT])
            cut("p3.2")
            for d in range(2):
                for hf in range(2):
                    cs = slice(hf * 512, (hf + 1) * 512)
                    p = PS()
                    MM(p[:, :], loT[0:97, d, :], wup[d][0:97, cs], [loT, wup[d]], [p])
                    A(lambda e: e.activation(out=sg[d][:, cs], in_=p[:, :], func=AF.Sigmoid), [p], [sg[d]])
                    p = PS()
                    MM(p[:, :], loT[0:97, 2 + d, :], aup[d][0:97, cs], [loT, aup[d]], [p])
                    A(lambda e: e.activation(out=ad[d][:, cs], in_=p[:, :], func=AF.Sigmoid), [p], [ad[d]])
            cut("p3.3")
            r_ap, k_ap, v_ap = rkv[:, 0:1024], rkv[:, 1024:2048], rkv[:, 2048:3072]
            V(lambda e: e.tensor_tensor(out=kx[:, :], in0=k_ap, in1=cst["k_k"][:, :], op=ALU.mult), [rkv, cst["k_k"]], [kx])
            A(lambda e: e.activation(out=kk[:, :], in_=kx[:, :], func=AF.Square), [kx], [kk])
            V(lambda e: e.tensor_reduce(out=sm[:, 0:16], in_=kk[:, :].rearrange("p (h k) -> p h k", k=64), axis=AX.X,
                                        op=ALU.add), [kk], [sm])
            rsqrt(sm[:, 0:16], sm[:, 0:16], 1.0, float(EPS), [sm])
            V(lambda e: e.tensor_tensor(out=kk[:, :].rearrange("p (h k) -> p h k", k=64),
                                        in0=kx[:, :].rearrange("p (h k) -> p h k", k=64),
                                        in1=bc3(sm[:, 0:16], 64), op=ALU.mult), [kx, sm], [kk])
            for d in range(2):
                V(lambda e: e.scalar_tensor_tensor(out=tmp[:, :], in0=ad[d][:, :], scalar=-1.0, in1=cst["k_a"][:, :],
                                                   op0=ALU.add, op1=ALU.mult), [ad[d], cst["k_a"]], [tmp])
                V(lambda e: e.scalar_tensor_tensor(out=kd[d][:, :], in0=tmp[:, :], scalar=1.0, in1=k_ap,
                                                   op0=ALU.add, op1=ALU.mult), [tmp, rkv], [kd[d]])
                G(lambda e: e.tensor_tensor(out=pd[d][:, :], in0=kk[:, :], in1=ad[d][:, :], op=ALU.mult), [kk, ad[d]], [pd[d]])
                G(lambda e: e.tensor_tensor(out=tmp[:, :], in0=kd[d][:, :], in1=cst["r_k"][:, :], op=ALU.mult),
                  [kd[d], cst["r_k"]], [tmp])
                V(lambda e: e.tensor_tensor(out=tmp[:, :], in0=tmp[:, :], in1=r_ap, op=ALU.mult), [tmp, rkv], [tmp])
                V(lambda e: e.tensor_reduce(out=bon[:, d * 16:(d + 1) * 16], in_=tmp[:, :].rearrange("p (h k) -> p h k", k=64),
                                            axis=AX.X, op=ALU.add), [tmp], [bon])
            ST(BON[i * 128:(i + 1) * 128, :], bon[:, :], [bon], [bBON])
            cut("p3.5")
            for d in range(2):
                def expo(lhs, scale, fn2):
                    for hf in range(2):
                        cs = slice(hf * 512, (hf + 1) * 512)
                        p = PS()
                        MM(p[:, :], lhs[:, :], sg[d][:, cs], [lhs, sg[d]], [p])
                        A(lambda e: e.activation(out=ex[:, cs], in_=p[:, :], func=AF.Exp, scale=scale), [p], [ex])
                    fn2()
                sc = -DECAY_SCALE
                expo(A1[d], sc, lambda: V(lambda e: e.tensor_tensor(out=X["rm"][:, :], in0=r_ap, in1=ex[:, :], op=ALU.mult),
                                          [rkv, ex], [X["rm"]]))
                expo(A2[d], sc, lambda: V(lambda e: e.scalar_tensor_tensor(out=X["nm"][:, :], in0=kk[:, :], scalar=-1.0,
                                                                           in1=ex[:, :], op0=ALU.mult, op1=ALU.mult),
                                          [kk, ex], [X["nm"]]))

                def f3():
                    V(lambda e: e.tensor_tensor(out=X["pm"][:, :], in0=pd[d][:, :], in1=ex[:, :], op=ALU.mult), [pd[d], ex], [X["pm"]])
                    G(lambda e: e.tensor_tensor(out=X["km"][:, :], in0=kd[d][:, :], in1=ex[:, :], op=ALU.mult), [kd[d], ex], [X["km"]])
                expo(A1[d], -sc, f3)
                expo(CI[d], sc, lambda: V(lambda e: e.tensor_tensor(out=X["rtr"][:, :], in0=r_ap, in1=ex[:, :], op=ALU.mult),
                                          [rkv, ex], [X["rtr"]]))
                expo(CS[d], sc, lambda: V(lambda e: e.scalar_tensor_tensor(out=X["ntr"][:, :], in0=kk[:, :], scalar=-1.0,
                                                                           in1=ex[:, :], op0=ALU.mult, op1=ALU.mult),
                                          [kk, ex], [X["ntr"]]))

                def f6():
                    V(lambda e: e.tensor_tensor(out=X["ph"][:, :], in0=pd[d][:, :], in1=ex[:, :], op=ALU.mult), [pd[d], ex], [X["ph"]])
                    G(lambda e: e.tensor_tensor(out=X["kh"][:, :], in0=kd[d][:, :], in1=ex[:, :], op=ALU.mult), [kd[d], ex], [X["kh"]])
                expo(A6[d], sc, f6)
                expo(ones, sc, lambda: G(lambda e: e.tensor_copy(out=X["gam"][:, :], in_=ex[:, :]), [ex], [X["gam"]]))
                cut("p3.7")
                Xd = dict(X)
                Xd["v"] = Tl(rkv.t[:, 2048:3072], rkv.b)
                summaries(ph3, d, 64, 64, 16, Xd, SR, i, {"outs": outs, "bsum": bSR, "sets": usets})
                cut("p3.8")
        ph3.close()
        if stop == "P3":
            S.barrier()
            return nc

        ph4 = Phase()
        cw_ = [ph4.sb([128, 3072], name="convw") for _ in range(3)]
        for j in range(3):
            LD(cw_[j][:, :], conv_w[l, j].partition_broadcast(128), [cw_[j]])
        usets = unit_bufs(ph4, dec=True)
        alg = ph4.sb([128, 16], name="alg")
        dtb = ph4.sb([128, 16], name="dtb")
        LD(alg[:, :], a_log[l].partition_broadcast(128), [alg])
        LD(dtb[:, :], dt_bias[l].partition_broadcast(128), [dtb])
        A(lambda e: e.activation(out=alg[:, :], in_=alg[:, :], func=AF.Exp), [alg], [alg])
        acc = ph4.sb([128, 3072], name="acc")
        xw = [ph4.sb([128, 3072], name="xw"), acc, ph4.sb([128, 3072], name="xw")]
        ba = ph4.sb([128, 32], name="ba")
        sm = ph4.sb([128, 16 * 12], name="sm4")
        X = {k_: ph4.sb([128, AW], name=k_) for k_ in ("pm", "km", "ntr", "rtr", "ph", "kh", "gam")}
        outs = {"GT": ph4.sb([128, 1024], name="oGT"), "H": ph4.sb([128, 1024], name="oH"),
                "RT": ph4.sb([128, 1024], name="oRT"), "Y0": ph4.sb([128, 1024], name="oY0")}
        for i in range(NT):
            for j in range(3):
                r0 = i * 128 + j
                LD(xw[j][:, :], P[r0:r0 + 128, C_QKV:C_QKV + 3072], [xw[j]], [bP])
            LD(ba[:, :], P[1 + i * 128:1 + (i + 1) * 128, C_BETA:C_BETA + 32], [ba], [bP])
            V(lambda e: e.tensor_tensor(out=acc[:, :], in0=acc[:, :], in1=cw_[1][:, :], op=ALU.mult), [acc, cw_[1]], [acc])
            V(lambda e: e.scalar_tensor_tensor(out=xw[0][:, :], in0=xw[0][:, :], scalar=cmask[:, 2 * i:2 * i + 1], in1=cw_[0][:, :],
                                               op0=ALU.mult, op1=ALU.mult), [xw[0], cmask, cw_[0]], [xw[0]])
            G(lambda e: e.tensor_tensor(out=acc[:, :], in0=acc[:, :], in1=xw[0][:, :], op=ALU.add), [acc, xw[0]], [acc])
            V(lambda e: e.scalar_tensor_tensor(out=xw[2][:, :], in0=xw[2][:, :], scalar=cmask[:, 2 * i + 1:2 * i + 2],
                                               in1=cw_[2][:, :], op0=ALU.mult, op1=ALU.mult), [xw[2], cmask, cw_[2]], [xw[2]])
            G(lambda e: e.tensor_tensor(out=acc[:, :], in0=acc[:, :], in1=xw[2][:, :], op=ALU.add), [acc, xw[2]], [acc])
            tmp = xw[0]
            A(lambda e: e.activation(out=acc[:, :], in_=acc[:, :], func=AF.Silu), [acc], [acc])
            A(lambda e: e.activation(out=tmp[:, 0:2048], in_=acc[:, 0:2048], func=AF.Square), [acc], [tmp])
            V(lambda e: e.tensor_reduce(out=sm[:, 0:16], in_=tmp[:, 0:2048].rearrange("p (h k) -> p h k", k=128), axis=AX.X,
                                        op=ALU.add), [tmp], [sm])
            rsqrt(sm[:, 0:16], sm[:, 0:16], 1.0, float(EPS), [sm])
            V(lambda e: e.tensor_scalar(out=sm[:, 0:8], in0=sm[:, 0:8], scalar1=float(128 ** -0.5), scalar2=None, op0=ALU.mult),
              [sm], [sm])
            V(lambda e: e.tensor_tensor(out=acc[:, 0:2048].rearrange("p (h k) -> p h k", k=128),
                                        in0=acc[:, 0:2048].rearrange("p (h k) -> p h k", k=128),
                                        in1=bc3(sm[:, 0:16], 128), op=ALU.mult), [acc, sm], [acc])
            qn, kn = acc[:, 0:1024], acc[:, 1024:2048]
            A(lambda e: e.activation(out=sm[:, 16:32], in_=ba[:, 0:16], func=AF.Sigmoid), [ba], [sm])
            V(lambda e: e.tensor_tensor(out=sm[:, 32:48], in0=ba[:, 16:32], in1=dtb[:, :], op=ALU.add), [ba, dtb], [sm])
            A(lambda e: e.activation(out=sm[:, 32:48], in_=sm[:, 32:48], func=AF.Exp), [sm], [sm])
            A(lambda e: e.activation(out=sm[:, 32:48], in_=sm[:, 32:48], func=AF.Ln, bias=1.0), [sm], [sm])
            V(lambda e: e.scalar_tensor_tensor(out=sm[:, 32:48], in0=sm[:, 32:48], scalar=-1.0, in1=alg[:, :],
                                               op0=ALU.mult, op1=ALU.mult), [sm, alg], [sm])
            A(lambda e: e.activation(out=sm[:, 48:64], in_=sm[:, 32:48], func=AF.Exp), [sm], [sm])
            V(lambda e: e.scalar_tensor_tensor(out=sm[:, 48:64], in0=sm[:, 48:64], scalar=-1.0, in1=sm[:, 16:32],
                                               op0=ALU.mult, op1=ALU.mult), [sm], [sm])
            for d in range(2):
                gcol = sm[:, 32 + d * 8:32 + (d + 1) * 8]
                p = PS()
                for q, lhs in ((3, CI[d]), (4, CS[d]), (5, A6[d]), (6, ones)):
                    MM(p[:, q * 8:(q + 1) * 8], lhs[:, :], gcol, [lhs, sm], [p])
                A(lambda e: e.activation(out=sm[:, 64 + 24:64 + 56], in_=p[:, 24:56], func=AF.Exp), [p], [sm])
                V(lambda e: e.tensor_scalar(out=sm[:, 176:184], in0=p[:, 24:32], scalar1=-1.0, scalar2=None, op0=ALU.mult),
                  [p], [sm])
                V(lambda e: e.tensor_copy(out=sm[:, 168:176], in_=p[:, 32:40]), [p], [sm])
                E = lambda q: sm[:, 64 + q * 8:64 + (q + 1) * 8]
                bet = sm[:, 16 + d * 8:16 + (d + 1) * 8]
                nb = sm[:, 48 + d * 8:48 + (d + 1) * 8]
                sc_ = lambda q: sm[:, 128 + q * 8:128 + (q + 1) * 8]
                V(lambda e: e.tensor_tensor(out=sc_(2), in0=nb, in1=E(5), op=ALU.mult), [sm], [sm])
                V(lambda e: e.tensor_tensor(out=sc_(3), in0=bet, in1=E(5), op=ALU.mult), [sm], [sm])

                def bm(dst, src_ap, s_ap, eng):
                    eng(lambda e: e.tensor_tensor(out=dst[:, :].rearrange("p (h k) -> p h k", k=128),
                                                  in0=src_ap.rearrange("p (h k) -> p h k", k=128),
                                                  in1=bc3(s_ap, 128), op=ALU.mult), [acc, sm], [dst])
                bm(X["pm"], kn, nb, V)
                bm(X["km"], kn, bet, G)
                bm(X["rtr"], qn, E(3), V)
                bm(X["ntr"], kn, E(4), G)
                bm(X["ph"], kn, sc_(2), V)
                bm(X["kh"], kn, sc_(3), G)
                V(lambda e: e.tensor_copy(out=X["gam"][:, :].rearrange("p (h k) -> p h k", k=128), in_=bc3(E(6), 128)),
                  [sm], [X["gam"]])
                Xd = dict(X)
                Xd["v"] = Tl(acc.t[:, 2048:3072], acc.b)
                Xd["rm"] = Tl(acc.t[:, 0:1024], acc.b)
                Xd["nm"] = Tl(acc.t[:, 1024:2048], acc.b)
                decs = [dict(g=sm[:, 32 + d * 8 + hh_i:32 + d * 8 + hh_i + 1], ncw=sm[:, 176 + hh_i:177 + hh_i],
                             cwx=sm[:, 168 + hh_i:169 + hh_i], b=sm) for hh_i in range(8)]
                summaries(ph4, d, 128, 128, 8, Xd, SD, i, {"outs": outs, "bsum": bSD, "sets": usets, "decs": decs})
        ph4.close()
        if stop == "P4":
            S.barrier()
            return nc

        ph5 = Phase()
        for (K, Hn, Vd, SUM, bS, init, fin, YD, bY) in ((64, 16, 64, SR, bSR, init_r, fin_r, YA, bYA),
                                                        (128, 8, 128, SD, bSD, init_d, fin_d, YB, bYB)):
            M = [[ph5.sb([128, 1024], name="M") for _ in range(2)] for _ in range(2)]
            gt = [ph5.sb([128, 1024], name="gt") for _ in range(2)]
            hh_ = [ph5.sb([128, 1024], name="h") for _ in range(2)]
            rt = [ph5.sb([128, Hn * 128], name="rt") for _ in range(2)]
            y0 = [ph5.sb([128, 1024], name="y0") for _ in range(2)]
            yo = [ph5.sb([128, 1024], name="yo") for _ in range(2)]
            cur = [0, 0]
            for d in range(2):
                LD(M[d][0][0:K, :], init[l, d], [M[d][0]])
            hpj = 512 // Vd
            for step in range(NT):
                for d in range(2):
                    c = step if d == 0 else NT - 1 - step
                    Mc, Mn = M[d][cur[d]], M[d][1 - cur[d]]
                    if step > 0 and step % 2 == 0:
                        slot = (c // 2 - 1) if d == 0 else ((c + 1) // 2)
                        ST(fin[l, slot, d], Mc[0:K, :], [Mc], [bFIN])
                        V(lambda e: e.tensor_scalar(out=Mc[0:K, :], in0=Mc[0:K, :], scalar1=flagc[0:K, 0:1], scalar2=None,
                                                    op0=ALU.mult), [Mc, flagc], [Mc])
                    LD(gt[d][0:K, :], SUM["GT"][c, d], [gt[d]], [bS])
                    LD(hh_[d][0:K, :], SUM["H"][c, d], [hh_[d]], [bS])
                    LD(rt[d][0:K, :], SUM["RT"][c, d], [rt[d]], [bS])
                    LD(y0[d][:, :], SUM["Y0"][c, d], [y0[d]], [bS])
                    for j in range(Hn // hpj):
                        p = PS()
                        for hq in range(hpj):
                            h = j * hpj + hq
                            MM(p[:, hq * Vd:(hq + 1) * Vd], rt[d][0:K, h * 128:(h + 1) * 128], Mc[0:K, h * Vd:(h + 1) * Vd],
                               [rt[d], Mc], [p])
                        V(lambda e: e.tensor_tensor(out=yo[d][:, j * 512:(j + 1) * 512], in0=p[:, :],
                                                    in1=y0[d][:, j * 512:(j + 1) * 512], op=ALU.add), [p, y0[d]], [yo[d]])
                        p = PS()
                        for hq in range(hpj):
                            h = j * hpj + hq
                            MM(p[0:K, hq * Vd:(hq + 1) * Vd], gt[d][0:K, h * K:(h + 1) * K], Mc[0:K, h * Vd:(h + 1) * Vd],
                               [gt[d], Mc], [p])
                        V(lambda e: e.tensor_tensor(out=Mn[0:K, j * 512:(j + 1) * 512], in0=p[0:K, :],
                                                    in1=hh_[d][0:K, j * 512:(j + 1) * 512], op=ALU.add), [p, hh_[d]], [Mn])
                    ST(YD[d, c * 128:(c + 1) * 128, :], yo[d][:, :], [yo[d]], [bY])
                    cur[d] = 1 - cur[d]
            for d in range(2):
                slot = 7 if d == 0 else 0
                ST(fin[l, slot, d], M[d][cur[d]][0:K, :], [M[d][cur[d]]], [bFIN])
        ph5.close()
        if stop == "P5":
            S.barrier()
            return nc

        ph6 = Phase()
        yT = ph6.sb([128, 16, T], BF16, name="yT")
        ph6a = Phase()
        gnw = ph6a.sb([128, AW], name="gnw")
        gnb = ph6a.sb([128, AW], name="gnb")
        onw = ph6a.sb([128, 128], name="onw")
        LD(gnw[:, :], gn_w[l].partition_broadcast(128), [gnw])
        LD(gnb[:, :], gn_b[l].partition_broadcast(128), [gnb])
        LD(onw[:, :], o_norm_w[l].partition_broadcast(128), [onw])
        V(lambda e: e.tensor_scalar(out=gnw[:, :], in0=gnw[:, :], scalar1=8.0, scalar2=None, op0=ALU.mult), [gnw], [gnw])
        V(lambda e: e.tensor_scalar(out=onw[:, :], in0=onw[:, :], scalar1=float(128 ** 0.5), scalar2=None, op0=ALU.mult),
          [onw], [onw])
        yf = ph6a.sb([128, AW], name="yf")
        yb_ = ph6a.sb([128, AW], name="yb")
        vz = ph6a.sb([128, 2048], name="vz")
        zb = ph6a.sb([128, AW], name="zb")
        bon = ph6a.sb([128, 32], name="bon6")
        t6 = ph6a.sb([128, AW], name="t6")
        sm = ph6a.sb([128, 64], name="sm6")
        for i in range(NT):
            rows = slice(i * 128, (i + 1) * 128)
            prow = slice(1 + i * 128, 1 + (i + 1) * 128)
            LD(yf[:, :], YA[0, rows, :], [yf], [bYA])
            LD(yb_[:, :], YA[1, rows, :], [yb_], [bYA])
            LD(vz[:, :], P[prow, C_V:C_V + 2048], [vz], [bP])
            LD(bon[:, :], BON[rows, :], [bon], [bBON])
            V(lambda e: e.tensor_tensor(out=yf[:, :], in0=yf[:, :], in1=yb_[:, :], op=ALU.add), [yf, yb_], [yf])
            V(lambda e: e.tensor_tensor(out=bon[:, 0:16], in0=bon[:, 0:16], in1=bon[:, 16:32], op=ALU.add), [bon], [bon])
            V(lambda e: e.tensor_tensor(out=t6[:, :].rearrange("p (h k) -> p h k", k=64),
                                        in0=vz[:, 0:1024].rearrange("p (h k) -> p h k", k=64),
                                        in1=bc3(bon[:, 0:16], 64), op=ALU.mult), [vz, bon], [t6])
            G(lambda e: e.tensor_tensor(out=yf[:, :], in0=yf[:, :], in1=t6[:, :], op=ALU.add), [yf, t6], [yf])
            V(lambda e: e.tensor_reduce(out=sm[:, 0:16], in_=yf[:, :].rearrange("p (h k) -> p h k", k=64), axis=AX.X, op=ALU.add),
              [yf], [sm])
            V(lambda e: e.tensor_scalar(out=sm[:, 0:16], in0=sm[:, 0:16], scalar1=-1.0 / 64, scalar2=None, op0=ALU.mult), [sm], [sm])
            V(lambda e: e.tensor_tensor(out=yf[:, :].rearrange("p (h k) -> p h k", k=64),
                                        in0=yf[:, :].rearrange("p (h k) -> p h k", k=64),
                                        in1=bc3(sm[:, 0:16], 64), op=ALU.add), [yf, sm], [yf])
            A(lambda e: e.activation(out=t6[:, :], in_=yf[:, :], func=AF.Square), [yf], [t6])
            V(lambda e: e.tensor_reduce(out=sm[:, 16:32], in_=t6[:, :].rearrange("p (h k) -> p h k", k=64), axis=AX.X, op=ALU.add),
              [t6], [sm])
            rsqrt(sm[:, 16:32], sm[:, 16:32], 1.0, float(GN_EPS * 64), [sm])
            V(lambda e: e.tensor_tensor(out=yf[:, :].rearrange("p (h k) -> p h k", k=64),
                                        in0=yf[:, :].rearrange("p (h k) -> p h k", k=64),
                                        in1=bc3(sm[:, 16:32], 64), op=ALU.mult), [yf, sm], [yf])
            V(lambda e: e.tensor_tensor(out=yf[:, :], in0=yf[:, :], in1=gnw[:, :], op=ALU.mult), [yf, gnw], [yf])
            G(lambda e: e.tensor_tensor(out=yf[:, :], in0=yf[:, :], in1=gnb[:, :], op=ALU.add), [yf, gnb], [yf])
            A(lambda e: e.activation(out=vz[:, 1024:2048], in_=vz[:, 1024:2048], func=AF.Silu), [vz], [vz])
            V(lambda e: e.tensor_tensor(out=yf[:, :], in0=yf[:, :], in1=vz[:, 1024:2048], op=ALU.mult), [yf, vz], [yf])
            for c4 in range(2):
                p = PS()
                for q in range(4):
                    kc = c4 * 4 + q
                    TR(p[:, q * 128:(q + 1) * 128], yf[:, kc * 128:(kc + 1) * 128], [yf], [p])
                A(lambda e: e.activation(out=yT[:, c4 * 4:(c4 + 1) * 4, i * 128:(i + 1) * 128],
                                         in_=p[:, :].rearrange("p (q t) -> p q t", q=4), func=AF.Copy), [p], [yT])
            LD(yf[:, :], YB[0, rows, :], [yf], [bYB])
            LD(yb_[:, :], YB[1, rows, :], [yb_], [bYB])
            LD(zb[:, :], P[prow, C_ZB:C_ZB + 1024], [zb], [bP])
            V(lambda e: e.tensor_tensor(out=yf[:, :], in0=yf[:, :], in1=yb_[:, :], op=ALU.add), [yf, yb_], [yf])
            A(lambda e: e.activation(out=t6[:, :], in_=yf[:, :], func=AF.Square), [yf], [t6])
            V(lambda e: e.tensor_reduce(out=sm[:, 32:40], in_=t6[:, :].rearrange("p (h k) -> p h k", k=128), axis=AX.X, op=ALU.add),
              [t6], [sm])
            rsqrt(sm[:, 32:40], sm[:, 32:40], 1.0, float(EPS * 128), [sm])
            V(lambda e: e.tensor_tensor(out=yf[:, :].rearrange("p (h k) -> p h k", k=128),
                                        in0=yf[:, :].rearrange("p (h k) -> p h k", k=128),
                                        in1=bc3(sm[:, 32:40], 128), op=ALU.mult), [yf, sm], [yf])
            V(lambda e: e.tensor_tensor(out=yf[:, :].rearrange("p (h k) -> p h k", k=128),
                                        in0=yf[:, :].rearrange("p (h k) -> p h k", k=128),
                                        in1=onw[:, :].unsqueeze(1).broadcast_to([128, 8, 128]), op=ALU.mult), [yf, onw], [yf])
            A(lambda e: e.activation(out=zb[:, :], in_=zb[:, :], func=AF.Silu), [zb], [zb])
            V(lambda e: e.tensor_tensor(out=yf[:, :], in0=yf[:, :], in1=zb[:, :], op=ALU.mult), [yf, zb], [yf])
            for c4 in range(2):
                p = PS()
                for q in range(4):
                    kc = c4 * 4 + q
                    TR(p[:, q * 128:(q + 1) * 128], yf[:, kc * 128:(kc + 1) * 128], [yf], [p])
                A(lambda e: e.activation(out=yT[:, 8 + c4 * 4:8 + (c4 + 1) * 4, i * 128:(i + 1) * 128],
                                         in_=p[:, :].rearrange("p (q t) -> p q t", q=4), func=AF.Copy), [p], [yT])
        ph6a.close()
        if stop == "P6a":
            S.barrier()
            return nc

        ph6b = Phase()
        wst = [ph6b.sb([128, 16, 512], name="wst")] * 2
        wbf = [ph6b.sb([128, 16, 512], BF16, name="wbf") for _ in range(2)]
        gts = [ph6b.sb([128, 1024], name="gts") for _ in range(2)]
        mgt = [ph6b.sb([128, 512], name="mgt") for _ in range(2)]
        t2 = [ph6b.sb([128, 512], name="t2") for _ in range(2)]
        wpa = w_pa[l].rearrange("(kc p) n -> p kc n", p=128)
        wpb = w_pb[l].rearrange("(kc p) n -> p kc n", p=128)
        for cg in range(4):
            cs = slice(cg * 512, (cg + 1) * 512)
            w, wb = wst[cg % 2], wbf[cg % 2]
            for k4 in range(2):
                LD(w[:, k4 * 4:(k4 + 1) * 4, :], wpa[:, k4 * 4:(k4 + 1) * 4, cs], [w])
                LD(w[:, 8 + k4 * 4:8 + (k4 + 1) * 4, :], wpb[:, k4 * 4:(k4 + 1) * 4, cs], [w])
            for k4 in range(4):
                G(lambda e: e.tensor_copy(out=wb[:, k4 * 4:(k4 + 1) * 4, :], in_=w[:, k4 * 4:(k4 + 1) * 4, :]), [w], [wb])
            for i in range(NT):
                prow = slice(1 + i * 128, 1 + (i + 1) * 128)
                gt_, mg_, t2_ = gts[i % 2], mgt[i % 2], t2[i % 2]
                LD(gt_[:, 0:512], P[prow, C_GA + cg * 512:C_GA + (cg + 1) * 512], [gt_], [bP])
                LD(gt_[:, 512:1024], P[prow, C_GB + cg * 512:C_GB + (cg + 1) * 512], [gt_], [bP])
                A(lambda e: e.activation(out=gt_[:, :], in_=gt_[:, :], func=AF.Sigmoid), [gt_], [gt_])
                pa = PS()
                for kc in range(8):
                    MM(pa[:, :], yT[:, kc, i * 128:(i + 1) * 128], wb[:, kc, :], [yT, wb], [pa], start=(kc == 0), stop=(kc == 7))
                pb = PS()
                for kc in range(8):
                    MM(pb[:, :], yT[:, 8 + kc, i * 128:(i + 1) * 128], wb[:, 8 + kc, :], [yT, wb], [pb], start=(kc == 0), stop=(kc == 7))
                V(lambda e: e.tensor_tensor(out=mg_[:, :], in0=pa[:, :], in1=gt_[:, 0:512], op=ALU.mult), [pa, gt_], [mg_])
                V(lambda e: e.tensor_tensor(out=t2_[:, :], in0=pb[:, :], in1=gt_[:, 512:1024], op=ALU.mult), [pb, gt_], [t2_])
                G(lambda e: e.tensor_tensor(out=mg_[:, :], in0=mg_[:, :], in1=t2_[:, :], op=ALU.add), [mg_, t2_], [mg_])
                ST(MG[i * 128:(i + 1) * 128, cs], mg_[:, :], [mg_], [bMG])
        ph6b.close()
        ph6.close()
        if stop == "P6b":
            S.barrier()
            return nc

        ph7 = Phase()
        wo = ph7.sb([128, 16, D], BF16, name="wo")
        wst = [ph7.sb([128, 16, 256], name="wst7") for _ in range(2)]
        wov = w_o[l].rearrange("(kc p) n -> p kc n", p=128)
        for cg in range(8):
            w = wst[cg % 2]
            LD(w[:, :, :], wov[:, :, cg * 256:(cg + 1) * 256], [w])
            for k4 in range(2):
                G(lambda e: e.tensor_copy(out=wo[:, k4 * 8:(k4 + 1) * 8, cg * 256:(cg + 1) * 256], in_=w[:, k4 * 8:(k4 + 1) * 8, :]), [w], [wo])
        gpg = ph7.sb([128, D], name="gpg")
        LD(gpg[:, :], GPG, [gpg], [bGPG])
        mg = [ph7.sb([128, D], name="mg7") for _ in range(2)]
        mT = [ph7.sb([128, 16, 128], BF16, name="mT") for _ in range(2)]
        xr = [ph7.sb([128, D], name="xr") for _ in range(2)]
        ot = [ph7.sb([128, D], name="ot") for _ in range(2)]
        junk = ph7.sb([128, 512], name="junk7")
        ss4 = [ph7.sb([128, 8], name="ss4") for _ in range(2)]
        for i in range(NT):
            m_, mT_, x_, o_, s_ = mg[i % 2], mT[i % 2], xr[i % 2], ot[i % 2], ss4[i % 2]
            LD(m_[:, :], MG[i * 128:(i + 1) * 128, :], [m_], [bMG])
            gather(x_, i)
            for c4 in range(4):
                p = PS()
                for q in range(4):
                    kc = c4 * 4 + q
                    TR(p[:, q * 128:(q + 1) * 128], m_[:, kc * 128:(kc + 1) * 128], [m_], [p])
                A(lambda e: e.activation(out=mT_[:, c4 * 4:(c4 + 1) * 4, :], in_=p[:, :].rearrange("p (q t) -> p q t", q=4),
                                         func=AF.Copy), [p], [mT_])
            pj = []
            for cg in range(4):
                p = PS()
                pj.append(p)
                for kc in range(16):
                    MM(p[:, :], mT_[:, kc, :], wo[:, kc, cg * 512:(cg + 1) * 512], [mT_, wo], [p], start=(kc == 0), stop=(kc == 15))
                A(lambda e: e.activation(out=junk[:, :], in_=p[:, :], func=AF.Square, accum_out=s_[:, cg:cg + 1]), [p], [junk, s_])
            V(lambda e: e.tensor_reduce(out=s_[:, 4:5], in_=s_[:, 0:4], axis=AX.X, op=ALU.add), [s_], [s_])
            rsqrt(s_[:, 4:5], s_[:, 4:5], 1.0, float(EPS * D), [s_])
            for cg in range(4):
                cs = slice(cg * 512, (cg + 1) * 512)
                V(lambda e: e.scalar_tensor_tensor(out=o_[:, cs], in0=pj[cg][:, :], scalar=s_[:, 4:5], in1=gpg[:, cs],
                                                   op0=ALU.mult, op1=ALU.mult), [pj[cg], s_, gpg], [o_])
            G(lambda e: e.tensor_tensor(out=o_[:, :], in0=o_[:, :], in1=x_[:, :], op=ALU.add), [o_, x_], [o_])
            S.dma("pool", lambda e: e.indirect_dma_start(
                out=x_dst, out_offset=bass.IndirectOffsetOnAxis(ap=idxt[:, l * NT + i:l * NT + i + 1], axis=0),
                in_=o_[:, :], in_offset=None), [o_, idxt], [bxd])
        ph7.close()
        if stop == "P7":
            S.barrier()
            return nc

    try:
        for l in range(DEPTH):
            r_ = layer(l)
            if r_ is not None:
                break
    except _Cut:
        pass
    S.barrier()
    return nc


_PROG = {}


def kernel(x_prompt, x_sample, state_rwkv, state_delta, c, c_ctx, w_mod, b_mod, g_pre, g_post,
           w_in, w0, w_up, a0, a_up, k_k, k_a, r_k, gn_w, gn_b, conv_w, a_log, dt_bias,
           o_norm_w, w_pa, w_pb, w_o):
    f32 = np.float32
    if "nc" not in _PROG:
        _PROG["nc"] = build_program()
    nc = _PROG["nc"]
    arr = lambda z: np.ascontiguousarray(np.asarray(z, dtype=f32))
    shared = {"w_mod": arr(w_mod), "b_mod": arr(b_mod), "g_pre": arr(g_pre), "g_post": arr(g_post), "w_in": arr(w_in),
              "w0": arr(w0), "w_up": arr(w_up), "a0": arr(a0), "a_up": arr(a_up), "k_k": arr(k_k), "k_a": arr(k_a),
              "r_k": arr(r_k), "gn_w": arr(gn_w), "gn_b": arr(gn_b), "conv_w": arr(conv_w),
              "a_log": arr(a_log).reshape(DEPTH, 16), "dt_bias": arr(dt_bias).reshape(DEPTH, 16),
              "o_norm_w": arr(o_norm_w), "w_pa": arr(w_pa), "w_pb": arr(w_pb), "w_o": arr(w_o)}
    x_prompt = arr(x_prompt); x_sample = arr(x_sample)
    state_rwkv = arr(state_rwkv); state_delta = arr(state_delta)
    c = arr(c); c_ctx = arr(c_ctx)
    tpos = np.arange(T)
    perm = (tpos % 32) * 64 + tpos // 32
    ii = np.arange(128)
    lmask = np.zeros((7, 3, 128, 128), f32)
    for lv in range(7):
        bsz = 1 << lv
        pb = ii[:, None] // bsz
        fb = ii[None, :] // bsz
        la = ((pb % 2 == 1) & (fb == pb - 1)).astype(f32)
        lmask[lv, 0] = la
        lmask[lv, 1] = la.T
        lmask[lv, 2] = la
    lmask = np.ascontiguousarray(lmask.transpose(2, 0, 1, 3)).reshape(128, 7, 384)
    in_maps = []
    for core in range(8):
        m = dict(shared)
        idx = np.zeros((DEPTH, T), np.int32)
        cmask = np.ones((128, 2 * NT), f32)
        if core < 4:
            m["x_in"] = x_sample[core]
            cv = c[core]
            m["flag"] = np.ones((128, 1), f32)
            m["init_r"] = np.ascontiguousarray(state_rwkv[core].transpose(0, 1, 4, 2, 3)).reshape(DEPTH, 2, 64, 1024)
            m["init_d"] = np.ascontiguousarray(state_delta[core].transpose(0, 1, 3, 2, 4)).reshape(DEPTH, 2, 128, 1024)
            for l in range(DEPTH):
                idx[l] = perm if l % 2 == 1 else tpos
            cmask[0, 0] = 0.0
            cmask[127, 2 * (NT - 1) + 1] = 0.0
        else:
            j = core - 4
            xs = np.zeros((T, D), f32)
            xs[:1024] = x_prompt[4 * j:4 * j + 4].reshape(1024, D)
            xs[1024:] = xs[:1024]
            m["x_in"] = xs
            cv = c_ctx
            m["flag"] = np.zeros((128, 1), f32)
            m["init_r"] = np.zeros((DEPTH, 2, 64, 1024), f32)
            m["init_d"] = np.zeros((DEPTH, 2, 128, 1024), f32)
            for l in range(DEPTH):
                idx[l] = tpos
            for i in range(NT):
                if i % 2 == 0:
                    cmask[0, 2 * i] = 0.0
                else:
                    cmask[127, 2 * i + 1] = 0.0
        m["cvec"] = np.ascontiguousarray(cv.reshape(16, 128).T)
        m["idx"] = np.ascontiguousarray(idx.reshape(DEPTH, NT, 128).transpose(2, 0, 1).reshape(128, DEPTH * NT))
        m["cmask"] = cmask
        m["lmask"] = lmask
        in_maps.append(m)
    res = run_bass_kernel_spmd(nc, in_maps, core_ids=list(range(8)))
    R = res.results
    y_sample = np.stack([R[b]["y_out"] for b in range(4)], 0).astype(f32)
    y_prompt = np.zeros((16, 256, D), f32)
    new_r = np.zeros((16, DEPTH, 2, 16, 64, 64), f32)
    new_d = np.zeros((16, DEPTH, 2, 8, 128, 128), f32)
    for j in range(4):
        r = R[4 + j]
        y_prompt[4 * j:4 * j + 4] = r["y_out"][:1024].reshape(4, 256, D)
        fr = r["fin_r"].reshape(DEPTH, 8, 2, 64, 16, 64)
        fd = r["fin_d"].reshape(DEPTH, 8, 2, 128, 8, 128)
        for s in range(4):
            new_r[4 * j + s] = fr[:, s].transpose(0, 1, 3, 4, 2)
            new_d[4 * j + s] = fd[:, s].transpose(0, 1, 3, 2, 4)
    return (y_prompt, y_sample, new_r, new_d)
```

```python
import numpy as np
import concourse.bass as bass
import concourse.mybir as mybir
from concourse.bass_utils import run_bass_kernel_spmd

F32 = mybir.dt.float32
BF16 = mybir.dt.bfloat16
I32 = mybir.dt.int32
ALU = mybir.AluOpType
AF = mybir.ActivationFunctionType
AX = mybir.AxisListType

D = 2048
T = 2048
NT = 16
DEPTH = 4
NIN = 12704
AW = 1024
DECAY_SCALE = 0.606531
GN_EPS = 64e-5
EPS = 1e-6
C_R, C_K, C_V, C_ZA = 0, 1024, 2048, 3072
C_LO = 4096
C_QKV = 4480
C_ZB = 7552
C_BETA, C_ALPHA = 8576, 8592
C_GA, C_GB = 8608, 10656


class Buf:
    __slots__ = ("name", "w", "r", "excl")

    def __init__(self, name, excl=False):
        self.name = name
        self.w = None
        self.r = []
        self.excl = excl


class Tl:
    __slots__ = ("t", "b")

    def __init__(self, t, b):
        self.t = t
        self.b = b

    def __getitem__(self, k):
        return self.t[k]


class Sched:
    LIMIT = 30000

    def __init__(self, nc, n_dma_sems=10):
        self.nc = nc
        self.eng = {"pe": nc.tensor, "act": nc.scalar, "dve": nc.vector, "pool": nc.gpsimd, "sp": nc.sync}
        self.nsem = 0
        self.sem = {k: self._newsem(k) for k in self.eng}
        self.cnt = {k: 0 for k in self.eng}
        self.seen = {k: {} for k in self.eng}
        self.dsems = {}
        self.n_dma_sems = n_dma_sems
        self.all_dma = []

    def _newsem(self, k):
        self.nsem += 1
        return self.nc.alloc_semaphore("s_%s_%d" % (k, self.nsem))

    def _wait(self, e, tok):
        sem, val = tok
        sid = id(sem)
        if self.seen[e].get(sid, 0) >= val:
            return
        self.seen[e][sid] = val
        self.eng[e].wait_ge(sem, val)

    def _deps(self, e, reads, writes, pe_skip):
        for b in reads:
            if b.w is not None and not (pe_skip and b.w[2]):
                self._wait(e, b.w[:2])
            if b.excl:
                for t in b.r:
                    self._wait(e, t[:2])
        for b in writes:
            if b.w is not None and not (pe_skip and b.w[2]):
                self._wait(e, b.w[:2])
            for t in b.r:
                if not (pe_skip and t[2]):
                    self._wait(e, t[:2])

    def op(self, e, fn, reads=(), writes=()):
        reads = [x.b if isinstance(x, Tl) else x for x in reads]
        writes = [x.b if isinstance(x, Tl) else x for x in writes]
        self._deps(e, reads, writes, e == "pe")
        if self.cnt[e] >= self.LIMIT:
            self.sem[e] = self._newsem(e)
            self.cnt[e] = 0
        ins = fn(self.eng[e])
        self.cnt[e] += 1
        tok = (self.sem[e], self.cnt[e], e == "pe")
        ins.then_inc(self.sem[e], 1)
        for b in reads:
            b.r.append(tok)
        for b in writes:
            b.w = tok
            b.r = []
        return tok

    def dma(self, e, fn, reads=(), writes=()):
        reads = [x.b if isinstance(x, Tl) else x for x in reads]
        writes = [x.b if isinstance(x, Tl) else x for x in writes]
        if e not in self.dsems:
            self.dsems[e] = [[self.nc.alloc_semaphore("d_%s_%d" % (e, i)), 0] for i in range(self.n_dma_sems)]
            self.dsems[e + "_i"] = 0
            self.all_dma.extend(self.dsems[e])
        i = self.dsems[e + "_i"]
        self.dsems[e + "_i"] = (i + 1) % self.n_dma_sems
        slot = self.dsems[e][i]
        self._deps(e, reads, writes, False)
        if slot[1] > 0:
            self._wait(e, (slot[0], slot[1]))
        ins = fn(self.eng[e])
        slot[1] += 16
        tok = (slot[0], slot[1], False)
        ins.then_inc(slot[0], 16)
        for b in reads:
            b.r.append(tok)
        for b in writes:
            b.w = tok
            b.r = []
        return tok

    def barrier(self):
        for e in self.eng:
            for e2 in self.eng:
                if self.cnt[e2] > 0:
                    self._wait(e, (self.sem[e2], self.cnt[e2]))
            for s in self.all_dma:
                if s[1] > 0:
                    self._wait(e, (s[0], s[1]))


class Ctx:
    pass


class _Cut(Exception):
    pass


def build_program(stop=None):
    nc = bass.Bass("TRN2", target_bir_lowering=False)
    S = Sched(nc)
    g = Ctx()

    def cut(tag):
        if stop == tag:
            raise _Cut()

    def din(name, shape, dt=F32):
        return nc.dram_tensor(name, list(shape), dt, kind="ExternalInput").ap()

    def dout(name, shape):
        return nc.dram_tensor(name, list(shape), F32, kind="ExternalOutput").ap()

    import os as _os2
    _dbg_out = bool(_os2.environ.get("KDBG_OUT"))

    def dscr(name, shape):
        return nc.dram_tensor(name, list(shape), F32, kind="ExternalOutput" if _dbg_out else "Internal").ap()

    x_in = din("x_in", [T, D])
    cvec = din("cvec", [128, 16])
    flag_in = din("flag", [128, 1])
    idx_in = din("idx", [128, DEPTH * NT], I32)
    cmask_in = din("cmask", [128, 2 * NT])
    lmask_in = din("lmask", [128, 7, 384])
    init_r = din("init_r", [DEPTH, 2, 64, 1024])
    init_d = din("init_d", [DEPTH, 2, 128, 1024])
    w_mod = din("w_mod", [DEPTH, D, 3 * D])
    b_mod = din("b_mod", [DEPTH, 3 * D])
    g_pre = din("g_pre", [DEPTH, D])
    g_post = din("g_post", [DEPTH, D])
    w_in = din("w_in", [DEPTH, D, NIN])
    w0 = din("w0", [DEPTH, 2, AW])
    w_up = din("w_up", [DEPTH, 2, 96, AW])
    a0 = din("a0", [DEPTH, 2, AW])
    a_up = din("a_up", [DEPTH, 2, 96, AW])
    k_k = din("k_k", [DEPTH, AW])
    k_a = din("k_a", [DEPTH, AW])
    r_k = din("r_k", [DEPTH, AW])
    gn_w = din("gn_w", [DEPTH, AW])
    gn_b = din("gn_b", [DEPTH, AW])
    conv_w = din("conv_w", [DEPTH, 3, 3072])
    a_log = din("a_log", [DEPTH, 16])
    dt_bias = din("dt_bias", [DEPTH, 16])
    o_norm_w = din("o_norm_w", [DEPTH, 128])
    w_pa = din("w_pa", [DEPTH, AW, D])
    w_pb = din("w_pb", [DEPTH, AW, D])
    w_o = din("w_o", [DEPTH, D, D])

    y_out = dout("y_out", [T, D])
    fin_r = dout("fin_r", [DEPTH, 8, 2, 64, 1024])
    fin_d = dout("fin_d", [DEPTH, 8, 2, 128, 1024])

    XS = [dscr("xs0", [T, D]), dscr("xs1", [T, D])]
    P = dscr("proj", [T + 2, NIN])
    GPG = dscr("gpg", [128, D])
    SR = {"GT": dscr("sr_gt", [NT, 2, 64, 1024]), "H": dscr("sr_h", [NT, 2, 64, 1024]),
          "RT": dscr("sr_rt", [NT, 2, 64, 2048]), "Y0": dscr("sr_y0", [NT, 2, 128, 1024])}
    SD = {"GT": dscr("sd_gt", [NT, 2, 128, 1024]), "H": dscr("sd_h", [NT, 2, 128, 1024]),
          "RT": dscr("sd_rt", [NT, 2, 128, 1024]), "Y0": dscr("sd_y0", [NT, 2, 128, 1024])}
    YA = dscr("ya", [2, T, 1024])
    YB = dscr("yb", [2, T, 1024])
    BON = dscr("bon", [T, 32])
    MG = dscr("mg", [T, D])

    names = [0]

    def sb(shape, dt=F32, name=None):
        names[0] += 1
        nm = "%s_%d" % (name or "t", names[0])
        return Tl(nc.alloc_sbuf_tensor(nm, list(shape), dt), Buf(nm))

    ident = sb([128, 128], name="ident")
    ones = sb([128, 128], name="ones")
    mU, mUi, mL, mLi = (sb([128, 128], name="m") for _ in range(4))
    cm = [sb([128, 128], name="cm") for _ in range(2)]
    A1 = [sb([128, 128], name="a1") for _ in range(2)]
    A2 = [sb([128, 128], name="a2") for _ in range(2)]
    A6 = [sb([128, 128], name="a6") for _ in range(2)]
    mask4 = [sb([128, 512], name="mask4") for _ in range(2)]
    flagc = sb([128, 1], name="flag")
    idxt = sb([128, DEPTH * NT], I32, name="idx")
    cmask = sb([128, 2 * NT], name="cmask")
    zrow = sb([1, 512], name="zrow")
    LM = sb([128, 7, 384], name="LM")
    II = sb([128, 256], name="II")
    identb = sb([128, 128], BF16, name="identb")

    psb = [Tl(nc.alloc_psum_tensor("psb%d" % i, [128, 512], F32), Buf("psb%d" % i, excl=True)) for i in range(8)]
    pctr = [0]

    def PS():
        pctr[0] += 1
        return psb[pctr[0] % 8]

    def sel(t, pat, op, base, cmul):
        S.op("pool", lambda e: e.memset(t[:, :], 1.0), writes=[t])
        S.op("pool", lambda e: e.affine_select(out=t[:, :], in_=t[:, :], pattern=[[pat, 128]], compare_op=op,
                                               fill=0.0, base=base, channel_multiplier=cmul), reads=[t], writes=[t])

    sel(ident, -1, ALU.is_equal, 0, 1)
    S.op("pool", lambda e: e.tensor_copy(out=II[:, 0:128], in_=ident[:, :]), reads=[ident], writes=[II])
    S.op("pool", lambda e: e.tensor_copy(out=II[:, 128:256], in_=ident[:, :]), reads=[ident], writes=[II])
    S.op("pool", lambda e: e.tensor_copy(out=identb[:, :], in_=ident[:, :]), reads=[ident], writes=[identb])
    S.op("pool", lambda e: e.memset(ones[:, :], 1.0), writes=[ones])
    sel(mU, 1, ALU.is_gt, 0, -1)
    sel(mUi, 1, ALU.is_ge, 0, -1)
    sel(mL, -1, ALU.is_gt, 0, 1)
    sel(mLi, -1, ALU.is_ge, 0, 1)
    sel(cm[0], 0, ALU.is_ge, 64, -1)
    sel(cm[1], 0, ALU.is_ge, -64, 1)
    CI = [mUi, mLi]
    CS = [mU, mL]
    MN = [mL, mU]
    for d in range(2):
        S.op("dve", lambda e: e.tensor_tensor(out=A1[d][:, :], in0=CI[d][:, :], in1=cm[d][:, :], op=ALU.subtract),
             reads=[CI[d], cm[d]], writes=[A1[d]])
        S.op("dve", lambda e: e.tensor_tensor(out=A2[d][:, :], in0=CS[d][:, :], in1=cm[d][:, :], op=ALU.subtract),
             reads=[CS[d], cm[d]], writes=[A2[d]])
        S.op("dve", lambda e: e.tensor_tensor(out=A6[d][:, :], in0=ones[:, :], in1=CI[d][:, :], op=ALU.subtract),
             reads=[ones, CI[d]], writes=[A6[d]])
        for q, m in enumerate((CS[d], CI[d], CS[d], CI[d])):
            S.op("pool", lambda e: e.tensor_copy(out=mask4[d][:, q * 128:(q + 1) * 128], in_=m[:, :]),
                 reads=[m], writes=[mask4[d]])
    NEGT_S = [sb([128, 128], name="negts") for _ in range(2)]
    NEGT_I = [sb([128, 128], name="negti") for _ in range(2)]
    NEG_S = [sb([128, 128], name="negs") for _ in range(2)]
    for d in range(2):
        for dst, src in ((NEGT_S[d], CS[d]), (NEGT_I[d], CI[d]), (NEG_S[d], MN[d])):
            S.op("dve", lambda e: e.tensor_scalar(out=dst[:, :], in0=src[:, :], scalar1=-1.0, scalar2=1.0e5, op0=ALU.add,
                                                  op1=ALU.mult), reads=[src], writes=[dst])
    S.dma("sp", lambda e: e.dma_start(out=flagc[:, :], in_=flag_in), writes=[flagc])
    S.dma("sp", lambda e: e.dma_start(out=idxt[:, :], in_=idx_in), writes=[idxt])
    S.dma("sp", lambda e: e.dma_start(out=cmask[:, :], in_=cmask_in), writes=[cmask])
    S.dma("sp", lambda e: e.dma_start(out=LM[:, :, :], in_=lmask_in), writes=[LM])
    S.op("pool", lambda e: e.memset(zrow[:, :], 0.0), writes=[zrow])
    bP = Buf("P")
    for r0 in (0, T + 1):
        for c0 in range(0, NIN, 512):
            cw0 = min(512, NIN - c0)
            S.dma("sp", lambda e: e.dma_start(out=P[r0:r0 + 1, c0:c0 + cw0], in_=zrow[:, 0:cw0]), reads=[zrow], writes=[bP])

    def V(fn, r, w):
        S.op("dve", fn, r, w)

    def A(fn, r, w):
        S.op("act", fn, r, w)

    def G(fn, r, w):
        S.op("pool", fn, r, w)

    def MM(out, lhsT, rhs, r, w, start=True, stop=True):
        S.op("pe", lambda e: e.matmul(out, lhsT=lhsT, rhs=rhs, start=start, stop=stop), r, w)

    def TR(out, in_, r, w):
        S.op("pe", lambda e: e.transpose(out, in_, ident[:, :]), list(r) + [ident], w)

    def LD(out, in_, w, r=()):
        S.dma("sp", lambda e: e.dma_start(out=out, in_=in_), r, w)

    def ST(out, in_, r, w=()):
        S.dma("sp", lambda e: e.dma_start(out=out, in_=in_), r, w)

    def bc3(ap2, n):
        return ap2.unsqueeze(2).broadcast_to([ap2.shape[0], ap2.shape[1], n])

    def rsqrt(out_t, in_ap, scale, bias, r):
        A(lambda e: e.activation(out=out_t, in_=in_ap, func=AF.Ln, bias=bias, scale=scale), r, r)
        A(lambda e: e.activation(out=out_t, in_=out_t, func=AF.Exp, scale=-0.5), r, r)

    class Phase:
        def __init__(self):
            self.guards = []

        def sb(self, shape, dt=F32, name="p"):
            names[0] += 1
            nm = "%s_%d" % (name, names[0])
            gd = nc.sbuf_tensor(nm, list(shape), dt)
            t = gd.__enter__()
            self.guards.append(gd)
            return Tl(t, Buf(nm))

        def close(self):
            S.barrier()
            for gd in reversed(self.guards):
                gd.__exit__(None, None, None)

    def drive(gens, width=4):
        act = []
        gens = list(gens)
        while gens or act:
            while gens and len(act) < width:
                act.append(gens.pop(0))
            nxt = []
            for gen in act:
                try:
                    next(gen)
                    nxt.append(gen)
                except StopIteration:
                    pass
            act = nxt

    import os as _os3
    NU = int(_os3.environ.get("KDBG_NU", 4))

    def unit_bufs(ph, dec=False):
        return [(ph.sb([128, 512], BF16, name="chT"), ph.sb([128, 512], name="AT"),
                 ph.sb([128, 256], name="NQ"),
                 [ph.sb([128, 256], name="B") for _ in range(2)], ph.sb([128, 128], name="dg"),
                 ph.sb([128, 384], name="DM") if dec else None, ph.sb([128, 384], name="DR") if dec else None,
                 [ph.sb([128, 256], BF16, name="TW") for _ in range(2)], ph.sb([128, 256], BF16, name="YX"),
                 (ph.sb([128, 256], BF16, name="NQb"), ph.sb([128, 256], name="TWf")))
                for _ in range(NU)]

    def unit(bset, d, K, Vd, hcols, vcols, X, outs, h, dec=None):
        chT, AT, NQ0, B, dg, DM, DR, TW, YX, (NQb, TWf) = bset
        NQ = [NQ0]

        def run():
            if dec is not None:
                V(lambda e: e.tensor_scalar(out=DR[:, 0:128], in0=CS[d][:, :], scalar1=dec["g"], scalar2=None, op0=ALU.mult),
                  [CS[d], dec["b"]], [DR])
                V(lambda e: e.tensor_scalar(out=DR[:, 128:256], in0=CI[d][:, :], scalar1=dec["g"], scalar2=None, op0=ALU.mult),
                  [CI[d], dec["b"]], [DR])
                V(lambda e: e.tensor_scalar(out=DR[:, 256:384], in0=DR[:, 128:256], scalar1=-1.0, scalar2=None, op0=ALU.mult),
                  [DR], [DR])
                yield
                p = PS()
                for q, neg in enumerate((NEGT_S[d], NEGT_I[d], NEG_S[d])):
                    MM(p[:, q * 128:(q + 1) * 128], ones[:, :], DR[:, q * 128:(q + 1) * 128], [ones, DR], [p], start=True, stop=False)
                    MM(p[:, q * 128:(q + 1) * 128], ident[:, :], neg[:, :], [ident, neg], [p], start=False, stop=True)
                A(lambda e: e.activation(out=DM[:, 0:256], in_=p[:, 0:256], func=AF.Exp, bias=dec["ncw"], scale=1.0),
                  [p, dec["b"]], [DM])
                A(lambda e: e.activation(out=DM[:, 256:384], in_=p[:, 256:384], func=AF.Exp, bias=dec["cwx"], scale=1.0),
                  [p, dec["b"]], [DM])
                yield
            p = PS()
            for q, key in enumerate(("nm", "rm", "pm", "km")):
                TR(p[0:K, q * 128:(q + 1) * 128], X[key][:, hcols], [X[key]], [p])
            A(lambda e: e.activation(out=chT[0:K, :], in_=p[0:K, :], func=AF.Copy), [p], [chT])
            if h == 0:
                cut("u.1")
            yield
            p = PS()
            MM(p[:, 0:256], chT[0:K, 256:384], chT[0:K, 0:256], [chT], [p])
            MM(p[:, 256:512], chT[0:K, 384:512], chT[0:K, 0:256], [chT], [p])
            if dec is None:
                V(lambda e: e.tensor_tensor(out=AT[:, :], in0=p[:, :], in1=mask4[d][:, :], op=ALU.mult), [p, mask4[d]], [AT])
            else:
                V(lambda e: e.tensor_tensor(out=AT[:, 0:256], in0=p[:, 0:256], in1=DM[:, 0:256], op=ALU.mult), [p, DM], [AT])
                V(lambda e: e.tensor_tensor(out=AT[:, 256:512], in0=p[:, 256:512], in1=DM[:, 0:256], op=ALU.mult), [p, DM], [AT])
            p2 = PS()
            MM(p2[:, 0:128], chT[0:K, 0:128], chT[0:K, 256:384], [chT], [p2])
            if dec is None:
                V(lambda e: e.tensor_tensor(out=NQ[0][:, 0:128], in0=p2[:, 0:128], in1=MN[d][:, :], op=ALU.mult),
                  [p2, MN[d]], [NQ[0]])
            else:
                V(lambda e: e.tensor_tensor(out=NQ[0][:, 0:128], in0=p2[:, 0:128], in1=DM[:, 256:384], op=ALU.mult),
                  [p2, DM], [NQ[0]])
            G(lambda e: e.tensor_copy(out=NQ[0][:, 128:256], in_=AT[:, 0:128]), [AT], [NQ[0]])
            G(lambda e: e.tensor_copy(out=B[0][:, 0:K], in_=X["ntr"][:, hcols]), [X["ntr"]], [B[0]])
            if h == 0:
                cut("u.2")
            yield
            p = PS()
            MM(p[:, 0:Vd], AT[:, 256:384], X["v"][:, vcols], [AT, X["v"]], [p])
            A(lambda e: e.activation(out=B[0][:, K:K + Vd], in_=p[:, 0:Vd], func=AF.Copy), [p], [B[0]])
            if h == 0:
                cut("u.3")
            yield
            moff = 0 if d == 0 else 128
            G(lambda e: e.tensor_copy(out=NQb[:, :], in_=NQ0[:, :]), [NQ0], [NQb])
            G(lambda e: e.tensor_tensor(out=YX[:, :], in0=NQ0[:, :], in1=LM[:, 0, moff:moff + 256], op=ALU.mult), [NQ0, LM], [YX])
            G(lambda e: e.tensor_tensor(out=TW[1][:, :], in0=YX[:, :], in1=II[:, :], op=ALU.add), [YX, II], [TW[1]])
            cur = 1
            yield
            for lv in range(1, 7):
                twc = TW[cur]
                twn = TW[1 - cur] if lv < 6 else TWf
                p = PS()
                MM(p[:, 0:128], NQb[:, 128:256], twc[:, 0:128], [NQb, twc], [p])
                MM(p[:, 128:256], NQb[:, 0:128], twc[:, 128:256], [NQb, twc], [p])
                V(lambda e: e.tensor_tensor(out=YX[:, :], in0=p[:, 0:256], in1=LM[:, lv, moff:moff + 256], op=ALU.mult),
                  [p, LM], [YX])
                yield
                p2 = PS()
                MM(p2[:, 0:128], twc[:, 128:256], YX[:, 0:128], [twc, YX], [p2])
                MM(p2[:, 128:256], twc[:, 0:128], YX[:, 128:256], [twc, YX], [p2])
                V(lambda e: e.tensor_tensor(out=twn[:, :], in0=p2[:, 0:256], in1=twc[:, :], op=ALU.add), [p2, twc], [twn])
                cur = 1 - cur
                yield
            p = PS()
            MM(p[:, 0:K + Vd], TWf[:, 128:256], B[0][:, 0:K + Vd], [TWf, B[0]], [p])
            A(lambda e: e.activation(out=B[1][:, 0:K + Vd], in_=p[:, 0:K + Vd], func=AF.Copy), [p], [B[1]])
            cur = 1
            yield
            Bf = B[cur]
            if h == 0:
                cut("u.4")
            G(lambda e: e.tensor_tensor(out=dg[0:K, 0:K], in0=ident[0:K, 0:K], in1=X["gam"][0:K, hcols], op=ALU.mult),
              [ident, X["gam"]], [dg])
            if h == 0:
                cut("u.41")
            p = PS()
            MM(p[0:K, 0:K], Bf[:, 0:K], X["ph"][:, hcols], [Bf, X["ph"]], [p])
            if h == 0:
                cut("u.415")
            MM(p[0:K, K:K + Vd], X["ph"][:, hcols], Bf[:, K:K + Vd], [Bf, X["ph"]], [p], start=True, stop=False)
            MM(p[0:K, K:K + Vd], X["kh"][:, hcols], X["v"][:, vcols], [X["kh"], X["v"]], [p], start=False, stop=True)
            if h == 0:
                cut("u.42")
            V(lambda e: e.tensor_tensor(out=outs["GT"][0:K, h * K:(h + 1) * K], in0=p[0:K, 0:K], in1=dg[0:K, 0:K], op=ALU.add),
              [p, dg], [outs["GT"]])
            A(lambda e: e.activation(out=outs["H"][0:K, h * Vd:(h + 1) * Vd], in_=p[0:K, K:K + Vd], func=AF.Copy),
              [p], [outs["H"]])
            if h == 0:
                cut("u.43")
            yield
            p = PS()
            MM(p[0:K, 0:128], Bf[:, 0:K], AT[:, 128:256], [Bf, AT], [p], start=True, stop=False)
            MM(p[0:K, 0:128], X["rtr"][:, hcols], ident[:, :], [X["rtr"], ident], [p], start=False, stop=True)
            MM(p[:, 128:128 + Vd], AT[:, 128:256], Bf[:, K:K + Vd], [Bf, AT], [p], start=True, stop=False)
            MM(p[:, 128:128 + Vd], AT[:, 384:512], X["v"][:, vcols], [AT, X["v"]], [p], start=False, stop=True)
            if h == 0:
                cut("u.44")
            A(lambda e: e.activation(out=outs["RT"][0:K, h * 128:(h + 1) * 128], in_=p[0:K, 0:128], func=AF.Copy),
              [p], [outs["RT"]])
            V(lambda e: e.tensor_copy(out=outs["Y0"][:, h * Vd:(h + 1) * Vd], in_=p[:, 128:128 + Vd]), [p], [outs["Y0"]])
            if h == 0:
                cut("u.5")
            yield
        return run()

    def summaries(ph, d, K, Vd, H, X, SUM, tile, units_bufs):
        outs = units_bufs["outs"]
        decs = units_bufs.get("decs")
        gens = [unit(units_bufs["sets"][h % NU], d, K, Vd, slice(h * K, (h + 1) * K), slice(h * Vd, (h + 1) * Vd), X, outs, h,
                     None if decs is None else decs[h]) for h in range(H)]
        drive(gens, NU)
        bsum = units_bufs["bsum"]
        ST(SUM["GT"][tile, d], outs["GT"][0:K, :], [outs["GT"]], [bsum])
        ST(SUM["H"][tile, d], outs["H"][0:K, :], [outs["H"]], [bsum])
        ST(SUM["RT"][tile, d], outs["RT"][0:K, 0:H * 128], [outs["RT"]], [bsum])
        ST(SUM["Y0"][tile, d], outs["Y0"][:, :], [outs["Y0"]], [bsum])

    if stop == "init":
        S.barrier()
        return nc
    bX = [Buf("xs0"), Buf("xs1"), Buf("xin"), Buf("yout")]
    bGPG = Buf("gpg")
    bSR = Buf("sr")
    bSD = Buf("sd")
    bYA = Buf("ya")
    bYB = Buf("yb")
    bBON = Buf("bon")
    bMG = Buf("mg")
    bFIN = Buf("fin")

    def layer(l):
        x_src, bxs = (x_in, bX[2]) if l == 0 else (XS[(l - 1) % 2], bX[(l - 1) % 2])
        x_dst, bxd = (y_out, bX[3]) if l == DEPTH - 1 else (XS[l % 2], bX[l % 2])

        def gather(dst_tile, i, src=x_src, bsrc=bxs):
            S.dma("pool", lambda e: e.indirect_dma_start(
                out=dst_tile[:, :], out_offset=None, in_=src,
                in_offset=bass.IndirectOffsetOnAxis(ap=idxt[:, l * NT + i:l * NT + i + 1], axis=0)),
                [idxt, bsrc], [dst_tile])

        ph = Phase()
        hT = ph.sb([128, 16, T], BF16, name="hT")
        ph1 = Phase()
        cT = ph1.sb([128, 16], name="cT")
        scB = ph1.sb([128, 16, 128], name="scB")
        modt = ph1.sb([128, 3 * D], name="mod")
        bmod = [ph1.sb([128, 256], name="bmod") for _ in range(2)]
        gpre = ph1.sb([128, D], name="gpre")
        wst = [ph1.sb([128, 16, 256], name="wst") for _ in range(2)]
        LD(cT[:, :], cvec, [cT])
        A(lambda e: e.activation(out=cT[:, :], in_=cT[:, :], func=AF.Silu), [cT], [cT])
        V(lambda e: e.tensor_copy(out=scB[:, :, :], in_=bc3(cT[:, :], 128)), [cT], [scB])
        LD(gpre[:, :], g_pre[l].partition_broadcast(128), [gpre])
        wm = w_mod[l].rearrange("(kc p) n -> p kc n", p=128)
        for cg in range(24):
            w = wst[cg % 2]
            bm_ = bmod[cg % 2]
            LD(w[:, :, :], wm[:, :, cg * 256:(cg + 1) * 256], [w])
            LD(bm_[:, :], b_mod[l, cg * 256:(cg + 1) * 256].partition_broadcast(128), [bm_])
            p = PS()
            for kc in range(16):
                MM(p[:, 0:256], scB[:, kc, :], w[:, kc, :], [scB, w], [p], start=(kc == 0), stop=(kc == 15))
            V(lambda e: e.tensor_tensor(out=modt[:, cg * 256:(cg + 1) * 256], in0=p[:, 0:256],
                                        in1=bm_[:, :], op=ALU.add), [p, bm_], [modt])
        V(lambda e: e.scalar_tensor_tensor(out=modt[:, D:2 * D], in0=modt[:, D:2 * D], scalar=1.0, in1=gpre[:, :],
                                           op0=ALU.add, op1=ALU.mult), [modt, gpre], [modt])
        V(lambda e: e.tensor_scalar(out=modt[:, D:2 * D], in0=modt[:, D:2 * D], scalar1=float(D ** 0.5), scalar2=None,
                                    op0=ALU.mult), [modt], [modt])
        LD(gpre[:, :], g_post[l].partition_broadcast(128), [gpre])
        V(lambda e: e.scalar_tensor_tensor(out=modt[:, 2 * D:3 * D], in0=modt[:, 2 * D:3 * D], scalar=float(D ** 0.5),
                                           in1=gpre[:, :], op0=ALU.mult, op1=ALU.mult), [modt, gpre], [modt])
        ST(GPG, modt[:, 2 * D:3 * D], [modt], [bGPG])
        xt = [ph1.sb([128, D], name="xt") for _ in range(2)]
        hh = [ph1.sb([128, D], name="hh") for _ in range(2)]
        ssq = [ph1.sb([128, 1], name="ssq") for _ in range(2)]
        for i in range(NT):
            x_t, h_t, s_t = xt[i % 2], hh[i % 2], ssq[i % 2]
            gather(x_t, i)
            A(lambda e: e.activation(out=h_t[:, :], in_=x_t[:, :], func=AF.Square, accum_out=s_t[:, :]), [x_t], [h_t, s_t])
            rsqrt(s_t[:, :], s_t[:, :], 1.0, float(EPS * D), [s_t])
            V(lambda e: e.scalar_tensor_tensor(out=h_t[:, :], in0=x_t[:, :], scalar=s_t[:, 0:1], in1=modt[:, D:2 * D],
                                               op0=ALU.mult, op1=ALU.mult), [x_t, s_t, modt], [h_t])
            G(lambda e: e.tensor_tensor(out=h_t[:, :], in0=h_t[:, :], in1=modt[:, 0:D], op=ALU.add), [h_t, modt], [h_t])
            for c4 in range(4):
                p = PS()
                for q in range(4):
                    kc = c4 * 4 + q
                    TR(p[:, q * 128:(q + 1) * 128], h_t[:, kc * 128:(kc + 1) * 128], [h_t], [p])
                A(lambda e: e.activation(out=hT[:, c4 * 4:(c4 + 1) * 4, i * 128:(i + 1) * 128],
                                         in_=p[:, :].rearrange("p (q t) -> p q t", q=4), func=AF.Copy), [p], [hT])
        ph1.close()
        if stop == "P1":
            S.barrier()
            return nc

        ph2 = Phase()
        wst = [ph2.sb([128, 16, 512], name="wst") for _ in range(2)]
        wbf = [ph2.sb([128, 16, 512], BF16, name="wbf") for _ in range(2)]
        ost = [ph2.sb([128, 512], name="ost") for _ in range(4)]
        wi = w_in[l].rearrange("(kc p) n -> p kc n", p=128)
        oc = 0
        ncg = (NIN + 511) // 512

        def fetch(cg):
            c0 = min(cg * 512, NIN - 512)
            w, wb = wst[cg % 2], wbf[cg % 2]
            for k4 in range(4):
                LD(w[:, k4 * 4:(k4 + 1) * 4, :], wi[:, k4 * 4:(k4 + 1) * 4, c0:c0 + 512], [w])
            for k4 in range(4):
                G(lambda e: e.tensor_copy(out=wb[:, k4 * 4:(k4 + 1) * 4, :], in_=w[:, k4 * 4:(k4 + 1) * 4, :]), [w], [wb])
        fetch(0)
        for cg in range(ncg):
            if cg + 1 < ncg:
                fetch(cg + 1)
            c0 = min(cg * 512, NIN - 512)
            cw = 512
            wb = wbf[cg % 2]
            for i in range(NT):
                p = PS()
                for kc in range(16):
                    MM(p[:, 0:cw], hT[:, kc, i * 128:(i + 1) * 128], wb[:, kc, 0:cw], [hT, wb], [p],
                       start=(kc == 0), stop=(kc == 15))
                o = ost[oc % 4]
                oc += 1
                A(lambda e: e.activation(out=o[:, 0:cw], in_=p[:, 0:cw], func=AF.Copy), [p], [o])
                ST(P[1 + i * 128:1 + (i + 1) * 128, c0:c0 + cw], o[:, 0:cw], [o], [bP])
        ph2.close()
        ph.close()
        if stop == "P2":
            S.barrier()
            return nc

        ph3 = Phase()
        cst = {}
        for nm_, src in (("k_k", k_k[l]), ("k_a", k_a[l]), ("r_k", r_k[l])):
            cst[nm_] = ph3.sb([128, AW], name=nm_)
            LD(cst[nm_][:, :], src.partition_broadcast(128), [cst[nm_]])
        wup = [ph3.sb([128, AW], name="wup") for _ in range(2)]
        aup = [ph3.sb([128, AW], name="aup") for _ in range(2)]
        for d in range(2):
            LD(wup[d][0:96, :], w_up[l, d], [wup[d]])
            LD(wup[d][96:97, :], w0[l, d:d + 1, :], [wup[d]])
            LD(aup[d][0:96, :], a_up[l, d], [aup[d]])
            LD(aup[d][96:97, :], a0[l, d:d + 1, :], [aup[d]])
        usets = unit_bufs(ph3)
        rkv = ph3.sb([128, 3072], name="rkv")
        lo = ph3.sb([128, 384], name="lo")
        loT = ph3.sb([128, 4, 128], name="loT")
        G(lambda e: e.memset(loT[:, :, :], 1.0), [], [loT])
        kx = ph3.sb([128, AW], name="kx")
        kk = ph3.sb([128, AW], name="kk")
        sm = ph3.sb([128, 64], name="sm")
        bon = ph3.sb([128, 32], name="bon")
        sg = [ph3.sb([128, AW], name="sg") for _ in range(2)]
        ad = [ph3.sb([128, AW], name="ad") for _ in range(2)]
        kd = [ph3.sb([128, AW], name="kd") for _ in range(2)]
        pd = [ph3.sb([128, AW], name="pd") for _ in range(2)]
        tmp = kx
        ex = ad[0]
        X = {k_: ph3.sb([128, AW], name=k_) for k_ in ("nm", "rm", "pm", "km", "ntr", "rtr", "ph", "kh")}
        X["gam"] = ad[1]
        outs = {"GT": ph3.sb([128, 1024], name="oGT"), "H": ph3.sb([128, 1024], name="oH"),
                "RT": ph3.sb([128, 2048], name="oRT"), "Y0": ph3.sb([128, 1024], name="oY0")}
        cut("p3.0")
        for i in range(NT):
            rows = slice(1 + i * 128, 1 + (i + 1) * 128)
            LD(rkv[:, :], P[rows, 0:3072], [rkv], [bP])
            LD(lo[:, :], P[rows, C_LO:C_LO + 384], [lo], [bP])
            cut("p3.1")
            A(lambda e: e.activation(out=lo[:, 0:192], in_=lo[:, 0:192], func=AF.Tanh), [lo], [lo])
            p = PS()
            for q in range(4):
                TR(p[0:96, q * 128:(q + 1) * 128], lo[:, q * 96:(q + 1) * 96], [lo], [p])
            A(lambda e: e.activation(out=loT[0:96, :, :], in_=p[0:96, :].rearrange("p (q t) -> p q t", q=4), func=AF.Copy),
              [p], [loT])
            cut("p3.2")
            for d in range(2):
                for hf in range(2):
                    cs = slice(hf * 512, (hf + 1) * 512)
                    p = PS()
                    MM(p[:, :], loT[0:97, d, :], wup[d][0:97, cs], [loT, wup[d]], [p])
                    A(lambda e: e.activation(out=sg[d][:, cs], in_=p[:, :], func=AF.Sigmoid), [p], [sg[d]])
                    p = PS()
                    MM(p[:, :], loT[0:97, 2 + d, :], aup[d][0:97, cs], [loT, aup[d]], [p])
                    A(lambda e: e.activation(out=ad[d][:, cs], in_=p[:, :], func=AF.Sigmoid), [p], [ad[d]])
            cut("p3.3")
            r_ap, k_ap, v_ap = rkv[:, 0:1024], rkv[:, 1024:2048], rkv[:, 2048:3072]
            V(lambda e: e.tensor_tensor(out=kx[:, :], in0=k_ap, in1=cst["k_k"][:, :], op=ALU.mult), [rkv, cst["k_k"]], [kx])
            A(lambda e: e.activation(out=kk[:, :], in_=kx[:, :], func=AF.Square), [kx], [kk])
            V(lambda e: e.tensor_reduce(out=sm[:, 0:16], in_=kk[:, :].rearrange("p (h k) -> p h k", k=64), axis=AX.X,
                                        op=ALU.add), [kk], [sm])
            rsqrt(sm[:, 0:16], sm[:, 0:16], 1.0, float(EPS), [sm])
            V(lambda e: e.tensor_tensor(out=kk[:, :].rearrange("p (h k) -> p h k", k=64),
                                        in0=kx[:, :].rearrange("p (h k) -> p h k", k=64),
                                        in1=bc3(sm[:, 0:16], 64), op=ALU.mult), [kx, sm], [kk])
            for d in range(2):
                V(lambda e: e.scalar_tensor_tensor(out=tmp[:, :], in0=ad[d][:, :], scalar=-1.0, in1=cst["k_a"][:, :],
                                                   op0=ALU.add, op1=ALU.mult), [ad[d], cst["k_a"]], [tmp])
                V(lambda e: e.scalar_tensor_tensor(out=kd[d][:, :], in0=tmp[:, :], scalar=1.0, in1=k_ap,
                                                   op0=ALU.add, op1=ALU.mult), [tmp, rkv], [kd[d]])
                G(lambda e: e.tensor_tensor(out=pd[d][:, :], in0=kk[:, :], in1=ad[d][:, :], op=ALU.mult), [kk, ad[d]], [pd[d]])
                G(lambda e: e.tensor_tensor(out=tmp[:, :], in0=kd[d][:, :], in1=cst["r_k"][:, :], op=ALU.mult),
                  [kd[d], cst["r_k"]], [tmp])
                V(lambda e: e.tensor_tensor(out=tmp[:, :], in0=tmp[:, :], in1=r_ap, op=ALU.mult), [tmp, rkv], [tmp])
                V(lambda e: e.tensor_reduce(out=bon[:, d * 16:(d + 1) * 16], in_=tmp[:, :].rearrange("p (h k) -> p h k", k=64),
                                            axis=AX.X, op=ALU.add), [tmp], [bon])
            ST(BON[i * 128:(i + 1) * 128, :], bon[:, :], [bon], [bBON])
            cut("p3.5")
            for d in range(2):
                def expo(lhs, scale, fn2):
                    for hf in range(2):
                        cs = slice(hf * 512, (hf + 1) * 512)
                        p = PS()
                        MM(p[:, :], lhs[:, :], sg[d][:, cs], [lhs, sg[d]], [p])
                        A(lambda e: e.activation(out=ex[:, cs], in_=p[:, :], func=AF.Exp, scale=scale), [p], [ex])
                    fn2()
                sc = -DECAY_SCALE
                expo(A1[d], sc, lambda: V(lambda e: e.tensor_tensor(out=X["rm"][:, :], in0=r_ap, in1=ex[:, :], op=ALU.mult),
                                          [rkv, ex], [X["rm"]]))
                expo(A2[d], sc, lambda: V(lambda e: e.scalar_tensor_tensor(out=X["nm"][:, :], in0=kk[:, :], scalar=-1.0,
                                                                           in1=ex[:, :], op0=ALU.mult, op1=ALU.mult),
                                          [kk, ex], [X["nm"]]))

                def f3():
                    V(lambda e: e.tensor_tensor(out=X["pm"][:, :], in0=pd[d][:, :], in1=ex[:, :], op=ALU.mult), [pd[d], ex], [X["pm"]])
                    G(lambda e: e.tensor_tensor(out=X["km"][:, :], in0=kd[d][:, :], in1=ex[:, :], op=ALU.mult), [kd[d], ex], [X["km"]])
                expo(A1[d], -sc, f3)
                expo(CI[d], sc, lambda: V(lambda e: e.tensor_tensor(out=X["rtr"][:, :], in0=r_ap, in1=ex[:, :], op=ALU.mult),
                                          [rkv, ex], [X["rtr"]]))
                expo(CS[d], sc, lambda: V(lambda e: e.scalar_tensor_tensor(out=X["ntr"][:, :], in0=kk[:, :], scalar=-1.0,
                                                                           in1=ex[:, :], op0=ALU.mult, op1=ALU.mult),
                                          [kk, ex], [X["ntr"]]))

                def f6():
                    V(lambda e: e.tensor_tensor(out=X["ph"][:, :], in0=pd[d][:, :], in1=ex[:, :], op=ALU.mult), [pd[d], ex], [X["ph"]])
                    G(lambda e: e.tensor_tensor(out=X["kh"][:, :], in0=kd[d][:, :], in1=ex[:, :], op=ALU.mult), [kd[d], ex], [X["kh"]])
                expo(A6[d], sc, f6)
                expo(ones, sc, lambda: G(lambda e: e.tensor_copy(out=X["gam"][:, :], in_=ex[:, :]), [ex], [X["gam"]]))
                cut("p3.7")
                Xd = dict(X)
                Xd["v"] = Tl(rkv.t[:, 2048:3072], rkv.b)
                summaries(ph3, d, 64, 64, 16, Xd, SR, i, {"outs": outs, "bsum": bSR, "sets": usets})
                cut("p3.8")
        ph3.close()
        if stop == "P3":
            S.barrier()
            return nc

        ph4 = Phase()
        cw_ = [ph4.sb([128, 3072], name="convw") for _ in range(3)]
        for j in range(3):
            LD(cw_[j][:, :], conv_w[l, j].partition_broadcast(128), [cw_[j]])
        usets = unit_bufs(ph4, dec=True)
        alg = ph4.sb([128, 16], name="alg")
        dtb = ph4.sb([128, 16], name="dtb")
        LD(alg[:, :], a_log[l].partition_broadcast(128), [alg])
        LD(dtb[:, :], dt_bias[l].partition_broadcast(128), [dtb])
        A(lambda e: e.activation(out=alg[:, :], in_=alg[:, :], func=AF.Exp), [alg], [alg])
        acc = ph4.sb([128, 3072], name="acc")
        xw = [ph4.sb([128, 3072], name="xw"), acc, ph4.sb([128, 3072], name="xw")]
        ba = ph4.sb([128, 32], name="ba")
        sm = ph4.sb([128, 16 * 12], name="sm4")
        X = {k_: ph4.sb([128, AW], name=k_) for k_ in ("pm", "km", "ntr", "rtr", "ph", "kh", "gam")}
        outs = {"GT": ph4.sb([128, 1024], name="oGT"), "H": ph4.sb([128, 1024], name="oH"),
                "RT": ph4.sb([128, 1024], name="oRT"), "Y0": ph4.sb([128, 1024], name="oY0")}
        for i in range(NT):
            for j in range(3):
                r0 = i * 128 + j
                LD(xw[j][:, :], P[r0:r0 + 128, C_QKV:C_QKV + 3072], [xw[j]], [bP])
            LD(ba[:, :], P[1 + i * 128:1 + (i + 1) * 128, C_BETA:C_BETA + 32], [ba], [bP])
            V(lambda e: e.tensor_tensor(out=acc[:, :], in0=acc[:, :], in1=cw_[1][:, :], op=ALU.mult), [acc, cw_[1]], [acc])
            V(lambda e: e.scalar_tensor_tensor(out=xw[0][:, :], in0=xw[0][:, :], scalar=cmask[:, 2 * i:2 * i + 1], in1=cw_[0][:, :],
                                               op0=ALU.mult, op1=ALU.mult), [xw[0], cmask, cw_[0]], [xw[0]])
            G(lambda e: e.tensor_tensor(out=acc[:, :], in0=acc[:, :], in1=xw[0][:, :], op=ALU.add), [acc, xw[0]], [acc])
            V(lambda e: e.scalar_tensor_tensor(out=xw[2][:, :], in0=xw[2][:, :], scalar=cmask[:, 2 * i + 1:2 * i + 2],
                                               in1=cw_[2][:, :], op0=ALU.mult, op1=ALU.mult), [xw[2], cmask, cw_[2]], [xw[2]])
            G(lambda e: e.tensor_tensor(out=acc[:, :], in0=acc[:, :], in1=xw[2][:, :], op=ALU.add), [acc, xw[2]], [acc])
            tmp = xw[0]
            A(lambda e: e.activation(out=acc[:, :], in_=acc[:, :], func=AF.Silu), [acc], [acc])
            A(lambda e: e.activation(out=tmp[:, 0:2048], in_=acc[:, 0:2048], func=AF.Square), [acc], [tmp])
            V(lambda e: e.tensor_reduce(out=sm[:, 0:16], in_=tmp[:, 0:2048].rearrange("p (h k) -> p h k", k=128), axis=AX.X,
                                        op=ALU.add), [tmp], [sm])
            rsqrt(sm[:, 0:16], sm[:, 0:16], 1.0, float(EPS), [sm])
            V(lambda e: e.tensor_scalar(out=sm[:, 0:8], in0=sm[:, 0:8], scalar1=float(128 ** -0.5), scalar2=None, op0=ALU.mult),
              [sm], [sm])
            V(lambda e: e.tensor_tensor(out=acc[:, 0:2048].rearrange("p (h k) -> p h k", k=128),
                                        in0=acc[:, 0:2048].rearrange("p (h k) -> p h k", k=128),
                                        in1=bc3(sm[:, 0:16], 128), op=ALU.mult), [acc, sm], [acc])
            qn, kn = acc[:, 0:1024], acc[:, 1024:2048]
            A(lambda e: e.activation(out=sm[:, 16:32], in_=ba[:, 0:16], func=AF.Sigmoid), [ba], [sm])
            V(lambda e: e.tensor_tensor(out=sm[:, 32:48], in0=ba[:, 16:32], in1=dtb[:, :], op=ALU.add), [ba, dtb], [sm])
            A(lambda e: e.activation(out=sm[:, 32:48], in_=sm[:, 32:48], func=AF.Exp), [sm], [sm])
            A(lambda e: e.activation(out=sm[:, 32:48], in_=sm[:, 32:48], func=AF.Ln, bias=1.0), [sm], [sm])
            V(lambda e: e.scalar_tensor_tensor(out=sm[:, 32:48], in0=sm[:, 32:48], scalar=-1.0, in1=alg[:, :],
                                               op0=ALU.mult, op1=ALU.mult), [sm, alg], [sm])
            A(lambda e: e.activation(out=sm[:, 48:64], in_=sm[:, 32:48], func=AF.Exp), [sm], [sm])
            V(lambda e: e.scalar_tensor_tensor(out=sm[:, 48:64], in0=sm[:, 48:64], scalar=-1.0, in1=sm[:, 16:32],
                                               op0=ALU.mult, op1=ALU.mult), [sm], [sm])
            for d in range(2):
                gcol = sm[:, 32 + d * 8:32 + (d + 1) * 8]
                p = PS()
                for q, lhs in ((3, CI[d]), (4, CS[d]), (5, A6[d]), (6, ones)):
                    MM(p[:, q * 8:(q + 1) * 8], lhs[:, :], gcol, [lhs, sm], [p])
                A(lambda e: e.activation(out=sm[:, 64 + 24:64 + 56], in_=p[:, 24:56], func=AF.Exp), [p], [sm])
                V(lambda e: e.tensor_scalar(out=sm[:, 176:184], in0=p[:, 24:32], scalar1=-1.0, scalar2=None, op0=ALU.mult),
                  [p], [sm])
                V(lambda e: e.tensor_copy(out=sm[:, 168:176], in_=p[:, 32:40]), [p], [sm])
                E = lambda q: sm[:, 64 + q * 8:64 + (q + 1) * 8]
                bet = sm[:, 16 + d * 8:16 + (d + 1) * 8]
                nb = sm[:, 48 + d * 8:48 + (d + 1) * 8]
                sc_ = lambda q: sm[:, 128 + q * 8:128 + (q + 1) * 8]
                V(lambda e: e.tensor_tensor(out=sc_(2), in0=nb, in1=E(5), op=ALU.mult), [sm], [sm])
                V(lambda e: e.tensor_tensor(out=sc_(3), in0=bet, in1=E(5), op=ALU.mult), [sm], [sm])

                def bm(dst, src_ap, s_ap, eng):
                    eng(lambda e: e.tensor_tensor(out=dst[:, :].rearrange("p (h k) -> p h k", k=128),
                                                  in0=src_ap.rearrange("p (h k) -> p h k", k=128),
                                                  in1=bc3(s_ap, 128), op=ALU.mult), [acc, sm], [dst])
                bm(X["pm"], kn, nb, V)
                bm(X["km"], kn, bet, G)
                bm(X["rtr"], qn, E(3), V)
                bm(X["ntr"], kn, E(4), G)
                bm(X["ph"], kn, sc_(2), V)
                bm(X["kh"], kn, sc_(3), G)
                V(lambda e: e.tensor_copy(out=X["gam"][:, :].rearrange("p (h k) -> p h k", k=128), in_=bc3(E(6), 128)),
                  [sm], [X["gam"]])
                Xd = dict(X)
                Xd["v"] = Tl(acc.t[:, 2048:3072], acc.b)
                Xd["rm"] = Tl(acc.t[:, 0:1024], acc.b)
                Xd["nm"] = Tl(acc.t[:, 1024:2048], acc.b)
                decs = [dict(g=sm[:, 32 + d * 8 + hh_i:32 + d * 8 + hh_i + 1], ncw=sm[:, 176 + hh_i:177 + hh_i],
                             cwx=sm[:, 168 + hh_i:169 + hh_i], b=sm) for hh_i in range(8)]
                summaries(ph4, d, 128, 128, 8, Xd, SD, i, {"outs": outs, "bsum": bSD, "sets": usets, "decs": decs})
        ph4.close()
        if stop == "P4":
            S.barrier()
            return nc

        ph5 = Phase()
        for (K, Hn, Vd, SUM, bS, init, fin, YD, bY) in ((64, 16, 64, SR, bSR, init_r, fin_r, YA, bYA),
                                                        (128, 8, 128, SD, bSD, init_d, fin_d, YB, bYB)):
            M = [[ph5.sb([128, 1024], name="M") for _ in range(2)] for _ in range(2)]
            gt = [ph5.sb([128, 1024], name="gt") for _ in range(2)]
            hh_ = [ph5.sb([128, 1024], name="h") for _ in range(2)]
            rt = [ph5.sb([128, Hn * 128], name="rt") for _ in range(2)]
            y0 = [ph5.sb([128, 1024], name="y0") for _ in range(2)]
            yo = [ph5.sb([128, 1024], name="yo") for _ in range(2)]
            cur = [0, 0]
            for d in range(2):
                LD(M[d][0][0:K, :], init[l, d], [M[d][0]])
            hpj = 512 // Vd
            for step in range(NT):
                for d in range(2):
                    c = step if d == 0 else NT - 1 - step
                    Mc, Mn = M[d][cur[d]], M[d][1 - cur[d]]
                    if step > 0 and step % 2 == 0:
                        slot = (c // 2 - 1) if d == 0 else ((c + 1) // 2)
                        ST(fin[l, slot, d], Mc[0:K, :], [Mc], [bFIN])
                        V(lambda e: e.tensor_scalar(out=Mc[0:K, :], in0=Mc[0:K, :], scalar1=flagc[0:K, 0:1], scalar2=None,
                                                    op0=ALU.mult), [Mc, flagc], [Mc])
                    LD(gt[d][0:K, :], SUM["GT"][c, d], [gt[d]], [bS])
                    LD(hh_[d][0:K, :], SUM["H"][c, d], [hh_[d]], [bS])
                    LD(rt[d][0:K, :], SUM["RT"][c, d], [rt[d]], [bS])
                    LD(y0[d][:, :], SUM["Y0"][c, d], [y0[d]], [bS])
                    for j in range(Hn // hpj):
                        p = PS()
                        for hq in range(hpj):
                            h = j * hpj + hq
                            MM(p[:, hq * Vd:(hq + 1) * Vd], rt[d][0:K, h * 128:(h + 1) * 128], Mc[0:K, h * Vd:(h + 1) * Vd],
                               [rt[d], Mc], [p])
                        V(lambda e: e.tensor_tensor(out=yo[d][:, j * 512:(j + 1) * 512], in0=p[:, :],
                                                    in1=y0[d][:, j * 512:(j + 1) * 512], op=ALU.add), [p, y0[d]], [yo[d]])
                        p = PS()
                        for hq in range(hpj):
                            h = j * hpj + hq
                            MM(p[0:K, hq * Vd:(hq + 1) * Vd], gt[d][0:K, h * K:(h + 1) * K], Mc[0:K, h * Vd:(h + 1) * Vd],
                               [gt[d], Mc], [p])
                        V(lambda e: e.tensor_tensor(out=Mn[0:K, j * 512:(j + 1) * 512], in0=p[0:K, :],
                                                    in1=hh_[d][0:K, j * 512:(j + 1) * 512], op=ALU.add), [p, hh_[d]], [Mn])
                    ST(YD[d, c * 128:(c + 1) * 128, :], yo[d][:, :], [yo[d]], [bY])
                    cur[d] = 1 - cur[d]
            for d in range(2):
                slot = 7 if d == 0 else 0
                ST(fin[l, slot, d], M[d][cur[d]][0:K, :], [M[d][cur[d]]], [bFIN])
        ph5.close()
        if stop == "P5":
            S.barrier()
            return nc

        ph6 = Phase()
        yT = ph6.sb([128, 16, T], BF16, name="yT")
        ph6a = Phase()
        gnw = ph6a.sb([128, AW], name="gnw")
        gnb = ph6a.sb([128, AW], name="gnb")
        onw = ph6a.sb([128, 128], name="onw")
        LD(gnw[:, :], gn_w[l].partition_broadcast(128), [gnw])
        LD(gnb[:, :], gn_b[l].partition_broadcast(128), [gnb])
        LD(onw[:, :], o_norm_w[l].partition_broadcast(128), [onw])
        V(lambda e: e.tensor_scalar(out=gnw[:, :], in0=gnw[:, :], scalar1=8.0, scalar2=None, op0=ALU.mult), [gnw], [gnw])
        V(lambda e: e.tensor_scalar(out=onw[:, :], in0=onw[:, :], scalar1=float(128 ** 0.5), scalar2=None, op0=ALU.mult),
          [onw], [onw])
        yf = ph6a.sb([128, AW], name="yf")
        yb_ = ph6a.sb([128, AW], name="yb")
        vz = ph6a.sb([128, 2048], name="vz")
        zb = ph6a.sb([128, AW], name="zb")
        bon = ph6a.sb([128, 32], name="bon6")
        t6 = ph6a.sb([128, AW], name="t6")
        sm = ph6a.sb([128, 64], name="sm6")
        for i in range(NT):
            rows = slice(i * 128, (i + 1) * 128)
            prow = slice(1 + i * 128, 1 + (i + 1) * 128)
            LD(yf[:, :], YA[0, rows, :], [yf], [bYA])
            LD(yb_[:, :], YA[1, rows, :], [yb_], [bYA])
            LD(vz[:, :], P[prow, C_V:C_V + 2048], [vz], [bP])
            LD(bon[:, :], BON[rows, :], [bon], [bBON])
            V(lambda e: e.tensor_tensor(out=yf[:, :], in0=yf[:, :], in1=yb_[:, :], op=ALU.add), [yf, yb_], [yf])
            V(lambda e: e.tensor_tensor(out=bon[:, 0:16], in0=bon[:, 0:16], in1=bon[:, 16:32], op=ALU.add), [bon], [bon])
            V(lambda e: e.tensor_tensor(out=t6[:, :].rearrange("p (h k) -> p h k", k=64),
                                        in0=vz[:, 0:1024].rearrange("p (h k) -> p h k", k=64),
                                        in1=bc3(bon[:, 0:16], 64), op=ALU.mult), [vz, bon], [t6])
            G(lambda e: e.tensor_tensor(out=yf[:, :], in0=yf[:, :], in1=t6[:, :], op=ALU.add), [yf, t6], [yf])
            V(lambda e: e.tensor_reduce(out=sm[:, 0:16], in_=yf[:, :].rearrange("p (h k) -> p h k", k=64), axis=AX.X, op=ALU.add),
              [yf], [sm])
            V(lambda e: e.tensor_scalar(out=sm[:, 0:16], in0=sm[:, 0:16], scalar1=-1.0 / 64, scalar2=None, op0=ALU.mult), [sm], [sm])
            V(lambda e: e.tensor_tensor(out=yf[:, :].rearrange("p (h k) -> p h k", k=64),
                                        in0=yf[:, :].rearrange("p (h k) -> p h k", k=64),
                                        in1=bc3(sm[:, 0:16], 64), op=ALU.add), [yf, sm], [yf])
            A(lambda e: e.activation(out=t6[:, :], in_=yf[:, :], func=AF.Square), [yf], [t6])
            V(lambda e: e.tensor_reduce(out=sm[:, 16:32], in_=t6[:, :].rearrange("p (h k) -> p h k", k=64), axis=AX.X, op=ALU.add),
              [t6], [sm])
            rsqrt(sm[:, 16:32], sm[:, 16:32], 1.0, float(GN_EPS * 64), [sm])
            V(lambda e: e.tensor_tensor(out=yf[:, :].rearrange("p (h k) -> p h k", k=64),
                                        in0=yf[:, :].rearrange("p (h k) -> p h k", k=64),
                                        in1=bc3(sm[:, 16:32], 64), op=ALU.mult), [yf, sm], [yf])
            V(lambda e: e.tensor_tensor(out=yf[:, :], in0=yf[:, :], in1=gnw[:, :], op=ALU.mult), [yf, gnw], [yf])
            G(lambda e: e.tensor_tensor(out=yf[:, :], in0=yf[:, :], in1=gnb[:, :], op=ALU.add), [yf, gnb], [yf])
            A(lambda e: e.activation(out=vz[:, 1024:2048], in_=vz[:, 1024:2048], func=AF.Silu), [vz], [vz])
            V(lambda e: e.tensor_tensor(out=yf[:, :], in0=yf[:, :], in1=vz[:, 1024:2048], op=ALU.mult), [yf, vz], [yf])
            for c4 in range(2):
                p = PS()
                for q in range(4):
                    kc = c4 * 4 + q
                    TR(p[:, q * 128:(q + 1) * 128], yf[:, kc * 128:(kc + 1) * 128], [yf], [p])
                A(lambda e: e.activation(out=yT[:, c4 * 4:(c4 + 1) * 4, i * 128:(i + 1) * 128],
                                         in_=p[:, :].rearrange("p (q t) -> p q t", q=4), func=AF.Copy), [p], [yT])
            LD(yf[:, :], YB[0, rows, :], [yf], [bYB])
            LD(yb_[:, :], YB[1, rows, :], [yb_], [bYB])
            LD(zb[:, :], P[prow, C_ZB:C_ZB + 1024], [zb], [bP])
            V(lambda e: e.tensor_tensor(out=yf[:, :], in0=yf[:, :], in1=yb_[:, :], op=ALU.add), [yf, yb_], [yf])
            A(lambda e: e.activation(out=t6[:, :], in_=yf[:, :], func=AF.Square), [yf], [t6])
            V(lambda e: e.tensor_reduce(out=sm[:, 32:40], in_=t6[:, :].rearrange("p (h k) -> p h k", k=128), axis=AX.X, op=ALU.add),
              [t6], [sm])
            rsqrt(sm[:, 32:40], sm[:, 32:40], 1.0, float(EPS * 128), [sm])
            V(lambda e: e.tensor_tensor(out=yf[:, :].rearrange("p (h k) -> p h k", k=128),
                                        in0=yf[:, :].rearrange("p (h k) -> p h k", k=128),
                                        in1=bc3(sm[:, 32:40], 128), op=ALU.mult), [yf, sm], [yf])
            V(lambda e: e.tensor_tensor(out=yf[:, :].rearrange("p (h k) -> p h k", k=128),
                                        in0=yf[:, :].rearrange("p (h k) -> p h k", k=128),
                                        in1=onw[:, :].unsqueeze(1).broadcast_to([128, 8, 128]), op=ALU.mult), [yf, onw], [yf])
            A(lambda e: e.activation(out=zb[:, :], in_=zb[:, :], func=AF.Silu), [zb], [zb])
            V(lambda e: e.tensor_tensor(out=yf[:, :], in0=yf[:, :], in1=zb[:, :], op=ALU.mult), [yf, zb], [yf])
            for c4 in range(2):
                p = PS()
                for q in range(4):
                    kc = c4 * 4 + q
                    TR(p[:, q * 128:(q + 1) * 128], yf[:, kc * 128:(kc + 1) * 128], [yf], [p])
                A(lambda e: e.activation(out=yT[:, 8 + c4 * 4:8 + (c4 + 1) * 4, i * 128:(i + 1) * 128],
                                         in_=p[:, :].rearrange("p (q t) -> p q t", q=4), func=AF.Copy), [p], [yT])
        ph6a.close()
        if stop == "P6a":
            S.barrier()
            return nc

        ph6b = Phase()
        wst = [ph6b.sb([128, 16, 512], name="wst")] * 2
        wbf = [ph6b.sb([128, 16, 512], BF16, name="wbf") for _ in range(2)]
        gts = [ph6b.sb([128, 1024], name="gts") for _ in range(2)]
        mgt = [ph6b.sb([128, 512], name="mgt") for _ in range(2)]
        t2 = [ph6b.sb([128, 512], name="t2") for _ in range(2)]
        wpa = w_pa[l].rearrange("(kc p) n -> p kc n", p=128)
        wpb = w_pb[l].rearrange("(kc p) n -> p kc n", p=128)
        for cg in range(4):
            cs = slice(cg * 512, (cg + 1) * 512)
            w, wb = wst[cg % 2], wbf[cg % 2]
            for k4 in range(2):
                LD(w[:, k4 * 4:(k4 + 1) * 4, :], wpa[:, k4 * 4:(k4 + 1) * 4, cs], [w])
                LD(w[:, 8 + k4 * 4:8 + (k4 + 1) * 4, :], wpb[:, k4 * 4:(k4 + 1) * 4, cs], [w])
            for k4 in range(4):
                G(lambda e: e.tensor_copy(out=wb[:, k4 * 4:(k4 + 1) * 4, :], in_=w[:, k4 * 4:(k4 + 1) * 4, :]), [w], [wb])
            for i in range(NT):
                prow = slice(1 + i * 128, 1 + (i + 1) * 128)
                gt_, mg_, t2_ = gts[i % 2], mgt[i % 2], t2[i % 2]
                LD(gt_[:, 0:512], P[prow, C_GA + cg * 512:C_GA + (cg + 1) * 512], [gt_], [bP])
                LD(gt_[:, 512:1024], P[prow, C_GB + cg * 512:C_GB + (cg + 1) * 512], [gt_], [bP])
                A(lambda e: e.activation(out=gt_[:, :], in_=gt_[:, :], func=AF.Sigmoid), [gt_], [gt_])
                pa = PS()
                for kc in range(8):
                    MM(pa[:, :], yT[:, kc, i * 128:(i + 1) * 128], wb[:, kc, :], [yT, wb], [pa], start=(kc == 0), stop=(kc == 7))
                pb = PS()
                for kc in range(8):
                    MM(pb[:, :], yT[:, 8 + kc, i * 128:(i + 1) * 128], wb[:, 8 + kc, :], [yT, wb], [pb], start=(kc == 0), stop=(kc == 7))
                V(lambda e: e.tensor_tensor(out=mg_[:, :], in0=pa[:, :], in1=gt_[:, 0:512], op=ALU.mult), [pa, gt_], [mg_])
                V(lambda e: e.tensor_tensor(out=t2_[:, :], in0=pb[:, :], in1=gt_[:, 512:1024], op=ALU.mult), [pb, gt_], [t2_])
                G(lambda e: e.tensor_tensor(out=mg_[:, :], in0=mg_[:, :], in1=t2_[:, :], op=ALU.add), [mg_, t2_], [mg_])
                ST(MG[i * 128:(i + 1) * 128, cs], mg_[:, :], [mg_], [bMG])
        ph6b.close()
        ph6.close()
        if stop == "P6b":
            S.barrier()
            return nc

        ph7 = Phase()
        wo = ph7.sb([128, 16, D], BF16, name="wo")
        wst = [ph7.sb([128, 16, 256], name="wst7") for _ in range(2)]
        wov = w_o[l].rearrange("(kc p) n -> p kc n", p=128)
        for cg in range(8):
            w = wst[cg % 2]
            LD(w[:, :, :], wov[:, :, cg * 256:(cg + 1) * 256], [w])
            for k4 in range(2):
                G(lambda e: e.tensor_copy(out=wo[:, k4 * 8:(k4 + 1) * 8, cg * 256:(cg + 1) * 256], in_=w[:, k4 * 8:(k4 + 1) * 8, :]), [w], [wo])
        gpg = ph7.sb([128, D], name="gpg")
        LD(gpg[:, :], GPG, [gpg], [bGPG])
        mg = [ph7.sb([128, D], name="mg7") for _ in range(2)]
        mT = [ph7.sb([128, 16, 128], BF16, name="mT") for _ in range(2)]
        xr = [ph7.sb([128, D], name="xr") for _ in range(2)]
        ot = [ph7.sb([128, D], name="ot") for _ in range(2)]
        junk = ph7.sb([128, 512], name="junk7")
        ss4 = [ph7.sb([128, 8], name="ss4") for _ in range(2)]
        for i in range(NT):
            m_, mT_, x_, o_, s_ = mg[i % 2], mT[i % 2], xr[i % 2], ot[i % 2], ss4[i % 2]
            LD(m_[:, :], MG[i * 128:(i + 1) * 128, :], [m_], [bMG])
            gather(x_, i)
            for c4 in range(4):
                p = PS()
                for q in range(4):
                    kc = c4 * 4 + q
                    TR(p[:, q * 128:(q + 1) * 128], m_[:, kc * 128:(kc + 1) * 128], [m_], [p])
                A(lambda e: e.activation(out=mT_[:, c4 * 4:(c4 + 1) * 4, :], in_=p[:, :].rearrange("p (q t) -> p q t", q=4),
                                         func=AF.Copy), [p], [mT_])
            pj = []
            for cg in range(4):
                p = PS()
                pj.append(p)
                for kc in range(16):
                    MM(p[:, :], mT_[:, kc, :], wo[:, kc, cg * 512:(cg + 1) * 512], [mT_, wo], [p], start=(kc == 0), stop=(kc == 15))
                A(lambda e: e.activation(out=junk[:, :], in_=p[:, :], func=AF.Square, accum_out=s_[:, cg:cg + 1]), [p], [junk, s_])
            V(lambda e: e.tensor_reduce(out=s_[:, 4:5], in_=s_[:, 0:4], axis=AX.X, op=ALU.add), [s_], [s_])
            rsqrt(s_[:, 4:5], s_[:, 4:5], 1.0, float(EPS * D), [s_])
            for cg in range(4):
                cs = slice(cg * 512, (cg + 1) * 512)
                V(lambda e: e.scalar_tensor_tensor(out=o_[:, cs], in0=pj[cg][:, :], scalar=s_[:, 4:5], in1=gpg[:, cs],
                                                   op0=ALU.mult, op1=ALU.mult), [pj[cg], s_, gpg], [o_])
            G(lambda e: e.tensor_tensor(out=o_[:, :], in0=o_[:, :], in1=x_[:, :], op=ALU.add), [o_, x_], [o_])
            S.dma("pool", lambda e: e.indirect_dma_start(
                out=x_dst, out_offset=bass.IndirectOffsetOnAxis(ap=idxt[:, l * NT + i:l * NT + i + 1], axis=0),
                in_=o_[:, :], in_offset=None), [o_, idxt], [bxd])
        ph7.close()
        if stop == "P7":
            S.barrier()
            return nc

    try:
        for l in range(DEPTH):
            r_ = layer(l)
            if r_ is not None:
                break
    except _Cut:
        pass
    S.barrier()
    return nc


_PROG = {}


def kernel(x_prompt, x_sample, state_rwkv, state_delta, c, c_ctx, w_mod, b_mod, g_pre, g_post,
           w_in, w0, w_up, a0, a_up, k_k, k_a, r_k, gn_w, gn_b, conv_w, a_log, dt_bias,
           o_norm_w, w_pa, w_pb, w_o):
    f32 = np.float32
    if "nc" not in _PROG:
        _PROG["nc"] = build_program()
    nc = _PROG["nc"]
    arr = lambda z: np.ascontiguousarray(np.asarray(z, dtype=f32))
    shared = {"w_mod": arr(w_mod), "b_mod": arr(b_mod), "g_pre": arr(g_pre), "g_post": arr(g_post), "w_in": arr(w_in),
              "w0": arr(w0), "w_up": arr(w_up), "a0": arr(a0), "a_up": arr(a_up), "k_k": arr(k_k), "k_a": arr(k_a),
              "r_k": arr(r_k), "gn_w": arr(gn_w), "gn_b": arr(gn_b), "conv_w": arr(conv_w),
              "a_log": arr(a_log).reshape(DEPTH, 16), "dt_bias": arr(dt_bias).reshape(DEPTH, 16),
              "o_norm_w": arr(o_norm_w), "w_pa": arr(w_pa), "w_pb": arr(w_pb), "w_o": arr(w_o)}
    x_prompt = arr(x_prompt); x_sample = arr(x_sample)
    state_rwkv = arr(state_rwkv); state_delta = arr(state_delta)
    c = arr(c); c_ctx = arr(c_ctx)
    tpos = np.arange(T)
    perm = (tpos % 32) * 64 + tpos // 32
    ii = np.arange(128)
    lmask = np.zeros((7, 3, 128, 128), f32)
    for lv in range(7):
        bsz = 1 << lv
        pb = ii[:, None] // bsz
        fb = ii[None, :] // bsz
        la = ((pb % 2 == 1) & (fb == pb - 1)).astype(f32)
        lmask[lv, 0] = la
        lmask[lv, 1] = la.T
        lmask[lv, 2] = la
    lmask = np.ascontiguousarray(lmask.transpose(2, 0, 1, 3)).reshape(128, 7, 384)
    in_maps = []
    for core in range(8):
        m = dict(shared)
        idx = np.zeros((DEPTH, T), np.int32)
        cmask = np.ones((128, 2 * NT), f32)
        if core < 4:
            m["x_in"] = x_sample[core]
            cv = c[core]
            m["flag"] = np.ones((128, 1), f32)
            m["init_r"] = np.ascontiguousarray(state_rwkv[core].transpose(0, 1, 4, 2, 3)).reshape(DEPTH, 2, 64, 1024)
            m["init_d"] = np.ascontiguousarray(state_delta[core].transpose(0, 1, 3, 2, 4)).reshape(DEPTH, 2, 128, 1024)
            for l in range(DEPTH):
                idx[l] = perm if l % 2 == 1 else tpos
            cmask[0, 0] = 0.0
            cmask[127, 2 * (NT - 1) + 1] = 0.0
        else:
            j = core - 4
            xs = np.zeros((T, D), f32)
            xs[:1024] = x_prompt[4 * j:4 * j + 4].reshape(1024, D)
            xs[1024:] = xs[:1024]
            m["x_in"] = xs
            cv = c_ctx
            m["flag"] = np.zeros((128, 1), f32)
            m["init_r"] = np.zeros((DEPTH, 2, 64, 1024), f32)
            m["init_d"] = np.zeros((DEPTH, 2, 128, 1024), f32)
            for l in range(DEPTH):
                idx[l] = tpos
            for i in range(NT):
                if i % 2 == 0:
                    cmask[0, 2 * i] = 0.0
                else:
                    cmask[127, 2 * i + 1] = 0.0
        m["cvec"] = np.ascontiguousarray(cv.reshape(16, 128).T)
        m["idx"] = np.ascontiguousarray(idx.reshape(DEPTH, NT, 128).transpose(2, 0, 1).reshape(128, DEPTH * NT))
        m["cmask"] = cmask
        m["lmask"] = lmask
        in_maps.append(m)
    res = run_bass_kernel_spmd(nc, in_maps, core_ids=list(range(8)))
    R = res.results
    y_sample = np.stack([R[b]["y_out"] for b in range(4)], 0).astype(f32)
    y_prompt = np.zeros((16, 256, D), f32)
    new_r = np.zeros((16, DEPTH, 2, 16, 64, 64), f32)
    new_d = np.zeros((16, DEPTH, 2, 8, 128, 128), f32)
    for j in range(4):
        r = R[4 + j]
        y_prompt[4 * j:4 * j + 4] = r["y_out"][:1024].reshape(4, 256, D)
        fr = r["fin_r"].reshape(DEPTH, 8, 2, 64, 16, 64)
        fd = r["fin_d"].reshape(DEPTH, 8, 2, 128, 8, 128)
        for s in range(4):
            new_r[4 * j + s] = fr[:, s].transpose(0, 1, 3, 4, 2)
            new_d[4 * j + s] = fd[:, s].transpose(0, 1, 3, 2, 4)
    return (y_prompt, y_sample, new_r, new_d)
```

```python
import numpy as np
import concourse.bass as bass
import concourse.mybir as mybir
from concourse.bass_utils import run_bass_kernel_spmd

F32 = mybir.dt.float32
BF16 = mybir.dt.bfloat16
I32 = mybir.dt.int32
ALU = mybir.AluOpType
AF = mybir.ActivationFunctionType
AX = mybir.AxisListType

D = 2048
T = 2048
NT = 16
DEPTH = 4
NIN = 12704
AW = 1024
DECAY_SCALE = 0.606531
GN_EPS = 64e-5
EPS = 1e-6
C_R, C_K, C_V, C_ZA = 0, 1024, 2048, 3072
C_LO = 4096
C_QKV = 4480
C_ZB = 7552
C_BETA, C_ALPHA = 8576, 8592
C_GA, C_GB = 8608, 10656


class Buf:
    __slots__ = ("name", "w", "r", "excl")

    def __init__(self, name, excl=False):
        self.name = name
        self.w = None
        self.r = []
        self.excl = excl


class Tl:
    __slots__ = ("t", "b")

    def __init__(self, t, b):
        self.t = t
        self.b = b

    def __getitem__(self, k):
        return self.t[k]


class Sched:
    LIMIT = 30000

    def __init__(self, nc, n_dma_sems=10):
        self.nc = nc
        self.eng = {"pe": nc.tensor, "act": nc.scalar, "dve": nc.vector, "pool": nc.gpsimd, "sp": nc.sync}
        self.nsem = 0
        self.sem = {k: self._newsem(k) for k in self.eng}
        self.cnt = {k: 0 for k in self.eng}
        self.seen = {k: {} for k in self.eng}
        self.dsems = {}
        self.n_dma_sems = n_dma_sems
        self.all_dma = []

    def _newsem(self, k):
        self.nsem += 1
        return self.nc.alloc_semaphore("s_%s_%d" % (k, self.nsem))

    def _wait(self, e, tok):
        sem, val = tok
        sid = id(sem)
        if self.seen[e].get(sid, 0) >= val:
            return
        self.seen[e][sid] = val
        self.eng[e].wait_ge(sem, val)

    def _deps(self, e, reads, writes, pe_skip):
        for b in reads:
            if b.w is not None and not (pe_skip and b.w[2]):
                self._wait(e, b.w[:2])
            if b.excl:
                for t in b.r:
                    self._wait(e, t[:2])
        for b in writes:
            if b.w is not None and not (pe_skip and b.w[2]):
                self._wait(e, b.w[:2])
            for t in b.r:
                if not (pe_skip and t[2]):
                    self._wait(e, t[:2])

    def op(self, e, fn, reads=(), writes=()):
        reads = [x.b if isinstance(x, Tl) else x for x in reads]
        writes = [x.b if isinstance(x, Tl) else x for x in writes]
        self._deps(e, reads, writes, e == "pe")
        if self.cnt[e] >= self.LIMIT:
            self.sem[e] = self._newsem(e)
            self.cnt[e] = 0
        ins = fn(self.eng[e])
        self.cnt[e] += 1
        tok = (self.sem[e], self.cnt[e], e == "pe")
        ins.then_inc(self.sem[e], 1)
        for b in reads:
            b.r.append(tok)
        for b in writes:
            b.w = tok
            b.r = []
        return tok

    def dma(self, e, fn, reads=(), writes=()):
        reads = [x.b if isinstance(x, Tl) else x for x in reads]
        writes = [x.b if isinstance(x, Tl) else x for x in writes]
        if e not in self.dsems:
            self.dsems[e] = [[self.nc.alloc_semaphore("d_%s_%d" % (e, i)), 0] for i in range(self.n_dma_sems)]
            self.dsems[e + "_i"] = 0
            self.all_dma.extend(self.dsems[e])
        i = self.dsems[e + "_i"]
        self.dsems[e + "_i"] = (i + 1) % self.n_dma_sems
        slot = self.dsems[e][i]
        self._deps(e, reads, writes, False)
        if slot[1] > 0:
            self._wait(e, (slot[0], slot[1]))
        ins = fn(self.eng[e])
        slot[1] += 16
        tok = (slot[0], slot[1], False)
        ins.then_inc(slot[0], 16)
        for b in reads:
            b.r.append(tok)
        for b in writes:
            b.w = tok
            b.r = []
        return tok

    def barrier(self):
        for e in self.eng:
            for e2 in self.eng:
                if self.cnt[e2] > 0:
                    self._wait(e, (self.sem[e2], self.cnt[e2]))
            for s in self.all_dma:
                if s[1] > 0:
                    self._wait(e, (s[0], s[1]))


class Ctx:
    pass


class _Cut(Exception):
    pass


def build_program(stop=None):
    nc = bass.Bass("TRN2", target_bir_lowering=False)
    S = Sched(nc)
    g = Ctx()

    def cut(tag):
        if stop == tag:
            raise _Cut()

    def din(name, shape, dt=F32):
        return nc.dram_tensor(name, list(shape), dt, kind="ExternalInput").ap()

    def dout(name, shape):
        return nc.dram_tensor(name, list(shape), F32, kind="ExternalOutput").ap()

    import os as _os2
    _dbg_out = bool(_os2.environ.get("KDBG_OUT"))

    def dscr(name, shape):
        return nc.dram_tensor(name, list(shape), F32, kind="ExternalOutput" if _dbg_out else "Internal").ap()

    x_in = din("x_in", [T, D])
    cvec = din("cvec", [128, 16])
    flag_in = din("flag", [128, 1])
    idx_in = din("idx", [128, DEPTH * NT], I32)
    cmask_in = din("cmask", [128, 2 * NT])
    lmask_in = din("lmask", [128, 7, 384])
    init_r = din("init_r", [DEPTH, 2, 64, 1024])
    init_d = din("init_d", [DEPTH, 2, 128, 1024])
    w_mod = din("w_mod", [DEPTH, D, 3 * D])
    b_mod = din("b_mod", [DEPTH, 3 * D])
    g_pre = din("g_pre", [DEPTH, D])
    g_post = din("g_post", [DEPTH, D])
    w_in = din("w_in", [DEPTH, D, NIN])
    w0 = din("w0", [DEPTH, 2, AW])
    w_up = din("w_up", [DEPTH, 2, 96, AW])
    a0 = din("a0", [DEPTH, 2, AW])
    a_up = din("a_up", [DEPTH, 2, 96, AW])
    k_k = din("k_k", [DEPTH, AW])
    k_a = din("k_a", [DEPTH, AW])
    r_k = din("r_k", [DEPTH, AW])
    gn_w = din("gn_w", [DEPTH, AW])
    gn_b = din("gn_b", [DEPTH, AW])
    conv_w = din("conv_w", [DEPTH, 3, 3072])
    a_log = din("a_log", [DEPTH, 16])
    dt_bias = din("dt_bias", [DEPTH, 16])
    o_norm_w = din("o_norm_w", [DEPTH, 128])
    w_pa = din("w_pa", [DEPTH, AW, D])
    w_pb = din("w_pb", [DEPTH, AW, D])
    w_o = din("w_o", [DEPTH, D, D])

    y_out = dout("y_out", [T, D])
    fin_r = dout("fin_r", [DEPTH, 8, 2, 64, 1024])
    fin_d = dout("fin_d", [DEPTH, 8, 2, 128, 1024])

    XS = [dscr("xs0", [T, D]), dscr("xs1", [T, D])]
    P = dscr("proj", [T + 2, NIN])
    GPG = dscr("gpg", [128, D])
    SR = {"GT": dscr("sr_gt", [NT, 2, 64, 1024]), "H": dscr("sr_h", [NT, 2, 64, 1024]),
          "RT": dscr("sr_rt", [NT, 2, 64, 2048]), "Y0": dscr("sr_y0", [NT, 2, 128, 1024])}
    SD = {"GT": dscr("sd_gt", [NT, 2, 128, 1024]), "H": dscr("sd_h", [NT, 2, 128, 1024]),
          "RT": dscr("sd_rt", [NT, 2, 128, 1024]), "Y0": dscr("sd_y0", [NT, 2, 128, 1024])}
    YA = dscr("ya", [2, T, 1024])
    YB = dscr("yb", [2, T, 1024])
    BON = dscr("bon", [T, 32])
    MG = dscr("mg", [T, D])

    names = [0]

    def sb(shape, dt=F32, name=None):
        names[0] += 1
        nm = "%s_%d" % (name or "t", names[0])
        return Tl(nc.alloc_sbuf_tensor(nm, list(shape), dt), Buf(nm))

    ident = sb([128, 128], name="ident")
    ones = sb([128, 128], name="ones")
    mU, mUi, mL, mLi = (sb([128, 128], name="m") for _ in range(4))
    cm = [sb([128, 128], name="cm") for _ in range(2)]
    A1 = [sb([128, 128], name="a1") for _ in range(2)]
    A2 = [sb([128, 128], name="a2") for _ in range(2)]
    A6 = [sb([128, 128], name="a6") for _ in range(2)]
    mask4 = [sb([128, 512], name="mask4") for _ in range(2)]
    flagc = sb([128, 1], name="flag")
    idxt = sb([128, DEPTH * NT], I32, name="idx")
    cmask = sb([128, 2 * NT], name="cmask")
    zrow = sb([1, 512], name="zrow")
    LM = sb([128, 7, 384], name="LM")
    II = sb([128, 256], name="II")
    identb = sb([128, 128], BF16, name="identb")

    psb = [Tl(nc.alloc_psum_tensor("psb%d" % i, [128, 512], F32), Buf("psb%d" % i, excl=True)) for i in range(8)]
    pctr = [0]

    def PS():
        pctr[0] += 1
        return psb[pctr[0] % 8]

    def sel(t, pat, op, base, cmul):
        S.op("pool", lambda e: e.memset(t[:, :], 1.0), writes=[t])
        S.op("pool", lambda e: e.affine_select(out=t[:, :], in_=t[:, :], pattern=[[pat, 128]], compare_op=op,
                                               fill=0.0, base=base, channel_multiplier=cmul), reads=[t], writes=[t])

    sel(ident, -1, ALU.is_equal, 0, 1)
    S.op("pool", lambda e: e.tensor_copy(out=II[:, 0:128], in_=ident[:, :]), reads=[ident], writes=[II])
    S.op("pool", lambda e: e.tensor_copy(out=II[:, 128:256], in_=ident[:, :]), reads=[ident], writes=[II])
    S.op("pool", lambda e: e.tensor_copy(out=identb[:, :], in_=ident[:, :]), reads=[ident], writes=[identb])
    S.op("pool", lambda e: e.memset(ones[:, :], 1.0), writes=[ones])
    sel(mU, 1, ALU.is_gt, 0, -1)
    sel(mUi, 1, ALU.is_ge, 0, -1)
    sel(mL, -1, ALU.is_gt, 0, 1)
    sel(mLi, -1, ALU.is_ge, 0, 1)
    sel(cm[0], 0, ALU.is_ge, 64, -1)
    sel(cm[1], 0, ALU.is_ge, -64, 1)
    CI = [mUi, mLi]
    CS = [mU, mL]
    MN = [mL, mU]
    for d in range(2):
        S.op("dve", lambda e: e.tensor_tensor(out=A1[d][:, :], in0=CI[d][:, :], in1=cm[d][:, :], op=ALU.subtract),
             reads=[CI[d], cm[d]], writes=[A1[d]])
        S.op("dve", lambda e: e.tensor_tensor(out=A2[d][:, :], in0=CS[d][:, :], in1=cm[d][:, :], op=ALU.subtract),
             reads=[CS[d], cm[d]], writes=[A2[d]])
        S.op("dve", lambda e: e.tensor_tensor(out=A6[d][:, :], in0=ones[:, :], in1=CI[d][:, :], op=ALU.subtract),
             reads=[ones, CI[d]], writes=[A6[d]])
        for q, m in enumerate((CS[d], CI[d], CS[d], CI[d])):
            S.op("pool", lambda e: e.tensor_copy(out=mask4[d][:, q * 128:(q + 1) * 128], in_=m[:, :]),
                 reads=[m], writes=[mask4[d]])
    NEGT_S = [sb([128, 128], name="negts") for _ in range(2)]
    NEGT_I = [sb([128, 128], name="negti") for _ in range(2)]
    NEG_S = [sb([128, 128], name="negs") for _ in range(2)]
    for d in range(2):
        for dst, src in ((NEGT_S[d], CS[d]), (NEGT_I[d], CI[d]), (NEG_S[d], MN[d])):
            S.op("dve", lambda e: e.tensor_scalar(out=dst[:, :], in0=src[:, :], scalar1=-1.0, scalar2=1.0e5, op0=ALU.add,
                                                  op1=ALU.mult), reads=[src], writes=[dst])
    S.dma("sp", lambda e: e.dma_start(out=flagc[:, :], in_=flag_in), writes=[flagc])
    S.dma("sp", lambda e: e.dma_start(out=idxt[:, :], in_=idx_in), writes=[idxt])
    S.dma("sp", lambda e: e.dma_start(out=cmask[:, :], in_=cmask_in), writes=[cmask])
    S.dma("sp", lambda e: e.dma_start(out=LM[:, :, :], in_=lmask_in), writes=[LM])
    S.op("pool", lambda e: e.memset(zrow[:, :], 0.0), writes=[zrow])
    bP = Buf("P")
    for r0 in (0, T + 1):
        for c0 in range(0, NIN, 512):
            cw0 = min(512, NIN - c0)
            S.dma("sp", lambda e: e.dma_start(out=P[r0:r0 + 1, c0:c0 + cw0], in_=zrow[:, 0:cw0]), reads=[zrow], writes=[bP])

    def V(fn, r, w):
        S.op("dve", fn, r, w)

    def A(fn, r, w):
        S.op("act", fn, r, w)

    def G(fn, r, w):
        S.op("pool", fn, r, w)

    def MM(out, lhsT, rhs, r, w, start=True, stop=True):
        S.op("pe", lambda e: e.matmul(out, lhsT=lhsT, rhs=rhs, start=start, stop=stop), r, w)

    def TR(out, in_, r, w):
        S.op("pe", lambda e: e.transpose(out, in_, ident[:, :]), list(r) + [ident], w)

    def LD(out, in_, w, r=()):
        S.dma("sp", lambda e: e.dma_start(out=out, in_=in_), r, w)

    def ST(out, in_, r, w=()):
        S.dma("sp", lambda e: e.dma_start(out=out, in_=in_), r, w)

    def bc3(ap2, n):
        return ap2.unsqueeze(2).broadcast_to([ap2.shape[0], ap2.shape[1], n])

    def rsqrt(out_t, in_ap, scale, bias, r):
        A(lambda e: e.activation(out=out_t, in_=in_ap, func=AF.Ln, bias=bias, scale=scale), r, r)
        A(lambda e: e.activation(out=out_t, in_=out_t, func=AF.Exp, scale=-0.5), r, r)

    class Phase:
        def __init__(self):
            self.guards = []

        def sb(self, shape, dt=F32, name="p"):
            names[0] += 1
            nm = "%s_%d" % (name, names[0])
            gd = nc.sbuf_tensor(nm, list(shape), dt)
            t = gd.__enter__()
            self.guards.append(gd)
            return Tl(t, Buf(nm))

        def close(self):
            S.barrier()
            for gd in reversed(self.guards):
                gd.__exit__(None, None, None)

    def drive(gens, width=4):
        act = []
        gens = list(gens)
        while gens or act:
            while gens and len(act) < width:
                act.append(gens.pop(0))
            nxt = []
            for gen in act:
                try:
                    next(gen)
                    nxt.append(gen)
                except StopIteration:
                    pass
            act = nxt

    import os as _os3
    NU = int(_os3.environ.get("KDBG_NU", 4))

    def unit_bufs(ph, dec=False):
        return [(ph.sb([128, 512], BF16, name="chT"), ph.sb([128, 512], name="AT"),
                 ph.sb([128, 256], name="NQ"),
                 [ph.sb([128, 256], name="B") for _ in range(2)], ph.sb([128, 128], name="dg"),
                 ph.sb([128, 384], name="DM") if dec else None, ph.sb([128, 384], name="DR") if dec else None,
                 [ph.sb([128, 256], BF16, name="TW") for _ in range(2)], ph.sb([128, 256], BF16, name="YX"),
                 (ph.sb([128, 256], BF16, name="NQb"), ph.sb([128, 256], name="TWf")))
                for _ in range(NU)]

    def unit(bset, d, K, Vd, hcols, vcols, X, outs, h, dec=None):
        chT, AT, NQ0, B, dg, DM, DR, TW, YX, (NQb, TWf) = bset
        NQ = [NQ0]

        def run():
            if dec is not None:
                V(lambda e: e.tensor_scalar(out=DR[:, 0:128], in0=CS[d][:, :], scalar1=dec["g"], scalar2=None, op0=ALU.mult),
                  [CS[d], dec["b"]], [DR])
                V(lambda e: e.tensor_scalar(out=DR[:, 128:256], in0=CI[d][:, :], scalar1=dec["g"], scalar2=None, op0=ALU.mult),
                  [CI[d], dec["b"]], [DR])
                V(lambda e: e.tensor_scalar(out=DR[:, 256:384], in0=DR[:, 128:256], scalar1=-1.0, scalar2=None, op0=ALU.mult),
                  [DR], [DR])
                yield
                p = PS()
                for q, neg in enumerate((NEGT_S[d], NEGT_I[d], NEG_S[d])):
                    MM(p[:, q * 128:(q + 1) * 128], ones[:, :], DR[:, q * 128:(q + 1) * 128], [ones, DR], [p], start=True, stop=False)
                    MM(p[:, q * 128:(q + 1) * 128], ident[:, :], neg[:, :], [ident, neg], [p], start=False, stop=True)
                A(lambda e: e.activation(out=DM[:, 0:256], in_=p[:, 0:256], func=AF.Exp, bias=dec["ncw"], scale=1.0),
                  [p, dec["b"]], [DM])
                A(lambda e: e.activation(out=DM[:, 256:384], in_=p[:, 256:384], func=AF.Exp, bias=dec["cwx"], scale=1.0),
                  [p, dec["b"]], [DM])
                yield
            p = PS()
            for q, key in enumerate(("nm", "rm", "pm", "km")):
                TR(p[0:K, q * 128:(q + 1) * 128], X[key][:, hcols], [X[key]], [p])
            A(lambda e: e.activation(out=chT[0:K, :], in_=p[0:K, :], func=AF.Copy), [p], [chT])
            if h == 0:
                cut("u.1")
            yield
            p = PS()
            MM(p[:, 0:256], chT[0:K, 256:384], chT[0:K, 0:256], [chT], [p])
            MM(p[:, 256:512], chT[0:K, 384:512], chT[0:K, 0:256], [chT], [p])
            if dec is None:
                V(lambda e: e.tensor_tensor(out=AT[:, :], in0=p[:, :], in1=mask4[d][:, :], op=ALU.mult), [p, mask4[d]], [AT])
            else:
                V(lambda e: e.tensor_tensor(out=AT[:, 0:256], in0=p[:, 0:256], in1=DM[:, 0:256], op=ALU.mult), [p, DM], [AT])
                V(lambda e: e.tensor_tensor(out=AT[:, 256:512], in0=p[:, 256:512], in1=DM[:, 0:256], op=ALU.mult), [p, DM], [AT])
            p2 = PS()
            MM(p2[:, 0:128], chT[0:K, 0:128], chT[0:K, 256:384], [chT], [p2])
            if dec is None:
                V(lambda e: e.tensor_tensor(out=NQ[0][:, 0:128], in0=p2[:, 0:128], in1=MN[d][:, :], op=ALU.mult),
                  [p2, MN[d]], [NQ[0]])
            else:
                V(lambda e: e.tensor_tensor(out=NQ[0][:, 0:128], in0=p2[:, 0:128], in1=DM[:, 256:384], op=ALU.mult),
                  [p2, DM], [NQ[0]])
            G(lambda e: e.tensor_copy(out=NQ[0][:, 128:256], in_=AT[:, 0:128]), [AT], [NQ[0]])
            G(lambda e: e.tensor_copy(out=B[0][:, 0:K], in_=X["ntr"][:, hcols]), [X["ntr"]], [B[0]])
            if h == 0:
                cut("u.2")
            yield
            p = PS()
            MM(p[:, 0:Vd], AT[:, 256:384], X["v"][:, vcols], [AT, X["v"]], [p])
            A(lambda e: e.activation(out=B[0][:, K:K + Vd], in_=p[:, 0:Vd], func=AF.Copy), [p], [B[0]])
            if h == 0:
                cut("u.3")
            yield
            moff = 0 if d == 0 else 128
            G(lambda e: e.tensor_copy(out=NQb[:, :], in_=NQ0[:, :]), [NQ0], [NQb])
            G(lambda e: e.tensor_tensor(out=YX[:, :], in0=NQ0[:, :], in1=LM[:, 0, moff:moff + 256], op=ALU.mult), [NQ0, LM], [YX])
            G(lambda e: e.tensor_tensor(out=TW[1][:, :], in0=YX[:, :], in1=II[:, :], op=ALU.add), [YX, II], [TW[1]])
            cur = 1
            yield
            for lv in range(1, 7):
                twc = TW[cur]
                twn = TW[1 - cur] if lv < 6 else TWf
                p = PS()
                MM(p[:, 0:128], NQb[:, 128:256], twc[:, 0:128], [NQb, twc], [p])
                MM(p[:, 128:256], NQb[:, 0:128], twc[:, 128:256], [NQb, twc], [p])
                V(lambda e: e.tensor_tensor(out=YX[:, :], in0=p[:, 0:256], in1=LM[:, lv, moff:moff + 256], op=ALU.mult),
                  [p, LM], [YX])
                yield
                p2 = PS()
                MM(p2[:, 0:128], twc[:, 128:256], YX[:, 0:128], [twc, YX], [p2])
                MM(p2[:, 128:256], twc[:, 0:128], YX[:, 128:256], [twc, YX], [p2])
                V(lambda e: e.tensor_tensor(out=twn[:, :], in0=p2[:, 0:256], in1=twc[:, :], op=ALU.add), [p2, twc], [twn])
                cur = 1 - cur
                yield
            p = PS()
            MM(p[:, 0:K + Vd], TWf[:, 128:256], B[0][:, 0:K + Vd], [TWf, B[0]], [p])
            A(lambda e: e.activation(out=B[1][:, 0:K + Vd], in_=p[:, 0:K + Vd], func=AF.Copy), [p], [B[1]])
            cur = 1
            yield
            Bf = B[cur]
            if h == 0:
                cut("u.4")
            G(lambda e: e.tensor_tensor(out=dg[0:K, 0:K], in0=ident[0:K, 0:K], in1=X["gam"][0:K, hcols], op=ALU.mult),
              [ident, X["gam"]], [dg])
            if h == 0:
                cut("u.41")
            p = PS()
            MM(p[0:K, 0:K], Bf[:, 0:K], X["ph"][:, hcols], [Bf, X["ph"]], [p])
            if h == 0:
                cut("u.415")
            MM(p[0:K, K:K + Vd], X["ph"][:, hcols], Bf[:, K:K + Vd], [Bf, X["ph"]], [p], start=True, stop=False)
            MM(p[0:K, K:K + Vd], X["kh"][:, hcols], X["v"][:, vcols], [X["kh"], X["v"]], [p], start=False, stop=True)
            if h == 0:
                cut("u.42")
            V(lambda e: e.tensor_tensor(out=outs["GT"][0:K, h * K:(h + 1) * K], in0=p[0:K, 0:K], in1=dg[0:K, 0:K], op=ALU.add),
              [p, dg], [outs["GT"]])
            A(lambda e: e.activation(out=outs["H"][0:K, h * Vd:(h + 1) * Vd], in_=p[0:K, K:K + Vd], func=AF.Copy),
              [p], [outs["H"]])
            if h == 0:
                cut("u.43")
            yield
            p = PS()
            MM(p[0:K, 0:128], Bf[:, 0:K], AT[:, 128:256], [Bf, AT], [p], start=True, stop=False)
            MM(p[0:K, 0:128], X["rtr"][:, hcols], ident[:, :], [X["rtr"], ident], [p], start=False, stop=True)
            MM(p[:, 128:128 + Vd], AT[:, 128:256], Bf[:, K:K + Vd], [Bf, AT], [p], start=True, stop=False)
            MM(p[:, 128:128 + Vd], AT[:, 384:512], X["v"][:, vcols], [AT, X["v"]], [p], start=False, stop=True)
            if h == 0:
                cut("u.44")
            A(lambda e: e.activation(out=outs["RT"][0:K, h * 128:(h + 1) * 128], in_=p[0:K, 0:128], func=AF.Copy),
              [p], [outs["RT"]])
            V(lambda e: e.tensor_copy(out=outs["Y0"][:, h * Vd:(h + 1) * Vd], in_=p[:, 128:128 + Vd]), [p], [outs["Y0"]])
            if h == 0:
                cut("u.5")
            yield
        return run()

    def summaries(ph, d, K, Vd, H, X, SUM, tile, units_bufs):
        outs = units_bufs["outs"]
        decs = units_bufs.get("decs")
        gens = [unit(units_bufs["sets"][h % NU], d, K, Vd, slice(h * K, (h + 1) * K), slice(h * Vd, (h + 1) * Vd), X, outs, h,
                     None if decs is None else decs[h]) for h in range(H)]
        drive(gens, NU)
        bsum = units_bufs["bsum"]
        ST(SUM["GT"][tile, d], outs["GT"][0:K, :], [outs["GT"]], [bsum])
        ST(SUM["H"][tile, d], outs["H"][0:K, :], [outs["H"]], [bsum])
        ST(SUM["RT"][tile, d], outs["RT"][0:K, 0:H * 128], [outs["RT"]], [bsum])
        ST(SUM["Y0"][tile, d], outs["Y0"][:, :], [outs["Y0"]], [bsum])

    if stop == "init":
        S.barrier()
        return nc
    bX = [Buf("xs0"), Buf("xs1"), Buf("xin"), Buf("yout")]
    bGPG = Buf("gpg")
    bSR = Buf("sr")
    bSD = Buf("sd")
    bYA = Buf("ya")
    bYB = Buf("yb")
    bBON = Buf("bon")
    bMG = Buf("mg")
    bFIN = Buf("fin")

    def layer(l):
        x_src, bxs = (x_in, bX[2]) if l == 0 else (XS[(l - 1) % 2], bX[(l - 1) % 2])
        x_dst, bxd = (y_out, bX[3]) if l == DEPTH - 1 else (XS[l % 2], bX[l % 2])

        def gather(dst_tile, i, src=x_src, bsrc=bxs):
            S.dma("pool", lambda e: e.indirect_dma_start(
                out=dst_tile[:, :], out_offset=None, in_=src,
                in_offset=bass.IndirectOffsetOnAxis(ap=idxt[:, l * NT + i:l * NT + i + 1], axis=0)),
                [idxt, bsrc], [dst_tile])

        ph = Phase()
        hT = ph.sb([128, 16, T], BF16, name="hT")
        ph1 = Phase()
        cT = ph1.sb([128, 16], name="cT")
        scB = ph1.sb([128, 16, 128], name="scB")
        modt = ph1.sb([128, 3 * D], name="mod")
        bmod = [ph1.sb([128, 256], name="bmod") for _ in range(2)]
        gpre = ph1.sb([128, D], name="gpre")
        wst = [ph1.sb([128, 16, 256], name="wst") for _ in range(2)]
        LD(cT[:, :], cvec, [cT])
        A(lambda e: e.activation(out=cT[:, :], in_=cT[:, :], func=AF.Silu), [cT], [cT])
        V(lambda e: e.tensor_copy(out=scB[:, :, :], in_=bc3(cT[:, :], 128)), [cT], [scB])
        LD(gpre[:, :], g_pre[l].partition_broadcast(128), [gpre])
        wm = w_mod[l].rearrange("(kc p) n -> p kc n", p=128)
        for cg in range(24):
            w = wst[cg % 2]
            bm_ = bmod[cg % 2]
            LD(w[:, :, :], wm[:, :, cg * 256:(cg + 1) * 256], [w])
            LD(bm_[:, :], b_mod[l, cg * 256:(cg + 1) * 256].partition_broadcast(128), [bm_])
            p = PS()
            for kc in range(16):
                MM(p[:, 0:256], scB[:, kc, :], w[:, kc, :], [scB, w], [p], start=(kc == 0), stop=(kc == 15))
            V(lambda e: e.tensor_tensor(out=modt[:, cg * 256:(cg + 1) * 256], in0=p[:, 0:256],
                                        in1=bm_[:, :], op=ALU.add), [p, bm_], [modt])
        V(lambda e: e.scalar_tensor_tensor(out=modt[:, D:2 * D], in0=modt[:, D:2 * D], scalar=1.0, in1=gpre[:, :],
                                           op0=ALU.add, op1=ALU.mult), [modt, gpre], [modt])
        V(lambda e: e.tensor_scalar(out=modt[:, D:2 * D], in0=modt[:, D:2 * D], scalar1=float(D ** 0.5), scalar2=None,
                                    op0=ALU.mult), [modt], [modt])
        LD(gpre[:, :], g_post[l].partition_broadcast(128), [gpre])
        V(lambda e: e.scalar_tensor_tensor(out=modt[:, 2 * D:3 * D], in0=modt[:, 2 * D:3 * D], scalar=float(D ** 0.5),
                                           in1=gpre[:, :], op0=ALU.mult, op1=ALU.mult), [modt, gpre], [modt])
        ST(GPG, modt[:, 2 * D:3 * D], [modt], [bGPG])
        xt = [ph1.sb([128, D], name="xt") for _ in range(2)]
        hh = [ph1.sb([128, D], name="hh") for _ in range(2)]
        ssq = [ph1.sb([128, 1], name="ssq") for _ in range(2)]
        for i in range(NT):
            x_t, h_t, s_t = xt[i % 2], hh[i % 2], ssq[i % 2]
            gather(x_t, i)
            A(lambda e: e.activation(out=h_t[:, :], in_=x_t[:, :], func=AF.Square, accum_out=s_t[:, :]), [x_t], [h_t, s_t])
            rsqrt(s_t[:, :], s_t[:, :], 1.0, float(EPS * D), [s_t])
            V(lambda e: e.scalar_tensor_tensor(out=h_t[:, :], in0=x_t[:, :], scalar=s_t[:, 0:1], in1=modt[:, D:2 * D],
                                               op0=ALU.mult, op1=ALU.mult), [x_t, s_t, modt], [h_t])
            G(lambda e: e.tensor_tensor(out=h_t[:, :], in0=h_t[:, :], in1=modt[:, 0:D], op=ALU.add), [h_t, modt], [h_t])
            for c4 in range(4):
                p = PS()
                for q in range(4):
                    kc = c4 * 4 + q
                    TR(p[:, q * 128:(q + 1) * 128], h_t[:, kc * 128:(kc + 1) * 128], [h_t], [p])
                A(lambda e: e.activation(out=hT[:, c4 * 4:(c4 + 1) * 4, i * 128:(i + 1) * 128],
                                         in_=p[:, :].rearrange("p (q t) -> p q t", q=4), func=AF.Copy), [p], [hT])
        ph1.close()
        if stop == "P1":
            S.barrier()
            return nc

        ph2 = Phase()
        wst = [ph2.sb([128, 16, 512], name="wst") for _ in range(2)]
        wbf = [ph2.sb([128, 16, 512], BF16, name="wbf") for _ in range(2)]
        ost = [ph2.sb([128, 512], name="ost") for _ in range(4)]
        wi = w_in[l].rearrange("(kc p) n -> p kc n", p=128)
        oc = 0
        ncg = (NIN + 511) // 512

        def fetch(cg):
            c0 = min(cg * 512, NIN - 512)
            w, wb = wst[cg % 2], wbf[cg % 2]
            for k4 in range(4):
                LD(w[:, k4 * 4:(k4 + 1) * 4, :], wi[:, k4 * 4:(k4 + 1) * 4, c0:c0 + 512], [w])
            for k4 in range(4):
                G(lambda e: e.tensor_copy(out=wb[:, k4 * 4:(k4 + 1) * 4, :], in_=w[:, k4 * 4:(k4 + 1) * 4, :]), [w], [wb])
        fetch(0)
        for cg in range(ncg):
            if cg + 1 < ncg:
                fetch(cg + 1)
            c0 = min(cg * 512, NIN - 512)
            cw = 512
            wb = wbf[cg % 2]
            for i in range(NT):
                p = PS()
                for kc in range(16):
                    MM(p[:, 0:cw], hT[:, kc, i * 128:(i + 1) * 128], wb[:, kc, 0:cw], [hT, wb], [p],
                       start=(kc == 0), stop=(kc == 15))
                o = ost[oc % 4]
                oc += 1
                A(lambda e: e.activation(out=o[:, 0:cw], in_=p[:, 0:cw], func=AF.Copy), [p], [o])
                ST(P[1 + i * 128:1 + (i + 1) * 128, c0:c0 + cw], o[:, 0:cw], [o], [bP])
        ph2.close()
        ph.close()
        if stop == "P2":
            S.barrier()
            return nc

        ph3 = Phase()
        cst = {}
        for nm_, src in (("k_k", k_k[l]), ("k_a", k_a[l]), ("r_k", r_k[l])):
            cst[nm_] = ph3.sb([128, AW], name=nm_)
            LD(cst[nm_][:, :], src.partition_broadcast(128), [cst[nm_]])
        wup = [ph3.sb([128, AW], name="wup") for _ in range(2)]
        aup = [ph3.sb([128, AW], name="aup") for _ in range(2)]
        for d in range(2):
            LD(wup[d][0:96, :], w_up[l, d], [wup[d]])
            LD(wup[d][96:97, :], w0[l, d:d + 1, :], [wup[d]])
            LD(aup[d][0:96, :], a_up[l, d], [aup[d]])
            LD(aup[d][96:97, :], a0[l, d:d + 1, :], [aup[d]])
        usets = unit_bufs(ph3)
        rkv = ph3.sb([128, 3072], name="rkv")
        lo = ph3.sb([128, 384], name="lo")
        loT = ph3.sb([128, 4, 128], name="loT")
        G(lambda e: e.memset(loT[:, :, :], 1.0), [], [loT])
        kx = ph3.sb([128, AW], name="kx")
        kk = ph3.sb([128, AW], name="kk")
        sm = ph3.sb([128, 64], name="sm")
        bon = ph3.sb([128, 32], name="bon")
        sg = [ph3.sb([128, AW], name="sg") for _ in range(2)]
        ad = [ph3.sb([128, AW], name="ad") for _ in range(2)]
        kd = [ph3.sb([128, AW], name="kd") for _ in range(2)]
        pd = [ph3.sb([128, AW], name="pd") for _ in range(2)]
        tmp = kx
        ex = ad[0]
        X = {k_: ph3.sb([128, AW], name=k_) for k_ in ("nm", "rm", "pm", "km", "ntr", "rtr", "ph", "kh")}
        X["gam"] = ad[1]
        outs = {"GT": ph3.sb([128, 1024], name="oGT"), "H": ph3.sb([128, 1024], name="oH"),
                "RT": ph3.sb([128, 2048], name="oRT"), "Y0": ph3.sb([128, 1024], name="oY0")}
        cut("p3.0")
        for i in range(NT):
            rows = slice(1 + i * 128, 1 + (i + 1) * 128)
            LD(rkv[:, :], P[rows, 0:3072], [rkv], [bP])
            LD(lo[:, :], P[rows, C_LO:C_LO + 384], [lo], [bP])
            cut("p3.1")
            A(lambda e: e.activation(out=lo[:, 0:192], in_=lo[:, 0:192], func=AF.Tanh), [lo], [lo])
            p = PS()
            for q in range(4):
                TR(p[0:96, q * 128:(q + 1) * 128], lo[:, q * 96:(q + 1) * 96], [lo], [p])
            A(lambda e: e.activation(out=loT[0:96, :, :], in_=p[0:96, :].rearrange("p (q t) -> p q t", q=4), func=AF.Copy),
              [p], [loT])
            cut("p3.2")
            for d in range(2):
                for hf in range(2):
                    cs = slice(hf * 512, (hf + 1) * 512)
                    p = PS()
                    MM(p[:, :], loT[0:97, d, :], wup[d][0:97, cs], [loT, wup[d]], [p])
                    A(lambda e: e.activation(out=sg[d][:, cs], in_=p[:, :], func=AF.Sigmoid), [p], [sg[d]])
                    p = PS()
                    MM(p[:, :], loT[0:97, 2 + d, :], aup[d][0:97, cs], [loT, aup[d]], [p])
                    A(lambda e: e.activation(out=ad[d][:, cs], in_=p[:, :], func=AF.Sigmoid), [p], [ad[d]])
            cut("p3.3")
            r_ap, k_ap, v_ap = rkv[:, 0:1024], rkv[:, 1024:2048], rkv[:, 2048:3072]
            V(lambda e: e.tensor_tensor(out=kx[:, :], in0=k_ap, in1=cst["k_k"][:, :], op=ALU.mult), [rkv, cst["k_k"]], [kx])
            A(lambda e: e.activation(out=kk[:, :], in_=kx[:, :], func=AF.Square), [kx], [kk])
            V(lambda e: e.tensor_reduce(out=sm[:, 0:16], in_=kk[:, :].rearrange("p (h k) -> p h k", k=64), axis=AX.X,
                                        op=ALU.add), [kk], [sm])
            rsqrt(sm[:, 0:16], sm[:, 0:16], 1.0, float(EPS), [sm])
            V(lambda e: e.tensor_tensor(out=kk[:, :].rearrange("p (h k) -> p h k", k=64),
                                        in0=kx[:, :].rearrange("p (h k) -> p h k", k=64),
                                        in1=bc3(sm[:, 0:16], 64), op=ALU.mult), [kx, sm], [kk])
            for d in range(2):
                V(lambda e: e.scalar_tensor_tensor(out=tmp[:, :], in0=ad[d][:, :], scalar=-1.0, in1=cst["k_a"][:, :],
                                                   op0=ALU.add, op1=ALU.mult), [ad[d], cst["k_a"]], [tmp])
                V(lambda e: e.scalar_tensor_tensor(out=kd[d][:, :], in0=tmp[:, :], scalar=1.0, in1=k_ap,
                                                   op0=ALU.add, op1=ALU.mult), [tmp, rkv], [kd[d]])
                G(lambda e: e.tensor_tensor(out=pd[d][:, :], in0=kk[:, :], in1=ad[d][:, :], op=ALU.mult), [kk, ad[d]], [pd[d]])
                G(lambda e: e.tensor_tensor(out=tmp[:, :], in0=kd[d][:, :], in1=cst["r_k"][:, :], op=ALU.mult),
                  [kd[d], cst["r_k"]], [tmp])
                V(lambda e: e.tensor_tensor(out=tmp[:, :], in0=tmp[:, :], in1=r_ap, op=ALU.mult), [tmp, rkv], [tmp])
                V(lambda e: e.tensor_reduce(out=bon[:, d * 16:(d + 1) * 16], in_=tmp[:, :].rearrange("p (h k) -> p h k", k=64),
                                            axis=AX.X, op=ALU.add), [tmp], [bon])
            ST(BON[i * 128:(i + 1) * 128, :], bon[:, :], [bon], [bBON])
            cut("p3.5")
            for d in range(2):
                def expo(lhs, scale, fn2):
                    for hf in range(2):
                        cs = slice(hf * 512, (hf + 1) * 512)
                        p = PS()
                        MM(p[:, :], lhs[:, :], sg[d][:, cs], [lhs, sg[d]], [p])
                        A(lambda e: e.activation(out=ex[:, cs], in_=p[:, :], func=AF.Exp, scale=scale), [p], [ex])
                    fn2()
                sc = -DECAY_SCALE
                expo(A1[d], sc, lambda: V(lambda e: e.tensor_tensor(out=X["rm"][:, :], in0=r_ap, in1=ex[:, :], op=ALU.mult),
                                          [rkv, ex], [X["rm"]]))
                expo(A2[d], sc, lambda: V(lambda e: e.scalar_tensor_tensor(out=X["nm"][:, :], in0=kk[:, :], scalar=-1.0,
                                                                           in1=ex[:, :], op0=ALU.mult, op1=ALU.mult),
                                          [kk, ex], [X["nm"]]))

                def f3():
                    V(lambda e: e.tensor_tensor(out=X["pm"][:, :], in0=pd[d][:, :], in1=ex[:, :], op=ALU.mult), [pd[d], ex], [X["pm"]])
                    G(lambda e: e.tensor_tensor(out=X["km"][:, :], in0=kd[d][:, :], in1=ex[:, :], op=ALU.mult), [kd[d], ex], [X["km"]])
                expo(A1[d], -sc, f3)
                expo(CI[d], sc, lambda: V(lambda e: e.tensor_tensor(out=X["rtr"][:, :], in0=r_ap, in1=ex[:, :], op=ALU.mult),
                                          [rkv, ex], [X["rtr"]]))
                expo(CS[d], sc, lambda: V(lambda e: e.scalar_tensor_tensor(out=X["ntr"][:, :], in0=kk[:, :], scalar=-1.0,
                                                                           in1=ex[:, :], op0=ALU.mult, op1=ALU.mult),
                                          [kk, ex], [X["ntr"]]))

                def f6():
                    V(lambda e: e.tensor_tensor(out=X["ph"][:, :], in0=pd[d][:, :], in1=ex[:, :], op=ALU.mult), [pd[d], ex], [X["ph"]])
                    G(lambda e: e.tensor_tensor(out=X["kh"][:, :], in0=kd[d][:, :], in1=ex[:, :], op=ALU.mult), [kd[d], ex], [X["kh"]])
                expo(A6[d], sc, f6)
                expo(ones, sc, lambda: G(lambda e: e.tensor_copy(out=X["gam"][:, :], in_=ex[:, :]), [ex], [X["gam"]]))
                cut("p3.7")
                Xd = dict(X)
                Xd["v"] = Tl(rkv.t[:, 2048:3072], rkv.b)
                summaries(ph3, d, 64, 64, 16, Xd, SR, i, {"outs": outs, "bsum": bSR, "sets": usets})
                cut("p3.8")
        ph3.close()
        if stop == "P3":
            S.barrier()
            return nc

        ph4 = Phase()
        cw_ = [ph4.sb([128, 3072], name="convw") for _ in range(3)]
        for j in range(3):
            LD(cw_[j][:, :], conv_w[l, j].partition_broadcast(128), [cw_[j]])
        usets = unit_bufs(ph4, dec=True)
        alg = ph4.sb([128, 16], name="alg")
        dtb = ph4.sb([128, 16], name="dtb")
        LD(alg[:, :], a_log[l].partition_broadcast(128), [alg])
        LD(dtb[:, :], dt_bias[l].partition_broadcast(128), [dtb])
        A(lambda e: e.activation(out=alg[:, :], in_=alg[:, :], func=AF.Exp), [alg], [alg])
        acc = ph4.sb([128, 3072], name="acc")
        xw = [ph4.sb([128, 3072], name="xw"), acc, ph4.sb([128, 3072], name="xw")]
        ba = ph4.sb([128, 32], name="ba")
        sm = ph4.sb([128, 16 * 12], name="sm4")
        X = {k_: ph4.sb([128, AW], name=k_) for k_ in ("pm", "km", "ntr", "rtr", "ph", "kh", "gam")}
        outs = {"GT": ph4.sb([128, 1024], name="oGT"), "H": ph4.sb([128, 1024], name="oH"),
                "RT": ph4.sb([128, 1024], name="oRT"), "Y0": ph4.sb([128, 1024], name="oY0")}
        for i in range(NT):
            for j in range(3):
                r0 = i * 128 + j
                LD(xw[j][:, :], P[r0:r0 + 128, C_QKV:C_QKV + 3072], [xw[j]], [bP])
            LD(ba[:, :], P[1 + i * 128:1 + (i + 1) * 128, C_BETA:C_BETA + 32], [ba], [bP])
            V(lambda e: e.tensor_tensor(out=acc[:, :], in0=acc[:, :], in1=cw_[1][:, :], op=ALU.mult), [acc, cw_[1]], [acc])
            V(lambda e: e.scalar_tensor_tensor(out=xw[0][:, :], in0=xw[0][:, :], scalar=cmask[:, 2 * i:2 * i + 1], in1=cw_[0][:, :],
                                               op0=ALU.mult, op1=ALU.mult), [xw[0], cmask, cw_[0]], [xw[0]])
            G(lambda e: e.tensor_tensor(out=acc[:, :], in0=acc[:, :], in1=xw[0][:, :], op=ALU.add), [acc, xw[0]], [acc])
            V(lambda e: e.scalar_tensor_tensor(out=xw[2][:, :], in0=xw[2][:, :], scalar=cmask[:, 2 * i + 1:2 * i + 2],
                                               in1=cw_[2][:, :], op0=ALU.mult, op1=ALU.mult), [xw[2], cmask, cw_[2]], [xw[2]])
            G(lambda e: e.tensor_tensor(out=acc[:, :], in0=acc[:, :], in1=xw[2][:, :], op=ALU.add), [acc, xw[2]], [acc])
            tmp = xw[0]
            A(lambda e: e.activation(out=acc[:, :], in_=acc[:, :], func=AF.Silu), [acc], [acc])
            A(lambda e: e.activation(out=tmp[:, 0:2048], in_=acc[:, 0:2048], func=AF.Square), [acc], [tmp])
            V(lambda e: e.tensor_reduce(out=sm[:, 0:16], in_=tmp[:, 0:2048].rearrange("p (h k) -> p h k", k=128), axis=AX.X,
                                        op=ALU.add), [tmp], [sm])
            rsqrt(sm[:, 0:16], sm[:, 0:16], 1.0, float(EPS), [sm])
            V(lambda e: e.tensor_scalar(out=sm[:, 0:8], in0=sm[:, 0:8], scalar1=float(128 ** -0.5), scalar2=None, op0=ALU.mult),
              [sm], [sm])
            V(lambda e: e.tensor_tensor(out=acc[:, 0:2048].rearrange("p (h k) -> p h k", k=128),
                                        in0=acc[:, 0:2048].rearrange("p (h k) -> p h k", k=128),
                                        in1=bc3(sm[:, 0:16], 128), op=ALU.mult), [acc, sm], [acc])
            qn, kn = acc[:, 0:1024], acc[:, 1024:2048]
            A(lambda e: e.activation(out=sm[:, 16:32], in_=ba[:, 0:16], func=AF.Sigmoid), [ba], [sm])
            V(lambda e: e.tensor_tensor(out=sm[:, 32:48], in0=ba[:, 16:32], in1=dtb[:, :], op=ALU.add), [ba, dtb], [sm])
            A(lambda e: e.activation(out=sm[:, 32:48], in_=sm[:, 32:48], func=AF.Exp), [sm], [sm])
            A(lambda e: e.activation(out=sm[:, 32:48], in_=sm[:, 32:48], func=AF.Ln, bias=1.0), [sm], [sm])
            V(lambda e: e.scalar_tensor_tensor(out=sm[:, 32:48], in0=sm[:, 32:48], scalar=-1.0, in1=alg[:, :],
                                               op0=ALU.mult, op1=ALU.mult), [sm, alg], [sm])
            A(lambda e: e.activation(out=sm[:, 48:64], in_=sm[:, 32:48], func=AF.Exp), [sm], [sm])
            V(lambda e: e.scalar_tensor_tensor(out=sm[:, 48:64], in0=sm[:, 48:64], scalar=-1.0, in1=sm[:, 16:32],
                                               op0=ALU.mult, op1=ALU.mult), [sm], [sm])
            for d in range(2):
                gcol = sm[:, 32 + d * 8:32 + (d + 1) * 8]
                p = PS()
                for q, lhs in ((3, CI[d]), (4, CS[d]), (5, A6[d]), (6, ones)):
                    MM(p[:, q * 8:(q + 1) * 8], lhs[:, :], gcol, [lhs, sm], [p])
                A(lambda e: e.activation(out=sm[:, 64 + 24:64 + 56], in_=p[:, 24:56], func=AF.Exp), [p], [sm])
                V(lambda e: e.tensor_scalar(out=sm[:, 176:184], in0=p[:, 24:32], scalar1=-1.0, scalar2=None, op0=ALU.mult),
                  [p], [sm])
                V(lambda e: e.tensor_copy(out=sm[:, 168:176], in_=p[:, 32:40]), [p], [sm])
                E = lambda q: sm[:, 64 + q * 8:64 + (q + 1) * 8]
                bet = sm[:, 16 + d * 8:16 + (d + 1) * 8]
                nb = sm[:, 48 + d * 8:48 + (d + 1) * 8]
                sc_ = lambda q: sm[:, 128 + q * 8:128 + (q + 1) * 8]
                V(lambda e: e.tensor_tensor(out=sc_(2), in0=nb, in1=E(5), op=ALU.mult), [sm], [sm])
                V(lambda e: e.tensor_tensor(out=sc_(3), in0=bet, in1=E(5), op=ALU.mult), [sm], [sm])

                def bm(dst, src_ap, s_ap, eng):
                    eng(lambda e: e.tensor_tensor(out=dst[:, :].rearrange("p (h k) -> p h k", k=128),
                                                  in0=src_ap.rearrange("p (h k) -> p h k", k=128),
                                                  in1=bc3(s_ap, 128), op=ALU.mult), [acc, sm], [dst])
                bm(X["pm"], kn, nb, V)
                bm(X["km"], kn, bet, G)
                bm(X["rtr"], qn, E(3), V)
                bm(X["ntr"], kn, E(4), G)
                bm(X["ph"], kn, sc_(2), V)
                bm(X["kh"], kn, sc_(3), G)
                V(lambda e: e.tensor_copy(out=X["gam"][:, :].rearrange("p (h k) -> p h k", k=128), in_=bc3(E(6), 128)),
                  [sm], [X["gam"]])
                Xd = dict(X)
                Xd["v"] = Tl(acc.t[:, 2048:3072], acc.b)
                Xd["rm"] = Tl(acc.t[:, 0:1024], acc.b)
                Xd["nm"] = Tl(acc.t[:, 1024:2048], acc.b)
                decs = [dict(g=sm[:, 32 + d * 8 + hh_i:32 + d * 8 + hh_i + 1], ncw=sm[:, 176 + hh_i:177 + hh_i],
                             cwx=sm[:, 168 + hh_i:169 + hh_i], b=sm) for hh_i in range(8)]
                summaries(ph4, d, 128, 128, 8, Xd, SD, i, {"outs": outs, "bsum": bSD, "sets": usets, "decs": decs})
        ph4.close()
        if stop == "P4":
            S.barrier()
            return nc

        for (K, Hn, Vd, SUM, bS, init, fin, YD, bY) in ((64, 16, 64, SR, bSR, init_r, fin_r, YA, bYA),
                                                        (128, 8, 128, SD, bSD, init_d, fin_d, YB, bYB)):
            ph5 = Phase()
            M = [[ph5.sb([128, 1024], name="M") for _ in range(2)] for _ in range(2)]
            gt = [[ph5.sb([128, 1024], name="gt") for _ in range(2)] for _ in range(2)]
            hh_ = [[ph5.sb([128, 1024], name="h") for _ in range(2)] for _ in range(2)]
            rt = [[ph5.sb([128, Hn * 128], name="rt") for _ in range(2)] for _ in range(2)]
            y0 = [[ph5.sb([128, 1024], name="y0") for _ in range(2)] for _ in range(2)]
            yo = [[ph5.sb([128, 1024], name="yo") for _ in range(2)] for _ in range(2)]
            cur = [0, 0]
            for d in range(2):
                LD(M[d][0][0:K, :], init[l, d], [M[d][0]])

            def load5(step, d):
                c = step if d == 0 else NT - 1 - step
                q = step % 2
                LD(gt[d][q][0:K, :], SUM["GT"][c, d], [gt[d][q]], [bS])
                LD(hh_[d][q][0:K, :], SUM["H"][c, d], [hh_[d][q]], [bS])
                LD(rt[d][q][0:K, :], SUM["RT"][c, d], [rt[d][q]], [bS])
                LD(y0[d][q][:, :], SUM["Y0"][c, d], [y0[d][q]], [bS])
            for d in range(2):
                load5(0, d)
            hpj = 512 // Vd
            for step in range(NT):
                for d in range(2):
                    if step + 1 < NT:
                        load5(step + 1, d)
                    c = step if d == 0 else NT - 1 - step
                    q = step % 2
                    gt_, h_, rt_, y0_, yo_ = gt[d][q], hh_[d][q], rt[d][q], y0[d][q], yo[d][q]
                    Mc, Mn = M[d][cur[d]], M[d][1 - cur[d]]
                    if step > 0 and step % 2 == 0:
                        slot = (c // 2 - 1) if d == 0 else ((c + 1) // 2)
                        ST(fin[l, slot, d], Mc[0:K, :], [Mc], [bFIN])
                        V(lambda e: e.tensor_scalar(out=Mc[0:K, :], in0=Mc[0:K, :], scalar1=flagc[0:K, 0:1], scalar2=None,
                                                    op0=ALU.mult), [Mc, flagc], [Mc])
                    for j in range(Hn // hpj):
                        p = PS()
                        for hq in range(hpj):
                            h = j * hpj + hq
                            MM(p[:, hq * Vd:(hq + 1) * Vd], rt_[0:K, h * 128:(h + 1) * 128], Mc[0:K, h * Vd:(h + 1) * Vd],
                               [rt_, Mc], [p])
                        V(lambda e: e.tensor_tensor(out=yo_[:, j * 512:(j + 1) * 512], in0=p[:, :],
                                                    in1=y0_[:, j * 512:(j + 1) * 512], op=ALU.add), [p, y0_], [yo_])
                        p = PS()
                        for hq in range(hpj):
                            h = j * hpj + hq
                            MM(p[0:K, hq * Vd:(hq + 1) * Vd], gt_[0:K, h * K:(h + 1) * K], Mc[0:K, h * Vd:(h + 1) * Vd],
                               [gt_, Mc], [p])
                        V(lambda e: e.tensor_tensor(out=Mn[0:K, j * 512:(j + 1) * 512], in0=p[0:K, :],
                                                    in1=h_[0:K, j * 512:(j + 1) * 512], op=ALU.add), [p, h_], [Mn])
                    ST(YD[d, c * 128:(c + 1) * 128, :], yo_[:, :], [yo_], [bY])
                    cur[d] = 1 - cur[d]
            for d in range(2):
                slot = 7 if d == 0 else 0
                ST(fin[l, slot, d], M[d][cur[d]][0:K, :], [M[d][cur[d]]], [bFIN])
            ph5.close()
        if stop == "P5":
            S.barrier()
            return nc

        ph6 = Phase()
        yT = ph6.sb([128, 16, T], BF16, name="yT")
        ph6a = Phase()
        gnw = ph6a.sb([128, AW], name="gnw")
        gnb = ph6a.sb([128, AW], name="gnb")
        onw = ph6a.sb([128, 128], name="onw")
        LD(gnw[:, :], gn_w[l].partition_broadcast(128), [gnw])
        LD(gnb[:, :], gn_b[l].partition_broadcast(128), [gnb])
        LD(onw[:, :], o_norm_w[l].partition_broadcast(128), [onw])
        V(lambda e: e.tensor_scalar(out=gnw[:, :], in0=gnw[:, :], scalar1=8.0, scalar2=None, op0=ALU.mult), [gnw], [gnw])
        V(lambda e: e.tensor_scalar(out=onw[:, :], in0=onw[:, :], scalar1=float(128 ** 0.5), scalar2=None, op0=ALU.mult),
          [onw], [onw])
        yf = ph6a.sb([128, AW], name="yf")
        yb_ = ph6a.sb([128, AW], name="yb")
        vz = ph6a.sb([128, 2048], name="vz")
        zb = ph6a.sb([128, AW], name="zb")
        bon = ph6a.sb([128, 32], name="bon6")
        t6 = ph6a.sb([128, AW], name="t6")
        sm = ph6a.sb([128, 64], name="sm6")
        for i in range(NT):
            rows = slice(i * 128, (i + 1) * 128)
            prow = slice(1 + i * 128, 1 + (i + 1) * 128)
            LD(yf[:, :], YA[0, rows, :], [yf], [bYA])
            LD(yb_[:, :], YA[1, rows, :], [yb_], [bYA])
            LD(vz[:, :], P[prow, C_V:C_V + 2048], [vz], [bP])
            LD(bon[:, :], BON[rows, :], [bon], [bBON])
            V(lambda e: e.tensor_tensor(out=yf[:, :], in0=yf[:, :], in1=yb_[:, :], op=ALU.add), [yf, yb_], [yf])
            V(lambda e: e.tensor_tensor(out=bon[:, 0:16], in0=bon[:, 0:16], in1=bon[:, 16:32], op=ALU.add), [bon], [bon])
            V(lambda e: e.tensor_tensor(out=t6[:, :].rearrange("p (h k) -> p h k", k=64),
                                        in0=vz[:, 0:1024].rearrange("p (h k) -> p h k", k=64),
                                        in1=bc3(bon[:, 0:16], 64), op=ALU.mult), [vz, bon], [t6])
            G(lambda e: e.tensor_tensor(out=yf[:, :], in0=yf[:, :], in1=t6[:, :], op=ALU.add), [yf, t6], [yf])
            V(lambda e: e.tensor_reduce(out=sm[:, 0:16], in_=yf[:, :].rearrange("p (h k) -> p h k", k=64), axis=AX.X, op=ALU.add),
              [yf], [sm])
            V(lambda e: e.tensor_scalar(out=sm[:, 0:16], in0=sm[:, 0:16], scalar1=-1.0 / 64, scalar2=None, op0=ALU.mult), [sm], [sm])
            V(lambda e: e.tensor_tensor(out=yf[:, :].rearrange("p (h k) -> p h k", k=64),
                                        in0=yf[:, :].rearrange("p (h k) -> p h k", k=64),
                                        in1=bc3(sm[:, 0:16], 64), op=ALU.add), [yf, sm], [yf])
            A(lambda e: e.activation(out=t6[:, :], in_=yf[:, :], func=AF.Square), [yf], [t6])
            V(lambda e: e.tensor_reduce(out=sm[:, 16:32], in_=t6[:, :].rearrange("p (h k) -> p h k", k=64), axis=AX.X, op=ALU.add),
              [t6], [sm])
            rsqrt(sm[:, 16:32], sm[:, 16:32], 1.0, float(GN_EPS * 64), [sm])
            V(lambda e: e.tensor_tensor(out=yf[:, :].rearrange("p (h k) -> p h k", k=64),
                                        in0=yf[:, :].rearrange("p (h k) -> p h k", k=64),
                                        in1=bc3(sm[:, 16:32], 64), op=ALU.mult), [yf, sm], [yf])
            V(lambda e: e.tensor_tensor(out=yf[:, :], in0=yf[:, :], in1=gnw[:, :], op=ALU.mult), [yf, gnw], [yf])
            G(lambda e: e.tensor_tensor(out=yf[:, :], in0=yf[:, :], in1=gnb[:, :], op=ALU.add), [yf, gnb], [yf])
            A(lambda e: e.activation(out=vz[:, 1024:2048], in_=vz[:, 1024:2048], func=AF.Silu), [vz], [vz])
            V(lambda e: e.tensor_tensor(out=yf[:, :], in0=yf[:, :], in1=vz[:, 1024:2048], op=ALU.mult), [yf, vz], [yf])
            for c4 in range(2):
                p = PS()
                for q in range(4):
                    kc = c4 * 4 + q
                    TR(p[:, q * 128:(q + 1) * 128], yf[:, kc * 128:(kc + 1) * 128], [yf], [p])
                A(lambda e: e.activation(out=yT[:, c4 * 4:(c4 + 1) * 4, i * 128:(i + 1) * 128],
                                         in_=p[:, :].rearrange("p (q t) -> p q t", q=4), func=AF.Copy), [p], [yT])
            LD(yf[:, :], YB[0, rows, :], [yf], [bYB])
            LD(yb_[:, :], YB[1, rows, :], [yb_], [bYB])
            LD(zb[:, :], P[prow, C_ZB:C_ZB + 1024], [zb], [bP])
            V(lambda e: e.tensor_tensor(out=yf[:, :], in0=yf[:, :], in1=yb_[:, :], op=ALU.add), [yf, yb_], [yf])
            A(lambda e: e.activation(out=t6[:, :], in_=yf[:, :], func=AF.Square), [yf], [t6])
            V(lambda e: e.tensor_reduce(out=sm[:, 32:40], in_=t6[:, :].rearrange("p (h k) -> p h k", k=128), axis=AX.X, op=ALU.add),
              [t6], [sm])
            rsqrt(sm[:, 32:40], sm[:, 32:40], 1.0, float(EPS * 128), [sm])
            V(lambda e: e.tensor_tensor(out=yf[:, :].rearrange("p (h k) -> p h k", k=128),
                                        in0=yf[:, :].rearrange("p (h k) -> p h k", k=128),
                                        in1=bc3(sm[:, 32:40], 128), op=ALU.mult), [yf, sm], [yf])
            V(lambda e: e.tensor_tensor(out=yf[:, :].rearrange("p (h k) -> p h k", k=128),
                                        in0=yf[:, :].rearrange("p (h k) -> p h k", k=128),
                                        in1=onw[:, :].unsqueeze(1).broadcast_to([128, 8, 128]), op=ALU.mult), [yf, onw], [yf])
            A(lambda e: e.activation(out=zb[:, :], in_=zb[:, :], func=AF.Silu), [zb], [zb])
            V(lambda e: e.tensor_tensor(out=yf[:, :], in0=yf[:, :], in1=zb[:, :], op=ALU.mult), [yf, zb], [yf])
            for c4 in range(2):
                p = PS()
                for q in range(4):
                    kc = c4 * 4 + q
                    TR(p[:, q * 128:(q + 1) * 128], yf[:, kc * 128:(kc + 1) * 128], [yf], [p])
                A(lambda e: e.activation(out=yT[:, 8 + c4 * 4:8 + (c4 + 1) * 4, i * 128:(i + 1) * 128],
                                         in_=p[:, :].rearrange("p (q t) -> p q t", q=4), func=AF.Copy), [p], [yT])
        ph6a.close()
        if stop == "P6a":
            S.barrier()
            return nc

        ph6b = Phase()
        wst = [ph6b.sb([128, 16, 512], name="wst")] * 2
        wbf = [ph6b.sb([128, 16, 512], BF16, name="wbf") for _ in range(2)]
        gts = [ph6b.sb([128, 1024], name="gts") for _ in range(2)]
        mgt = [ph6b.sb([128, 512], name="mgt") for _ in range(2)]
        t2 = [ph6b.sb([128, 512], name="t2") for _ in range(2)]
        wpa = w_pa[l].rearrange("(kc p) n -> p kc n", p=128)
        wpb = w_pb[l].rearrange("(kc p) n -> p kc n", p=128)
        for cg in range(4):
            cs = slice(cg * 512, (cg + 1) * 512)
            w, wb = wst[cg % 2], wbf[cg % 2]
            for k4 in range(2):
                LD(w[:, k4 * 4:(k4 + 1) * 4, :], wpa[:, k4 * 4:(k4 + 1) * 4, cs], [w])
                LD(w[:, 8 + k4 * 4:8 + (k4 + 1) * 4, :], wpb[:, k4 * 4:(k4 + 1) * 4, cs], [w])
            for k4 in range(4):
                G(lambda e: e.tensor_copy(out=wb[:, k4 * 4:(k4 + 1) * 4, :], in_=w[:, k4 * 4:(k4 + 1) * 4, :]), [w], [wb])
            for i in range(NT):
                prow = slice(1 + i * 128, 1 + (i + 1) * 128)
                gt_, mg_, t2_ = gts[i % 2], mgt[i % 2], t2[i % 2]
                LD(gt_[:, 0:512], P[prow, C_GA + cg * 512:C_GA + (cg + 1) * 512], [gt_], [bP])
                LD(gt_[:, 512:1024], P[prow, C_GB + cg * 512:C_GB + (cg + 1) * 512], [gt_], [bP])
                A(lambda e: e.activation(out=gt_[:, :], in_=gt_[:, :], func=AF.Sigmoid), [gt_], [gt_])
                pa = PS()
                for kc in range(8):
                    MM(pa[:, :], yT[:, kc, i * 128:(i + 1) * 128], wb[:, kc, :], [yT, wb], [pa], start=(kc == 0), stop=(kc == 7))
                pb = PS()
                for kc in range(8):
                    MM(pb[:, :], yT[:, 8 + kc, i * 128:(i + 1) * 128], wb[:, 8 + kc, :], [yT, wb], [pb], start=(kc == 0), stop=(kc == 7))
                V(lambda e: e.tensor_tensor(out=mg_[:, :], in0=pa[:, :], in1=gt_[:, 0:512], op=ALU.mult), [pa, gt_], [mg_])
                V(lambda e: e.tensor_tensor(out=t2_[:, :], in0=pb[:, :], in1=gt_[:, 512:1024], op=ALU.mult), [pb, gt_], [t2_])
                G(lambda e: e.tensor_tensor(out=mg_[:, :], in0=mg_[:, :], in1=t2_[:, :], op=ALU.add), [mg_, t2_], [mg_])
                ST(MG[i * 128:(i + 1) * 128, cs], mg_[:, :], [mg_], [bMG])
        ph6b.close()
        ph6.close()
        if stop == "P6b":
            S.barrier()
            return nc

        ph7 = Phase()
        wo = ph7.sb([128, 16, D], BF16, name="wo")
        wst = [ph7.sb([128, 16, 256], name="wst7") for _ in range(2)]
        wov = w_o[l].rearrange("(kc p) n -> p kc n", p=128)
        for cg in range(8):
            w = wst[cg % 2]
            LD(w[:, :, :], wov[:, :, cg * 256:(cg + 1) * 256], [w])
            for k4 in range(2):
                G(lambda e: e.tensor_copy(out=wo[:, k4 * 8:(k4 + 1) * 8, cg * 256:(cg + 1) * 256], in_=w[:, k4 * 8:(k4 + 1) * 8, :]), [w], [wo])
        gpg = ph7.sb([128, D], name="gpg")
        LD(gpg[:, :], GPG, [gpg], [bGPG])
        mg = [ph7.sb([128, D], name="mg7") for _ in range(2)]
        mT = [ph7.sb([128, 16, 128], BF16, name="mT") for _ in range(2)]
        xr = [ph7.sb([128, D], name="xr") for _ in range(2)]
        ot = [ph7.sb([128, D], name="ot") for _ in range(2)]
        junk = ph7.sb([128, 512], name="junk7")
        ss4 = [ph7.sb([128, 8], name="ss4") for _ in range(2)]
        for i in range(NT):
            m_, mT_, x_, o_, s_ = mg[i % 2], mT[i % 2], xr[i % 2], ot[i % 2], ss4[i % 2]
            LD(m_[:, :], MG[i * 128:(i + 1) * 128, :], [m_], [bMG])
            gather(x_, i)
            for c4 in range(4):
                p = PS()
                for q in range(4):
                    kc = c4 * 4 + q
                    TR(p[:, q * 128:(q + 1) * 128], m_[:, kc * 128:(kc + 1) * 128], [m_], [p])
                A(lambda e: e.activation(out=mT_[:, c4 * 4:(c4 + 1) * 4, :], in_=p[:, :].rearrange("p (q t) -> p q t", q=4),
                                         func=AF.Copy), [p], [mT_])
            pj = []
            for cg in range(4):
                p = PS()
                pj.append(p)
                for kc in range(16):
                    MM(p[:, :], mT_[:, kc, :], wo[:, kc, cg * 512:(cg + 1) * 512], [mT_, wo], [p], start=(kc == 0), stop=(kc == 15))
                A(lambda e: e.activation(out=junk[:, :], in_=p[:, :], func=AF.Square, accum_out=s_[:, cg:cg + 1]), [p], [junk, s_])
            V(lambda e: e.tensor_reduce(out=s_[:, 4:5], in_=s_[:, 0:4], axis=AX.X, op=ALU.add), [s_], [s_])
            rsqrt(s_[:, 4:5], s_[:, 4:5], 1.0, float(EPS * D), [s_])
            for cg in range(4):
                cs = slice(cg * 512, (cg + 1) * 512)
                V(lambda e: e.scalar_tensor_tensor(out=o_[:, cs], in0=pj[cg][:, :], scalar=s_[:, 4:5], in1=gpg[:, cs],
                                                   op0=ALU.mult, op1=ALU.mult), [pj[cg], s_, gpg], [o_])
            G(lambda e: e.tensor_tensor(out=o_[:, :], in0=o_[:, :], in1=x_[:, :], op=ALU.add), [o_, x_], [o_])
            S.dma("pool", lambda e: e.indirect_dma_start(
                out=x_dst, out_offset=bass.IndirectOffsetOnAxis(ap=idxt[:, l * NT + i:l * NT + i + 1], axis=0),
                in_=o_[:, :], in_offset=None), [o_, idxt], [bxd])
        ph7.close()
        if stop == "P7":
            S.barrier()
            return nc

    try:
        for l in range(DEPTH):
            r_ = layer(l)
            if r_ is not None:
                break
    except _Cut:
        pass
    S.barrier()
    return nc


_PROG = {}


def kernel(x_prompt, x_sample, state_rwkv, state_delta, c, c_ctx, w_mod, b_mod, g_pre, g_post,
           w_in, w0, w_up, a0, a_up, k_k, k_a, r_k, gn_w, gn_b, conv_w, a_log, dt_bias,
           o_norm_w, w_pa, w_pb, w_o):
    f32 = np.float32
    if "nc" not in _PROG:
        _PROG["nc"] = build_program()
    nc = _PROG["nc"]
    arr = lambda z: np.ascontiguousarray(np.asarray(z, dtype=f32))
    shared = {"w_mod": arr(w_mod), "b_mod": arr(b_mod), "g_pre": arr(g_pre), "g_post": arr(g_post), "w_in": arr(w_in),
              "w0": arr(w0), "w_up": arr(w_up), "a0": arr(a0), "a_up": arr(a_up), "k_k": arr(k_k), "k_a": arr(k_a),
              "r_k": arr(r_k), "gn_w": arr(gn_w), "gn_b": arr(gn_b), "conv_w": arr(conv_w),
              "a_log": arr(a_log).reshape(DEPTH, 16), "dt_bias": arr(dt_bias).reshape(DEPTH, 16),
              "o_norm_w": arr(o_norm_w), "w_pa": arr(w_pa), "w_pb": arr(w_pb), "w_o": arr(w_o)}
    x_prompt = arr(x_prompt); x_sample = arr(x_sample)
    state_rwkv = arr(state_rwkv); state_delta = arr(state_delta)
    c = arr(c); c_ctx = arr(c_ctx)
    tpos = np.arange(T)
    perm = (tpos % 32) * 64 + tpos // 32
    ii = np.arange(128)
    lmask = np.zeros((7, 3, 128, 128), f32)
    for lv in range(7):
        bsz = 1 << lv
        pb = ii[:, None] // bsz
        fb = ii[None, :] // bsz
        la = ((pb % 2 == 1) & (fb == pb - 1)).astype(f32)
        lmask[lv, 0] = la
        lmask[lv, 1] = la.T
        lmask[lv, 2] = la
    lmask = np.ascontiguousarray(lmask.transpose(2, 0, 1, 3)).reshape(128, 7, 384)
    in_maps = []
    for core in range(8):
        m = dict(shared)
        idx = np.zeros((DEPTH, T), np.int32)
        cmask = np.ones((128, 2 * NT), f32)
        if core < 4:
            m["x_in"] = x_sample[core]
            cv = c[core]
            m["flag"] = np.ones((128, 1), f32)
            m["init_r"] = np.ascontiguousarray(state_rwkv[core].transpose(0, 1, 4, 2, 3)).reshape(DEPTH, 2, 64, 1024)
            m["init_d"] = np.ascontiguousarray(state_delta[core].transpose(0, 1, 3, 2, 4)).reshape(DEPTH, 2, 128, 1024)
            for l in range(DEPTH):
                idx[l] = perm if l % 2 == 1 else tpos
            cmask[0, 0] = 0.0
            cmask[127, 2 * (NT - 1) + 1] = 0.0
        else:
            j = core - 4
            xs = np.zeros((T, D), f32)
            xs[:1024] = x_prompt[4 * j:4 * j + 4].reshape(1024, D)
            xs[1024:] = xs[:1024]
            m["x_in"] = xs
            cv = c_ctx
            m["flag"] = np.zeros((128, 1), f32)
            m["init_r"] = np.zeros((DEPTH, 2, 64, 1024), f32)
            m["init_d"] = np.zeros((DEPTH, 2, 128, 1024), f32)
            for l in range(DEPTH):
                idx[l] = tpos
            for i in range(NT):
                if i % 2 == 0:
                    cmask[0, 2 * i] = 0.0
                else:
                    cmask[127, 2 * i + 1] = 0.0
        m["cvec"] = np.ascontiguousarray(cv.reshape(16, 128).T)
        m["idx"] = np.ascontiguousarray(idx.reshape(DEPTH, NT, 128).transpose(2, 0, 1).reshape(128, DEPTH * NT))
        m["cmask"] = cmask
        m["lmask"] = lmask
        in_maps.append(m)
    res = run_bass_kernel_spmd(nc, in_maps, core_ids=list(range(8)))
    R = res.results
    y_sample = np.stack([R[b]["y_out"] for b in range(4)], 0).astype(f32)
    y_prompt = np.zeros((16, 256, D), f32)
    new_r = np.zeros((16, DEPTH, 2, 16, 64, 64), f32)
    new_d = np.zeros((16, DEPTH, 2, 8, 128, 128), f32)
    for j in range(4):
        r = R[4 + j]
        y_prompt[4 * j:4 * j + 4] = r["y_out"][:1024].reshape(4, 256, D)
        fr = r["fin_r"].reshape(DEPTH, 8, 2, 64, 16, 64)
        fd = r["fin_d"].reshape(DEPTH, 8, 2, 128, 8, 128)
        for s in range(4):
            new_r[4 * j + s] = fr[:, s].transpose(0, 1, 3, 4, 2)
            new_d[4 * j + s] = fd[:, s].transpose(0, 1, 3, 2, 4)
    return (y_prompt, y_sample, new_r, new_d)
```
